# Optimizing a Trainium2 kernel written in Bass

```python
import math
import jax, jax.numpy as jnp
from jax import lax
import numpy as np

D_MODEL = 1024
BATCH = 8
SEQ = 2048
DEPTH = 4
DEC_BATCH = 32
DEC_SEQ = 4
PAST_LEN = 8192
PAGE_SIZE = 128

N_MIXERS = 3
N_META = 16
SSM_GROUP_CH = 16
SSM_GROUPS = D_MODEL // SSM_GROUP_CH
SSM_STATE = 64
POOL_WINDOWS = (2, 4, 8, 16)
POOL_GROUPS = 4
POOL_GROUP_CH = D_MODEL // POOL_GROUPS
POOL_BUF = 15
N_HEADS = 16
HEAD_DIM = D_MODEL // N_HEADS
Q_BLOCK = 128
SB_BIAS_INIT = -6.0
D_FF = 2816
CONV_W = 3
RMS_EPS = 1e-6
N_SSM_LAYERS = (DEPTH + 2) // 3
N_POOL_LAYERS = (DEPTH + 1) // 3
N_ATTN_LAYERS = DEPTH // 3

kernel_name = "hybrid_s5_pool_stickbreak_decoder_step"


def rmsnorm(x, g):
    xf = x.astype(jnp.float32)
    y = xf * lax.rsqrt(jnp.mean(xf * xf, axis=-1, keepdims=True) + RMS_EPS) * g.astype(jnp.float32)
    return y.astype(x.dtype)


def _ssm_combine(e1, e2):
    a1, b1 = e1
    a2, b2 = e2
    return a1 * a2, a2 * b1 + b2


def ssm_mix(u, h0_re, h0_im, lam_re, lam_im, log_dt, b_re, b_im, c_re, c_im, d_skip, w_glu):
    f32 = jnp.float32
    bsz, seq, _ = u.shape
    uf = u.astype(f32)
    lam = lax.complex(lam_re.astype(f32), lam_im.astype(f32))
    dt = jnp.exp(log_dt.astype(f32))[:, None]
    a_bar = jnp.exp(lam * dt)
    b_mat = lax.complex(b_re.astype(f32), b_im.astype(f32))
    c_mat = lax.complex(c_re.astype(f32), c_im.astype(f32))
    b_bar = ((a_bar - 1.0) / lam)[:, :, None] * b_mat
    ug = uf.reshape(bsz, seq, SSM_GROUPS, SSM_GROUP_CH).astype(jnp.complex64)
    bu = jnp.einsum('gpc,blgc->blgp', b_bar, ug)
    h0 = lax.complex(h0_re.astype(f32), h0_im.astype(f32))
    bu = bu.at[:, 0].add(a_bar * h0)
    a_seq = jnp.broadcast_to(a_bar, (1, seq, SSM_GROUPS, SSM_STATE))
    _, h = lax.associative_scan(_ssm_combine, (a_seq, bu), axis=1)
    y = jnp.real(jnp.einsum('gcp,blgp->blgc', c_mat, h)).reshape(bsz, seq, D_MODEL) + d_skip.astype(f32) * uf
    z = jax.nn.gelu(y) @ w_glu.astype(f32)
    out = z[..., :D_MODEL] * jax.nn.sigmoid(z[..., D_MODEL:])
    return out.astype(u.dtype), jnp.real(h[:, -1]), jnp.imag(h[:, -1])


def pool_mix(u, buf, pos0, w_lin, scale):
    f32 = jnp.float32
    bsz, seq, _ = u.shape
    ext = jnp.concatenate([buf.astype(u.dtype), u], axis=1)
    extf = ext.astype(f32)
    cs = jnp.concatenate([jnp.zeros((bsz, 1, D_MODEL), f32), jnp.cumsum(extf, axis=1)], axis=1)
    pos = pos0 + jnp.arange(seq)
    end = POOL_BUF + 1
    outs = []
    for gi, w in enumerate(POOL_WINDOWS):
        lo, hi = gi * POOL_GROUP_CH, (gi + 1) * POOL_GROUP_CH
        wsum = cs[:, end:end + seq, lo:hi] - cs[:, end - w:end - w + seq, lo:hi]
        cnt = jnp.minimum(w, pos + 1).astype(f32)[None, :, None]
        outs.append(wsum / cnt - extf[:, POOL_BUF:, lo:hi])
    pooled = jnp.stack(outs, axis=2)
    mixed = jnp.einsum('blgc,gce->blge', pooled, w_lin.astype(f32)).reshape(bsz, seq, D_MODEL)
    mixed = mixed * scale.astype(f32)
    return mixed.astype(u.dtype), ext[:, -POOL_BUF:]


def sb_attend(q, k, v, q_pos, k_pos, bias):
    f32 = jnp.float32
    bsz, lq = q.shape[0], q.shape[1]
    qb = min(Q_BLOCK, lq)
    nb = -(-lq // qb)
    pad = nb * qb - lq
    qf = jnp.pad(q.astype(f32), ((0, 0), (0, pad), (0, 0), (0, 0)))
    qp = jnp.pad(q_pos, (0, pad), constant_values=-1)
    qs = qf.reshape(bsz, nb, qb, N_HEADS, HEAD_DIM).transpose(1, 0, 2, 3, 4)
    ps = qp.reshape(nb, qb)
    kf = k.astype(f32)
    vf = v.astype(f32)
    bf = bias.astype(f32)[None, :, None, None]
    scale = 1.0 / math.sqrt(HEAD_DIM)

    def block(args):
        qblk, pblk = args
        z = jnp.einsum('bqhd,bshd->bhqs', qblk, kf) * scale + bf
        mask = (k_pos[None, :] < pblk[:, None])[None, None]
        log_not = jnp.where(mask, jax.nn.log_sigmoid(-z), 0.0)
        between = lax.cumsum(log_not, axis=3, reverse=True) - log_not
        wts = jnp.where(mask, jnp.exp(jax.nn.log_sigmoid(z) + between), 0.0)
        return jnp.einsum('bhqs,bshd->bqhd', wts, vf)

    o = lax.map(block, (qs, ps))
    o = o.transpose(1, 0, 2, 3, 4).reshape(bsz, nb * qb, N_HEADS, HEAD_DIM)[:, :lq]
    return o.astype(q.dtype)


def split_qkv(u, w_qkv):
    bsz, seq, _ = u.shape
    qkv = u @ w_qkv
    q, k, v = jnp.split(qkv, 3, axis=-1)
    shp = (bsz, seq, N_HEADS, HEAD_DIM)
    return q.reshape(shp), k.reshape(shp), v.reshape(shp)


def conv_ffn(u, buf, w_up, conv_w, conv_b, w_down):
    seq = u.shape[1]
    h = u @ w_up
    ext = jnp.concatenate([buf.astype(h.dtype), h], axis=1)
    c = conv_b
    for tap in range(CONV_W):
        c = c + conv_w[tap] * ext[:, tap:tap + seq]
    gate, val = jnp.split(c, 2, axis=-1)
    out = (jax.nn.silu(gate) * val) @ w_down
    return out.astype(u.dtype), ext[:, -(CONV_W - 1):]


def setup_inputs(seed: int = 0) -> dict:
    key = jax.random.key(seed)
    keys = iter(jax.random.split(key, 40))
    f32 = jnp.float32
    n_pages = PAST_LEN // PAGE_SIZE
    n_used = DEC_BATCH * n_pages
    n_pool = n_used + n_used // 4

    def normal(shape, s):
        return jax.random.normal(next(keys), shape, f32) * s

    ns, npl, na = N_SSM_LAYERS, N_POOL_LAYERS, N_ATTN_LAYERS
    G, P, GC = SSM_GROUPS, SSM_STATE, SSM_GROUP_CH
    inp = {}
    inp['x_prompt'] = normal((BATCH, SEQ, D_MODEL), 1.0)
    inp['x_sample'] = normal((DEC_BATCH, DEC_SEQ, D_MODEL), 1.0)
    inp['cache_k'] = normal((na, n_pool, PAGE_SIZE, N_HEADS, HEAD_DIM), 1.0)
    inp['cache_v'] = normal((na, n_pool, PAGE_SIZE, N_HEADS, HEAD_DIM), 1.0)
    inp['page_table'] = jax.random.permutation(next(keys), n_pool)[:n_used].reshape(DEC_BATCH, n_pages).astype(jnp.int32)
    inp['state_ssm_re'] = normal((ns, DEC_BATCH, G, P), 1.0)
    inp['state_ssm_im'] = normal((ns, DEC_BATCH, G, P), 1.0)
    inp['state_pool'] = normal((npl, DEC_BATCH, POOL_BUF, D_MODEL), 1.0)
    inp['state_ffn_conv'] = normal((DEPTH, DEC_BATCH, CONV_W - 1, 2 * D_FF), 1.0)
    inp['meta_tokens'] = normal((N_META, D_MODEL), 1.0)
    inp['norm_mix_g'] = 1.0 + normal((DEPTH, D_MODEL), 0.02)
    inp['norm_ffn_g'] = 1.0 + normal((DEPTH, D_MODEL), 0.02)
    inp['norm_final_g'] = 1.0 + normal((D_MODEL,), 0.02)
    inp['ssm_lambda_re'] = -0.5 + normal((ns, G, P), 0.01)
    inp['ssm_lambda_im'] = jnp.pi * jnp.arange(P, dtype=f32) + normal((ns, G, P), 0.01)
    inp['ssm_log_dt'] = jax.random.uniform(next(keys), (ns, G), f32, math.log(0.001), math.log(0.1))
    inp['ssm_b_re'] = normal((ns, G, P, GC), (2 * GC) ** -0.5)
    inp['ssm_b_im'] = normal((ns, G, P, GC), (2 * GC) ** -0.5)
    inp['ssm_c_re'] = normal((ns, G, GC, P), (2 * P) ** -0.5)
    inp['ssm_c_im'] = normal((ns, G, GC, P), (2 * P) ** -0.5)
    inp['ssm_d'] = normal((ns, D_MODEL), 0.5)
    inp['ssm_w_glu'] = normal((ns, D_MODEL, 2 * D_MODEL), D_MODEL ** -0.5)
    inp['pool_w'] = normal((npl, POOL_GROUPS, POOL_GROUP_CH, POOL_GROUP_CH), POOL_GROUP_CH ** -0.5)
    inp['pool_scale'] = 1.0 + normal((npl, D_MODEL), 0.02)
    inp['attn_w_qkv'] = normal((na, D_MODEL, 3 * D_MODEL), D_MODEL ** -0.5)
    inp['attn_w_o'] = normal((na, D_MODEL, D_MODEL), D_MODEL ** -0.5)
    inp['attn_logit_bias'] = SB_BIAS_INIT + normal((na, N_HEADS), 0.1)
    inp['ffn_w_up'] = normal((DEPTH, D_MODEL, 2 * D_FF), D_MODEL ** -0.5)
    inp['ffn_conv_w'] = normal((DEPTH, CONV_W, 2 * D_FF), CONV_W ** -0.5)
    inp['ffn_conv_b'] = normal((DEPTH, 2 * D_FF), 0.01)
    inp['ffn_w_down'] = normal((DEPTH, D_FF, D_MODEL), D_FF ** -0.5)
    return inp


def reference(x_prompt, x_sample, cache_k, cache_v, page_table, state_ssm_re, state_ssm_im, state_pool,
              state_ffn_conv, meta_tokens, norm_mix_g, norm_ffn_g, norm_final_g, ssm_lambda_re, ssm_lambda_im,
              ssm_log_dt, ssm_b_re, ssm_b_im, ssm_c_re, ssm_c_im, ssm_d, ssm_w_glu, pool_w, pool_scale,
              attn_w_qkv, attn_w_o, attn_logit_bias, ffn_w_up, ffn_conv_w, ffn_conv_b, ffn_w_down):
    f32 = jnp.float32
    bp = x_prompt.shape[0]
    meta = jnp.broadcast_to(meta_tokens.astype(x_prompt.dtype)[None], (bp, N_META, D_MODEL))
    xp = jnp.concatenate([meta, x_prompt], axis=1)
    xs = x_sample
    lp = xp.shape[1]
    bs, ls = xs.shape[0], xs.shape[1]
    past_len = page_table.shape[1] * PAGE_SIZE
    pos_p = jnp.arange(lp)
    pos_s = past_len + jnp.arange(ls)
    pos_all_s = jnp.arange(past_len + ls)

    kp_l, vp_l, ks_l, vs_l = [], [], [], []
    srp_l, sip_l, srs_l, sis_l = [], [], [], []
    poolp_l, pools_l, convp_l, convs_l = [], [], [], []
    for i in range(DEPTH):
        j = i // N_MIXERS
        up = rmsnorm(xp, norm_mix_g[i])
        us = rmsnorm(xs, norm_mix_g[i])
        if i % N_MIXERS == 0:
            zero = jnp.zeros((bp, SSM_GROUPS, SSM_STATE), f32)
            prm = (ssm_lambda_re[j], ssm_lambda_im[j], ssm_log_dt[j], ssm_b_re[j], ssm_b_im[j],
                   ssm_c_re[j], ssm_c_im[j], ssm_d[j], ssm_w_glu[j])
            mp, hpr, hpi = ssm_mix(up, zero, zero, *prm)
            ms, hsr, hsi = ssm_mix(us, state_ssm_re[j], state_ssm_im[j], *prm)
            srp_l.append(hpr); sip_l.append(hpi); srs_l.append(hsr); sis_l.append(hsi)
        elif i % N_MIXERS == 1:
            mp, bufp = pool_mix(up, jnp.zeros((bp, POOL_BUF, D_MODEL), up.dtype), 0, pool_w[j], pool_scale[j])
            ms, bufs = pool_mix(us, state_pool[j], past_len, pool_w[j], pool_scale[j])
            poolp_l.append(bufp); pools_l.append(bufs)
        else:
            qp, kp, vp = split_qkv(up, attn_w_qkv[j])
            op = sb_attend(qp, kp, vp, pos_p, pos_p, attn_logit_bias[j])
            qs, ks, vs = split_qkv(us, attn_w_qkv[j])
            past_k = cache_k[j][page_table].reshape(bs, past_len, N_HEADS, HEAD_DIM).astype(ks.dtype)
            past_v = cache_v[j][page_table].reshape(bs, past_len, N_HEADS, HEAD_DIM).astype(vs.dtype)
            k_all = jnp.concatenate([past_k, ks], axis=1)
            v_all = jnp.concatenate([past_v, vs], axis=1)
            os_ = sb_attend(qs, k_all, v_all, pos_s, pos_all_s, attn_logit_bias[j])
            mp = op.reshape(bp, lp, D_MODEL) @ attn_w_o[j]
            ms = os_.reshape(bs, ls, D_MODEL) @ attn_w_o[j]
            kp_l.append(kp); vp_l.append(vp); ks_l.append(ks); vs_l.append(vs)
        xp = xp + mp.astype(xp.dtype)
        xs = xs + ms.astype(xs.dtype)

        fp = rmsnorm(xp, norm_ffn_g[i])
        fs = rmsnorm(xs, norm_ffn_g[i])
        zero_buf = jnp.zeros((bp, CONV_W - 1, 2 * D_FF), fp.dtype)
        yp, cbp = conv_ffn(fp, zero_buf, ffn_w_up[i], ffn_conv_w[i], ffn_conv_b[i], ffn_w_down[i])
        ys, cbs = conv_ffn(fs, state_ffn_conv[i], ffn_w_up[i], ffn_conv_w[i], ffn_conv_b[i], ffn_w_down[i])
        convp_l.append(cbp); convs_l.append(cbs)
        xp = xp + yp.astype(xp.dtype)
        xs = xs + ys.astype(xs.dtype)

    y_prompt = rmsnorm(xp, norm_final_g)[:, N_META:]
    y_sample = rmsnorm(xs, norm_final_g)
    k_prompt = jnp.stack(kp_l)
    v_prompt = jnp.stack(vp_l)
    k_sample = jnp.stack(ks_l)
    v_sample = jnp.stack(vs_l)
    ssm_re_prompt = jnp.stack(srp_l)
    ssm_im_prompt = jnp.stack(sip_l)
    ssm_re_sample = jnp.stack(srs_l)
    ssm_im_sample = jnp.stack(sis_l)
    pool_prompt = jnp.stack(poolp_l)
    pool_sample = jnp.stack(pools_l)
    conv_prompt = jnp.stack(convp_l)
    conv_sample = jnp.stack(convs_l)
    return (y_prompt, y_sample, k_prompt, v_prompt, k_sample, v_sample, ssm_re_prompt, ssm_im_prompt,
            ssm_re_sample, ssm_im_sample, pool_prompt, pool_sample, conv_prompt, conv_sample)
```

```python
import contextlib
import math
import os
import numpy as np
import concourse.bass as bass
import concourse.mybir as mybir
from concourse.bass_utils import run_bass_kernel_spmd

F32 = mybir.dt.float32
BF16 = mybir.dt.bfloat16
I32 = mybir.dt.int32
AF = mybir.ActivationFunctionType
ALU = mybir.AluOpType

NCORES = 8
D = 1024
NCH = 8
TP = 2064
TS = 16
T = TP + TS
OFF = 2
TW = T + OFF
DFF = 2816
NFF = 22
EPS = 1e-6
NT = [(0, 416), (416, 416), (832, 416), (1248, 416), (1664, 416)]
NKC = 129
TWO_PI = 2.0 * math.pi
DEPTH = 4


class Op:
    __slots__ = ("eng", "fn", "deps", "needed", "val", "sem", "is_dma")

    def __init__(self, eng, fn, is_dma=False):
        self.eng = eng
        self.fn = fn
        self.deps = []
        self.needed = False
        self.val = None
        self.sem = None
        self.is_dma = is_dma


class Sched:
    ENGS = ("sync", "act", "dve", "pool", "pe")
    UID = 0

    def __init__(self, nc):
        self.nc = nc
        self.ops = {e: [] for e in self.ENGS}
        self.last_w = {}
        self.readers = {}
        self.ndma = {"sync": 0, "pool": 0}
        self.npool = {"sync": 8, "pool": 6}
        self.dma_hist = {"sync": [], "pool": []}

    def _add(self, op, reads, writes):
        reads = list(reads)
        writes = list(writes)
        for k in list(reads):
            if isinstance(k, tuple) and k[0] == "bank":
                writes.append(("bankrd", k[1]))
        deps = []
        for k in reads:
            w = self.last_w.get(k)
            if w is not None:
                deps.append(w)
        for k in writes:
            w = self.last_w.get(k)
            if w is not None:
                deps.append(w)
            lastr = {}
            for r in self.readers.get(k, ()):
                if r.is_dma:
                    deps.append(r)
                else:
                    lastr[r.eng] = r
            deps.extend(lastr.values())
        if op.eng == "pe" and not op.is_dma:
            deps = [d for d in deps if not (d.eng == "pe" and not d.is_dma)]
        seen = set()
        for d in deps:
            if id(d) not in seen and d is not op:
                seen.add(id(d))
                op.deps.append(d)
                d.needed = True
        for k in writes:
            self.last_w[k] = op
            self.readers[k] = []
        for k in reads:
            self.readers.setdefault(k, []).append(op)
        self.ops[op.eng].append(op)
        return op

    def op(self, eng, fn, reads=(), writes=()):
        return self._add(Op(eng, fn), reads, writes)

    def dma(self, fn, reads=(), writes=(), q="sync"):
        op = Op(q, fn, is_dma=True)
        i = self.ndma[q]
        self.ndma[q] += 1
        n = self.npool[q]
        op.sem = (q, i % n)
        op.val = 16 * (i // n + 1)
        hist = self.dma_hist[q]
        if i >= n:
            op.deps.append(hist[i - n])
        hist.append(op)
        op.needed = True
        return self._add(op, reads, writes)

    def emit(self):
        nc = self.nc
        with contextlib.ExitStack() as st:
            Sched.UID += 1
            u = Sched.UID
            allsems = []

            def newsem(name):
                h_ = nc.alloc_semaphore(name=name)
                allsems.append(h_)
                return h_
            esem = {e: newsem("se%d_%s" % (u, e)) for e in self.ENGS}
            dsem = {}
            for q, n in self.npool.items():
                for j in range(min(n, self.ndma[q])):
                    dsem[(q, j)] = newsem("sd%d_%s%d" % (u, q, j))
            for e in self.ENGS:
                c = 0
                for op in self.ops[e]:
                    if op.is_dma:
                        continue
                    if op.needed:
                        c += 1
                        op.val = c
                        op.sem = ("e", e)

            def semof(op):
                return esem[op.sem[1]] if op.sem[0] == "e" else dsem[op.sem]

            block = st.enter_context(nc.Block())

            def replay(e, h):
                waited = {}
                for op in self.ops[e]:
                    need = {}
                    for d in op.deps:
                        if waited.get(d.sem, 0) >= d.val:
                            continue
                        if need.get(d.sem, (0, None))[0] < d.val:
                            need[d.sem] = (d.val, d)
                    for key, (v, d) in need.items():
                        waited[key] = v
                        h.wait_ge(semof(d), v)
                    ins = op.fn(h)
                    if op.is_dma:
                        ins.then_inc(semof(op), 16)
                    elif op.needed:
                        ins.then_inc(semof(op), 1)
                if e in self.dma_hist:
                    last = {}
                    for op in self.dma_hist[e]:
                        last[op.sem] = op
                    for key, op in last.items():
                        if waited.get(key, 0) < op.val:
                            h.wait_ge(semof(op), op.val)

            @block.sync
            def _(h):
                replay("sync", h)

            @block.scalar
            def _(h):
                replay("act", h)

            @block.vector
            def _(h):
                replay("dve", h)

            @block.gpsimd
            def _(h):
                replay("pool", h)

            @block.tensor
            def _(h):
                replay("pe", h)
            st.close()
            nc.clear_and_free_semaphores(allsems)
            nc.all_engine_barrier()


def OP(S, eng, method, reads, writes, *args, **kw):
    return S.op(eng, lambda h: getattr(h, method)(*args, **kw), reads, writes)


def DMA(S, out, in_, reads, writes, q="sync", **kw):
    return S.dma(lambda h: h.dma_start(out=out, in_=in_, **kw), reads, writes, q=q)


def build_nc():
    nc = bass.Bass("TRN2", target_bir_lowering=False)
    STOP = int(os.environ.get("KSTOP", "99"))
    SSTOP = int(os.environ.get("KSSTOP", "99"))
    ASTOP = int(os.environ.get("KASTOP", "99"))

    def din(name, shape, dt=F32):
        return nc.dram_tensor(name, list(shape), dt, kind="ExternalInput").ap()

    def dout(name, shape, dt=F32):
        return nc.dram_tensor(name, list(shape), dt, kind="ExternalOutput").ap()

    xp = din("xp", [2048, D])
    xs = din("xs", [TS, D])
    meta = din("meta", [16, D])
    ident = din("ident", [128, 128])
    gains_d = din("gains", [128, 9, NCH])
    cw_d = din("cw", [128, DEPTH, 3, 44])
    cb_d = din("cb", [128, DEPTH, 44])
    cbuf_d = din("cbuf", [128, DEPTH, 44, 8])
    ssmd_d = din("ssmd", [128, 2, NCH])
    par_d = din("par", [128, 3, 2, 32])
    sst_d = din("sst", [128, 2, 2, 32, 4])
    bm_d = din("bm", [128, 2, 2, 32, 32])
    cm_d = din("cm", [128, 2, 2, 32, 32])
    rowmask_d = din("rowmask", [128, 4])
    spool_d = din("spool", [128, NCH, 4, 15])
    pscale_d = din("pscale", [128, NCH])
    rcnt_d = din("rcnt", [128, 4, 16])
    pool_w = din("pool_w", [4, 256, 256])
    w_qkv = din("w_qkv", [D, 3 * D])
    w_o = din("w_o", [D, D])
    abias_d = din("abias", [128, 16])
    tri_d = din("tri", [128, 128])
    tm_d = din("tm", [128, 1088])
    mcur_d = din("mcur", [16, 4, 64])
    ptab_d = din("ptab", [128, 256], I32)
    iota_d = din("iota", [128, 1])
    cache_k = din("cache_k", [2560 * 128, D])
    cache_v = din("cache_v", [2560 * 128, D])
    w_glu = din("w_glu", [2, D, 2 * D])
    w_up = din("w_up", [DEPTH, D, 2 * DFF])
    w_down = din("w_down", [DEPTH, DFF, D])

    y_p = dout("y_p", [2048, D])
    y_s = dout("y_s", [TS, D])
    ssm_p = dout("ssm_p", [128, 2, 2, 32])
    ssm_s = dout("ssm_s", [128, 2, 2, 32, 4])
    conv_o = dout("conv_o", [128, DEPTH, 44, 10])
    k_p = dout("k_p", [TP, D])
    v_p = dout("v_p", [TP, D])
    k_s = dout("k_s", [TS, D])
    v_s = dout("v_s", [TS, D])
    pool_p = dout("pool_p", [128, NCH, 15])
    pool_s = dout("pool_s", [128, NCH, 4, 15])

    pst = contextlib.ExitStack()

    def sbp(name, shape, dt=F32):
        return pst.enter_context(nc.sbuf_tensor(name, list(shape), dt))

    x = sbp("x", [128, NCH, T])
    u16 = sbp("u16", [128, NCH, TW], BF16)
    idt = sbp("idt", [128, 128])
    ones_b = sbp("ones_b", [128, 128], BF16)
    gains = sbp("gains_sb", [128, 9, NCH])
    cw = sbp("cw_sb", [128, DEPTH, 3, 44])
    cb = sbp("cb_sb", [128, DEPTH, 44])
    cbuf = sbp("cbuf_sb", [128, DEPTH, 44, 8])
    ssmd = sbp("ssmd_sb", [128, 2, NCH])
    par = sbp("par_sb", [128, 3, 2, 32])
    sst = sbp("sst_sb", [128, 2, 2, 32, 4])
    rowmask = sbp("rowmask_sb", [128, 4])
    spool = sbp("spool_sb", [128, NCH, 4, 15])
    pscale = sbp("pscale_sb", [128, NCH])
    rcnt = sbp("rcnt_sb", [128, 4, 16])
    banks = [pst.enter_context(nc.psum_tensor("bank%d" % i, [128, 512], F32)) for i in range(8)]
    uid = [0]

    class Phase:
        def __enter__(self):
            self.S = Sched(nc)
            self.st = contextlib.ExitStack()

            def sb(shape, dt=F32):
                uid[0] += 1
                return self.st.enter_context(nc.sbuf_tensor("t%d" % uid[0], list(shape), dt))
            self.sb = sb
            return self

        def __exit__(self, *a):
            if a[0] is None:
                self.S.emit()
            self.st.close()
            return False

    def bk(i):
        return ("bank", i)

    with Phase() as ph:
        S = ph.S
        for dst, src, key in ((idt, ident, "idt"), (gains, gains_d, "gains"), (cw, cw_d, "cw"), (cb, cb_d, "cb"),
                              (cbuf, cbuf_d, "cbuf"), (ssmd, ssmd_d, "ssmd"), (par, par_d, "par"),
                              (sst, sst_d, "sst"), (rowmask, rowmask_d, "rowmask"),
                              (spool, spool_d, "spool"), (pscale, pscale_d, "pscale"), (rcnt, rcnt_d, "rcnt")):
            DMA(S, dst[:], src, [], [key])
        OP(S, "pool", "memset", [], ["ones_b"], ones_b[:], 1.0)
        OP(S, "pool", "memset", [], ["u16"], u16[:], 0.0)
        stg = [ph.sb([128, D]) for _ in range(2)]
        blocks = [(meta, 16, 0)]
        for b in range(16):
            blocks.append((xp[b * 128:(b + 1) * 128, :], 128, 16 + b * 128))
        blocks.append((xs, 16, TP))
        for bi, (src, nr, c0) in enumerate(blocks):
            sg = stg[bi % 2]
            DMA(S, sg[0:nr, :], src, [], [("stg", bi % 2)])
            for g in range(2):
                b_ = (bi * 2 + g) % 4
                for jj in range(4):
                    c = g * 4 + jj
                    OP(S, "pe", "transpose", [("stg", bi % 2), "idt"], [bk(b_)],
                       out=banks[b_][:, jj * 128:jj * 128 + nr], in_=sg[0:nr, c * 128:(c + 1) * 128],
                       identity=idt[0:nr, 0:nr])
                src_v = banks[b_][:].rearrange("p (j n) -> p j n", j=4)[:, :, 0:nr]
                dst_v = x[:, g * 4:(g + 1) * 4, c0:c0 + nr]
                if g == 0:
                    OP(S, "act", "activation", [bk(b_)], [("x", bi, g)], out=dst_v, in_=src_v, func=AF.Copy)
                else:
                    OP(S, "dve", "tensor_copy", [bk(b_)], [("x", bi, g)], out=dst_v, in_=src_v)

    def norm_stats(ph):
        S = ph.S
        sq = ph.sb([128, NCH, T], BF16)
        rstd = ph.sb([128, T])
        for c in range(NCH):
            OP(S, "act", "activation", ["x"], [("sq", c)], out=sq[:, c, :], in_=x[:, c, :], func=AF.Square)
        for ti, (n0, nn) in enumerate(NT):
            b_ = 6 + ti % 2
            for c in range(NCH):
                OP(S, "pe", "matmul", [("sq", c), "ones_b"], [bk(b_)], banks[b_][:, 0:nn], lhsT=ones_b[:],
                   rhs=sq[:, c, n0:n0 + nn], start=(c == 0), stop=(c == NCH - 1))
            OP(S, "act", "activation", [bk(b_)], [("rstd", ti)], out=rstd[:, n0:n0 + nn], in_=banks[b_][:, 0:nn],
               func=AF.Sqrt, bias=EPS, scale=1.0 / D)
            OP(S, "dve", "reciprocal", [("rstd", ti)], [("rstd", ti)], out=rstd[:, n0:n0 + nn], in_=rstd[:, n0:n0 + nn])
        return rstd

    def norm_to_u16(gi):
        with Phase() as ph:
            S = ph.S
            rstd = norm_stats(ph)
            rk = [("rstd", ti) for ti in range(5)]
            for c in range(NCH):
                OP(S, "dve", "scalar_tensor_tensor", ["x"] + rk, [("u16", c)], out=u16[:, c, OFF:OFF + T],
                   in0=x[:, c, :], scalar=gains[:, gi, c:c + 1], in1=rstd[:, :], op0=ALU.mult, op1=ALU.mult)

    class WStream:
        def __init__(self, ph, ncols, nst=2, nbf=2, name="w"):
            self.ph = ph
            self.stg = [ph.sb([128, 8, ncols]) for _ in range(nst)]
            self.bf = [ph.sb([128, 8, ncols], BF16) for _ in range(nbf)]
            self.i = 0
            self.name = name

        def load(self, src2d, cast_eng="pool"):
            S = self.ph.S
            i = self.i
            self.i += 1
            sg = self.stg[i % len(self.stg)]
            bf = self.bf[i % len(self.bf)]
            ks = (self.name + "s", i % len(self.stg))
            kb = (self.name + "b", i % len(self.bf))
            DMA(S, sg[:], src2d.rearrange("(k p) n -> p k n", p=128), [], [ks])
            OP(S, cast_eng, "tensor_copy", [ks], [kb], out=bf[:], in_=sg[:])
            return bf, kb

    def ssm_layer(j, li):
        sst_ = contextlib.ExitStack()

        def sbs(name, shape, dt=F32):
            return sst_.enter_context(nc.sbuf_tensor("%s_%d" % (name, j), list(shape), dt))
        S_r = sbs("S_r", [128, 32, NKC + 1])
        S_i = sbs("S_i", [128, 32, NKC + 1])
        pw_r = sbs("pw_r", [128, 17, 32])
        pw_i = sbs("pw_i", [128, 17, 32])
        wri = sbs("wri", [128, 2, 32])
        aa = sbs("aa", [128, 2, 32])
        bus = sbs("bus", [128, 2, 32, 16])
        hs = sbs("hs", [128, 2, 32, 4, 4])

        lamr = par[:, 0, j, :]
        lami = par[:, 1, j, :]
        ldt = par[:, 2, j, :]

        with Phase() as ph:
            S = ph.S
            pp = ph.sb([128, 16, 32])
            ki = ph.sb([128, 32], I32)

            def P(i):
                return pp[:, i, :]

            def K(i):
                return ("pp", i)
            dt_, mag, phi, kf, tmp, msk, sinv, cosv, den, am1, t1, t2, phc = range(13)
            OP(S, "act", "activation", [], [K(dt_)], out=P(dt_), in_=ldt, func=AF.Exp)
            OP(S, "dve", "tensor_tensor", [K(dt_)], [K(tmp)], out=P(tmp), in0=lamr, in1=P(dt_), op=ALU.mult)
            OP(S, "act", "activation", [K(tmp)], [K(mag)], out=P(mag), in_=P(tmp), func=AF.Exp)
            OP(S, "dve", "tensor_tensor", [K(dt_)], [K(phi)], out=P(phi), in0=lami, in1=P(dt_), op=ALU.mult)
            OP(S, "dve", "tensor_scalar", [K(phi)], ["ki"], out=ki[:], in0=P(phi), scalar1=1.0 / TWO_PI, scalar2=None,
               op0=ALU.mult)
            OP(S, "dve", "tensor_copy", ["ki"], [K(kf)], out=P(kf), in_=ki[:])
            OP(S, "dve", "scalar_tensor_tensor", [K(kf), K(phi)], [K(phi)], out=P(phi), in0=P(kf), scalar=-TWO_PI,
               in1=P(phi), op0=ALU.mult, op1=ALU.add)

            def fold(pi_):
                OP(S, "dve", "tensor_scalar", [K(pi_)], [K(msk)], out=P(msk), in0=P(pi_), scalar1=math.pi, scalar2=None,
                   op0=ALU.is_gt)
                OP(S, "dve", "scalar_tensor_tensor", [K(msk), K(pi_)], [K(pi_)], out=P(pi_), in0=P(msk), scalar=-TWO_PI,
                   in1=P(pi_), op0=ALU.mult, op1=ALU.add)
                OP(S, "dve", "tensor_scalar", [K(pi_)], [K(msk)], out=P(msk), in0=P(pi_), scalar1=-math.pi, scalar2=None,
                   op0=ALU.is_lt)
                OP(S, "dve", "scalar_tensor_tensor", [K(msk), K(pi_)], [K(pi_)], out=P(pi_), in0=P(msk), scalar=TWO_PI,
                   in1=P(pi_), op0=ALU.mult, op1=ALU.add)
            fold(phi)
            OP(S, "act", "activation", [K(phi)], [K(sinv)], out=P(sinv), in_=P(phi), func=AF.Sin)
            OP(S, "dve", "tensor_scalar", [K(phi)], [K(phc)], out=P(phc), in0=P(phi), scalar1=math.pi / 2, scalar2=None,
               op0=ALU.add)
            fold(phc)
            OP(S, "act", "activation", [K(phc)], [K(cosv)], out=P(cosv), in_=P(phc), func=AF.Sin)
            ar = aa[:, 0, :]
            ai = aa[:, 1, :]
            OP(S, "dve", "tensor_tensor", [K(mag), K(cosv)], ["ar"], out=ar, in0=P(mag), in1=P(cosv), op=ALU.mult)
            OP(S, "dve", "tensor_tensor", [K(mag), K(sinv)], ["ai"], out=ai, in0=P(mag), in1=P(sinv), op=ALU.mult)
            OP(S, "dve", "tensor_scalar", ["ar"], [K(am1)], out=P(am1), in0=ar, scalar1=-1.0, scalar2=None, op0=ALU.add)
            OP(S, "dve", "tensor_tensor", [], [K(t1)], out=P(t1), in0=lamr, in1=lamr, op=ALU.mult)
            OP(S, "dve", "tensor_tensor", [], [K(t2)], out=P(t2), in0=lami, in1=lami, op=ALU.mult)
            OP(S, "dve", "tensor_tensor", [K(t1), K(t2)], [K(den)], out=P(den), in0=P(t1), in1=P(t2), op=ALU.add)
            OP(S, "dve", "reciprocal", [K(den)], [K(den)], out=P(den), in_=P(den))
            OP(S, "dve", "tensor_tensor", [K(am1)], [K(t1)], out=P(t1), in0=P(am1), in1=lamr, op=ALU.mult)
            OP(S, "dve", "tensor_tensor", ["ai"], [K(t2)], out=P(t2), in0=ai, in1=lami, op=ALU.mult)
            OP(S, "dve", "tensor_tensor", [K(t1), K(t2)], [K(t1)], out=P(t1), in0=P(t1), in1=P(t2), op=ALU.add)
            OP(S, "dve", "tensor_tensor", [K(t1), K(den)], ["wr"], out=wri[:, 0, :], in0=P(t1), in1=P(den), op=ALU.mult)
            OP(S, "dve", "tensor_tensor", ["ai"], [K(t1)], out=P(t1), in0=ai, in1=lamr, op=ALU.mult)
            OP(S, "dve", "tensor_tensor", [K(am1)], [K(t2)], out=P(t2), in0=P(am1), in1=lami, op=ALU.mult)
            OP(S, "dve", "tensor_tensor", [K(t1), K(t2)], [K(t1)], out=P(t1), in0=P(t1), in1=P(t2), op=ALU.subtract)
            OP(S, "dve", "tensor_tensor", [K(t1), K(den)], ["wi"], out=wri[:, 1, :], in0=P(t1), in1=P(den), op=ALU.mult)
            OP(S, "dve", "memset", [], [("pwr", 0)], pw_r[:, 0, :], 1.0)
            OP(S, "dve", "memset", [], [("pwi", 0)], pw_i[:, 0, :], 0.0)
            for n in range(1, 17):
                OP(S, "dve", "tensor_tensor", ["ar", ("pwr", n - 1)], [K(t1)], out=P(t1), in0=ar, in1=pw_r[:, n - 1, :], op=ALU.mult)
                OP(S, "dve", "tensor_tensor", ["ai", ("pwi", n - 1)], [K(t2)], out=P(t2), in0=ai, in1=pw_i[:, n - 1, :], op=ALU.mult)
                OP(S, "dve", "tensor_tensor", [K(t1), K(t2)], [("pwr", n)], out=pw_r[:, n, :], in0=P(t1), in1=P(t2), op=ALU.subtract)
                OP(S, "dve", "tensor_tensor", ["ar", ("pwi", n - 1)], [K(t1)], out=P(t1), in0=ar, in1=pw_i[:, n - 1, :], op=ALU.mult)
                OP(S, "dve", "tensor_tensor", ["ai", ("pwr", n - 1)], [K(t2)], out=P(t2), in0=ai, in1=pw_r[:, n - 1, :], op=ALU.mult)
                OP(S, "dve", "tensor_tensor", [K(t1), K(t2)], [("pwi", n)], out=pw_i[:, n, :], in0=P(t1), in1=P(t2), op=ALU.add)

        def bbar(ph, m, eng="dve"):
            S = ph.S
            if not hasattr(ph, "bb_tiles"):
                ph.bb_tiles = ([ph.sb([128, 2, 4, 32]) for _ in range(2)], ph.sb([128, 2, 4, 32]), ph.sb([128, 2, 4, 32]))
            bms, tb, bb = ph.bb_tiles
            bmt = bms[m % 2]
            DMA(S, bmt[:], bm_d[:, :, j, 4 * m:4 * m + 4, :], [], [("bm", m % 2)])
            wr_b = wri[:, 0, 4 * m:4 * m + 4].unsqueeze(2).broadcast_to([128, 4, 32])
            wi_b = wri[:, 1, 4 * m:4 * m + 4].unsqueeze(2).broadcast_to([128, 4, 32])
            OP(S, eng, "tensor_tensor", [("bm", m % 2)], ["tb0"], out=tb[:, 0], in0=bmt[:, 0], in1=wr_b, op=ALU.mult)
            OP(S, eng, "tensor_tensor", [("bm", m % 2)], ["tb1"], out=tb[:, 1], in0=bmt[:, 1], in1=wi_b, op=ALU.mult)
            OP(S, eng, "tensor_tensor", ["tb0", "tb1"], ["bbr"], out=bb[:, 0], in0=tb[:, 0], in1=tb[:, 1], op=ALU.subtract)
            OP(S, eng, "tensor_tensor", [("bm", m % 2)], ["tb0"], out=tb[:, 0], in0=bmt[:, 1], in1=wr_b, op=ALU.mult)
            OP(S, eng, "tensor_tensor", [("bm", m % 2)], ["tb1"], out=tb[:, 1], in0=bmt[:, 0], in1=wi_b, op=ALU.mult)
            OP(S, eng, "tensor_tensor", ["tb0", "tb1"], ["bbi"], out=bb[:, 1], in0=tb[:, 0], in1=tb[:, 1], op=ALU.add)
            return bb

        with Phase() as ph:
            S = ph.S
            ta = ph.sb([128, 8, 128])
            td_ = ph.sb([128, 8, 128])
            Dr = ph.sb([128, 8, 128])
            Di = ph.sb([128, 8, 128])
            BT = [ph.sb([128, 16, 128], BF16) for _ in range(2)]
            um = [ph.sb([128, TW], BF16) for _ in range(4)]
            OP(S, "pool", "memset", [], [("S", 0, -1)], S_r[:, :, 0:1], 0.0)
            OP(S, "pool", "memset", [], [("S", 1, -1)], S_i[:, :, 0:1], 0.0)
            tcount = 0
            for m in range(NCH):
                bb = bbar(ph, m)
                for jj in range(4):
                    OP(S, "act", "activation", ["rowmask"], [("um", jj)], out=um[jj][:], in_=u16[:, m, :], func=AF.Identity,
                       scale=rowmask[:, jj:jj + 1], bias=0.0)
                for half in range(2):
                    n0 = 8 * half
                    pr = pw_r[:, n0:n0 + 8, 4 * m:4 * m + 4].unsqueeze(3).broadcast_to([128, 8, 4, 32])
                    pi_ = pw_i[:, n0:n0 + 8, 4 * m:4 * m + 4].unsqueeze(3).broadcast_to([128, 8, 4, 32])
                    br = bb[:, 0].unsqueeze(1).broadcast_to([128, 8, 4, 32])
                    bi = bb[:, 1].unsqueeze(1).broadcast_to([128, 8, 4, 32])

                    def v4(t):
                        return t[:].rearrange("p n (i c) -> p n i c", i=4)
                    OP(S, "dve", "tensor_tensor", ["bbr"], ["Dr"], out=v4(Dr), in0=pr, in1=br, op=ALU.mult)
                    OP(S, "dve", "tensor_tensor", ["bbi"], ["ta"], out=v4(ta), in0=pi_, in1=bi, op=ALU.mult)
                    OP(S, "dve", "tensor_tensor", ["Dr", "ta"], ["Dr"], out=Dr[:], in0=Dr[:], in1=ta[:], op=ALU.subtract)
                    OP(S, "pool", "tensor_tensor", ["bbi"], ["Di"], out=v4(Di), in0=pr, in1=bi, op=ALU.mult)
                    OP(S, "pool", "tensor_tensor", ["bbr"], ["td"], out=v4(td_), in0=pi_, in1=br, op=ALU.mult)
                    OP(S, "pool", "tensor_tensor", ["Di", "td"], ["Di"], out=Di[:], in0=Di[:], in1=td_[:], op=ALU.add)
                    for ri, (Dsrc, dk) in enumerate(((Dr, "Dr"), (Di, "Di"))):
                        for g in range(2):
                            b_ = tcount % 2
                            tcount += 1
                            for q in range(4):
                                OP(S, "pe", "transpose", [dk, "idt"], [bk(b_)], out=banks[b_][:, q * 128:(q + 1) * 128],
                                   in_=Dsrc[:, g * 4 + q, :], identity=idt[:])
                            OP(S, "act", "activation", [bk(b_)], [("BT", ri)],
                               out=BT[ri][:, n0 + g * 4:n0 + g * 4 + 4, :].rearrange("p n q -> p (n q)"),
                               in_=banks[b_][:], func=AF.Copy)
                pb = [2, 3, 4] if m % 2 == 0 else [5, 6, 7]
                slot = 0
                for jj in range(4):
                    for ri in range(2):
                        b_ = pb[slot // 3]
                        c0 = (slot % 3) * NKC
                        slot += 1
                        for tq in range(16):
                            OP(S, "pe", "matmul", [("BT", ri), ("um", jj)], [bk(b_)], banks[b_][:, c0:c0 + NKC],
                               lhsT=BT[ri][:, 15 - tq, :], rhs=um[jj][:, OFF + tq:OFF + tq + TP:16],
                               start=(tq == 0), stop=(tq == 15))
                b_ = pb[2]
                for jj in range(4):
                    for ri in range(2):
                        c0 = 258 + (jj * 2 + ri) * 16
                        OP(S, "pe", "matmul", [("BT", ri), ("um", jj)], [bk(b_)], banks[b_][:, c0:c0 + 16],
                           lhsT=BT[ri][:, 0, :], rhs=um[jj][:, OFF + TP:OFF + T], start=True, stop=True)
                slot = 0
                for jj in range(4):
                    for ri, Sdst in enumerate((S_r, S_i)):
                        b_ = pb[slot // 3]
                        c0 = (slot % 3) * NKC
                        slot += 1
                        OP(S, "act", "activation", [bk(b_)], [("S", ri, m)], out=Sdst[:, 4 * m + jj, 1:NKC + 1],
                           in_=banks[b_][:, c0:c0 + NKC], func=AF.Copy)
                OP(S, "act", "activation", [bk(pb[2])], [("bus", m)],
                   out=bus[:, :, 4 * m:4 * m + 4, :].rearrange("p r i c -> p i r c"),
                   in_=banks[pb[2]][:, 258:386].rearrange("p (i r c) -> p i r c", i=4, r=2), func=AF.Copy)

        if SSTOP <= 1:
            sst_.close()
            return
        with Phase() as ph:
            S = ph.S
            tt_ = ph.sb([128, 4, 32])
            ts_ = ph.sb([128, 4, 32, 4])
            A_r = pw_r[:, 16, :]
            A_i = pw_i[:, 16, :]
            ar_b = aa[:, 0, :].unsqueeze(2).broadcast_to([128, 32, 4])
            ai_b = aa[:, 1, :].unsqueeze(2).broadcast_to([128, 32, 4])
            for stp in range(4):
                if stp == 0:
                    hr_p = sst[:, 0, j, :, :]
                    hi_p = sst[:, 1, j, :, :]
                else:
                    hr_p = hs[:, 0, :, :, stp - 1]
                    hi_p = hs[:, 1, :, :, stp - 1]
                bur = bus[:, 0, :, :].rearrange("p i (s t) -> p i s t", t=4)[:, :, :, stp]
                bui = bus[:, 1, :, :].rearrange("p i (s t) -> p i s t", t=4)[:, :, :, stp]
                hr_n = hs[:, 0, :, :, stp]
                hi_n = hs[:, 1, :, :, stp]
                e = "pool"
                OP(S, e, "tensor_tensor", [("hs", stp - 1)], ["ts0"], out=ts_[:, 0], in0=hr_p, in1=ar_b, op=ALU.mult)
                OP(S, e, "tensor_tensor", [("hs", stp - 1)], ["ts1"], out=ts_[:, 1], in0=hi_p, in1=ai_b, op=ALU.mult)
                OP(S, e, "tensor_tensor", ["ts0", "ts1"], ["ts0"], out=ts_[:, 0], in0=ts_[:, 0], in1=ts_[:, 1], op=ALU.subtract)
                OP(S, e, "tensor_tensor", ["ts0"], [("hsr", stp)], out=hr_n, in0=ts_[:, 0], in1=bur, op=ALU.add)
                OP(S, e, "tensor_tensor", [("hs", stp - 1)], ["ts2"], out=ts_[:, 2], in0=hi_p, in1=ar_b, op=ALU.mult)
                OP(S, e, "tensor_tensor", [("hs", stp - 1)], ["ts3"], out=ts_[:, 3], in0=hr_p, in1=ai_b, op=ALU.mult)
                OP(S, e, "tensor_tensor", ["ts2", "ts3"], ["ts2"], out=ts_[:, 2], in0=ts_[:, 2], in1=ts_[:, 3], op=ALU.add)
                OP(S, e, "tensor_tensor", ["ts2", ("hsr", stp)], [("hs", stp)], out=hi_n, in0=ts_[:, 2], in1=bui, op=ALU.add)
            fin_s = ph.sb([128, 2, 32, 4])
            OP(S, "pool", "tensor_copy", [("hs", 3)], ["fin_s"], out=fin_s[:], in_=hs[:, :, :, :, 3])
            DMA(S, ssm_s[:, :, j, :, :], fin_s[:], ["fin_s"], [])
            for k in range(NKC):
                sr = S_r[:, :, k]
                si = S_i[:, :, k]
                nr = S_r[:, :, k + 1]
                ni = S_i[:, :, k + 1]
                kk = ("Sk", k)
                kn = ("Sk", k + 1)
                OP(S, "dve", "tensor_tensor", [kk], ["t0"], out=tt_[:, 0], in0=sr, in1=A_r, op=ALU.mult)
                OP(S, "dve", "tensor_tensor", [kk], ["t1"], out=tt_[:, 1], in0=si, in1=A_i, op=ALU.mult)
                OP(S, "dve", "tensor_tensor", ["t0", "t1"], ["t0"], out=tt_[:, 0], in0=tt_[:, 0], in1=tt_[:, 1], op=ALU.subtract)
                OP(S, "dve", "tensor_tensor", ["t0"], [("Skr", k + 1)], out=nr, in0=nr, in1=tt_[:, 0], op=ALU.add)
                OP(S, "dve", "tensor_tensor", [kk], ["t2"], out=tt_[:, 2], in0=si, in1=A_r, op=ALU.mult)
                OP(S, "dve", "tensor_tensor", [kk], ["t3"], out=tt_[:, 3], in0=sr, in1=A_i, op=ALU.mult)
                OP(S, "dve", "tensor_tensor", ["t2", "t3"], ["t2"], out=tt_[:, 2], in0=tt_[:, 2], in1=tt_[:, 3], op=ALU.add)
                OP(S, "dve", "tensor_tensor", ["t2", ("Skr", k + 1)], [kn], out=ni, in0=ni, in1=tt_[:, 2], op=ALU.add)
            fin_p = ph.sb([128, 2, 32])
            OP(S, "dve", "tensor_copy", [("Sk", NKC)], ["fin_p"], out=fin_p[:, 0, :], in_=S_r[:, :, NKC])
            OP(S, "dve", "tensor_copy", [("Sk", NKC)], ["fin_p"], out=fin_p[:, 1, :], in_=S_i[:, :, NKC])
            DMA(S, ssm_p[:, :, j, :], fin_p[:], ["fin_p"], [])

        if SSTOP <= 2:
            sst_.close()
            return
        with Phase() as ph:
            S = ph.S
            cms = [ph.sb([128, 2, 4, 32]) for _ in range(2)]
            t1 = ph.sb([128, 17, 4, 32])
            t2 = ph.sb([128, 17, 4, 32])
            Er = ph.sb([128, 17, 4, 32], BF16)
            nEi = ph.sb([128, 17, 4, 32], BF16)
            bb16 = ph.sb([128, 2, 4, 32], BF16)
            Kb = ph.sb([128, 16, 128], BF16)
            Sb = ph.sb([128, 2, 4, NKC + 1], BF16)
            hb = ph.sb([128, 2, 4, 16], BF16)
            yt = [ph.sb([128, 3, NKC]) for _ in range(2)]
            y2 = [ph.sb([128, 3, NKC]) for _ in range(2)]
            y3 = [ph.sb([128, 3, NKC]) for _ in range(2)]
            OP(S, "pool", "memset", [], ["Kb"], Kb[:], 0.0)
            for m in range(NCH):
                cmt = cms[m % 2]
                DMA(S, cmt[:], cm_d[:, :, j, 4 * m:4 * m + 4, :], [], [("cm", m % 2)])
                bb = bbar(ph, m, eng="pool")
                OP(S, "pool", "tensor_copy", ["bbr", "bbi"], ["bb16"], out=bb16[:], in_=bb[:])
                OP(S, "pool", "tensor_copy", [], ["Sb"], out=Sb[:, 0], in_=S_r[:, 4 * m:4 * m + 4, :])
                OP(S, "pool", "tensor_copy", [], ["Sb"], out=Sb[:, 1], in_=S_i[:, 4 * m:4 * m + 4, :])
                OP(S, "pool", "tensor_copy", [], ["hb"], out=hb[:].rearrange("p r i (s t) -> p r i s t", t=4),
                   in_=hs[:, :, 4 * m:4 * m + 4, :, :])
                pr = pw_r[:, :, 4 * m:4 * m + 4].unsqueeze(3).broadcast_to([128, 17, 4, 32])
                pi_ = pw_i[:, :, 4 * m:4 * m + 4].unsqueeze(3).broadcast_to([128, 17, 4, 32])
                cr = cmt[:, 0].unsqueeze(1).broadcast_to([128, 17, 4, 32])
                ci = cmt[:, 1].unsqueeze(1).broadcast_to([128, 17, 4, 32])
                ck = ("cm", m % 2)
                OP(S, "dve", "tensor_tensor", [ck], ["t1"], out=t1[:], in0=pr, in1=cr, op=ALU.mult)
                OP(S, "dve", "tensor_tensor", [ck], ["t2"], out=t2[:], in0=pi_, in1=ci, op=ALU.mult)
                OP(S, "dve", "tensor_tensor", ["t1", "t2"], ["Er"], out=Er[:], in0=t1[:], in1=t2[:], op=ALU.subtract)
                OP(S, "dve", "tensor_tensor", [ck], ["t1"], out=t1[:], in0=pr, in1=ci, op=ALU.mult)
                OP(S, "dve", "tensor_tensor", [ck], ["t2"], out=t2[:], in0=pi_, in1=cr, op=ALU.mult)
                OP(S, "dve", "scalar_tensor_tensor", ["t1", "t2"], ["nEi"], out=nEi[:].rearrange("p a b c -> p (a b c)"),
                   in0=t1[:].rearrange("p a b c -> p (a b c)"), scalar=-1.0, in1=t2[:].rearrange("p a b c -> p (a b c)"),
                   op0=ALU.mult, op1=ALU.subtract)
                for jj in range(4):
                    o = banks[6][32 * jj:32 * jj + 32, :]
                    OP(S, "pe", "matmul", ["bb16", "Er"], [bk(6)], o, lhsT=bb16[:, 0, jj, :], rhs=Er[:, 0:16, jj, :],
                       start=True, stop=False, tile_position=(0, 32 * jj))
                    OP(S, "pe", "matmul", ["bb16", "nEi"], [bk(6)], o, lhsT=bb16[:, 1, jj, :], rhs=nEi[:, 0:16, jj, :],
                       start=False, stop=True, tile_position=(0, 32 * jj))
                for jj in range(4):
                    OP(S, "act", "activation", [bk(6)], ["Kb"], out=Kb[32 * jj:32 * jj + 32, :, 32 * jj:32 * jj + 32],
                       in_=banks[6][32 * jj:32 * jj + 32, :].rearrange("p (d c) -> p d c", c=32), func=AF.Copy)
                ukey = ("u16", m)
                for tq in range(16):
                    b_ = tq // 3
                    c0 = (tq % 3) * NKC
                    for tp in range(tq + 1):
                        OP(S, "pe", "matmul", ["Kb", ukey], [bk(b_)], banks[b_][:, c0:c0 + NKC], lhsT=Kb[:, tq - tp, :],
                           rhs=u16[:, m, OFF + tp:OFF + tp + TP:16], start=(tp == 0), stop=False)
                    for jj in range(4):
                        o = banks[b_][32 * jj:32 * jj + 32, c0:c0 + NKC]
                        OP(S, "pe", "matmul", ["Er", "Sb"], [bk(b_)], o, lhsT=Er[:, tq + 1, jj, :], rhs=Sb[:, 0, jj, 0:NKC],
                           start=False, stop=False, tile_position=(0, 32 * jj))
                        OP(S, "pe", "matmul", ["nEi", "Sb"], [bk(b_)], o, lhsT=nEi[:, tq + 1, jj, :], rhs=Sb[:, 1, jj, 0:NKC],
                           start=False, stop=True, tile_position=(0, 32 * jj))
                for jj in range(4):
                    o = banks[7][32 * jj:32 * jj + 32, 0:16]
                    OP(S, "pe", "matmul", ["Er", "hb"], [bk(7)], o, lhsT=Er[:, 0, jj, :], rhs=hb[:, 0, jj, :],
                       start=True, stop=False, tile_position=(0, 32 * jj))
                    OP(S, "pe", "matmul", ["nEi", "hb"], [bk(7)], o, lhsT=nEi[:, 0, jj, :], rhs=hb[:, 1, jj, :],
                       start=False, stop=True, tile_position=(0, 32 * jj))
                dsc = ssmd[:, j, m:m + 1]
                uv = u16[:, m, OFF:OFF + TP].rearrange("p (k t) -> p t k", t=16)
                groups = [(b_, 3 if b_ < 5 else 1, uv[:, 3 * b_:3 * b_ + (3 if b_ < 5 else 1), :]) for b_ in range(6)]
                groups.append((7, 0, None))
                for gi, (b_, ntq, uview) in enumerate(groups):
                    a = yt[gi % 2]
                    b2 = y2[gi % 2]
                    b3 = y3[gi % 2]
                    if uview is None:
                        av = a[:, 0, 0:16]
                        b2v = b2[:, 0, 0:16]
                        b3v = b3[:, 0, 0:16]
                        pv = banks[7][:, 0:16]
                        uview = u16[:, m, OFF + TP:OFF + T]
                    else:
                        av = a[:, 0:ntq, :]
                        b2v = b2[:, 0:ntq, :]
                        b3v = b3[:, 0:ntq, :]
                        pv = banks[b_][:, 0:ntq * NKC].rearrange("p (t k) -> p t k", k=NKC)
                    ka = ("yt", gi % 2)
                    k2 = ("y2", gi % 2)
                    k3 = ("y3", gi % 2)
                    OP(S, "dve", "scalar_tensor_tensor", [bk(b_), ukey], [ka], out=av, in0=uview, scalar=dsc, in1=pv,
                       op0=ALU.mult, op1=ALU.add)
                    OP(S, "act", "activation", [ka], [k2], out=b2v, in_=av, func=AF.Square)
                    OP(S, "dve", "tensor_scalar", [k2], [k2], out=b2v, in0=b2v, scalar1=0.044715, scalar2=1.0,
                       op0=ALU.mult, op1=ALU.add)
                    OP(S, "dve", "tensor_tensor", [k2, ka], [k3], out=b3v, in0=b2v, in1=av, op=ALU.mult)
                    OP(S, "act", "activation", [k3], [k2], out=b2v, in_=b3v, func=AF.Sigmoid, scale=1.5957691216057308)
                    OP(S, "dve", "tensor_tensor", [k2, ka], [ukey], out=uview, in0=b2v, in1=av, op=ALU.mult)

        sst_.close()
        if SSTOP <= 3:
            return
        with Phase() as ph:
            S = ph.S
            ws = WStream(ph, 128, nst=4, nbf=4, name="wg")
            sg = [ph.sb([128, 416]) for _ in range(2)]
            tg = [ph.sb([128, 416]) for _ in range(2)]
            it = 0
            gl = {}

            def gload(mo_):
                gl[mo_] = ws.load(w_glu[j, :, mo_ * 128:(mo_ + 1) * 128]) + ws.load(w_glu[j, :, D + mo_ * 128:D + (mo_ + 1) * 128])
            gload(0)
            for mo in range(NCH):
                if mo + 1 < NCH:
                    gload(mo + 1)
                w1, k1, w2, k2 = gl.pop(mo)
                for ti, (n0, nn) in enumerate(NT):
                    ba = 2 * (it % 2)
                    bb_ = ba + 1
                    for k in range(NCH):
                        OP(S, "pe", "matmul", [k1, "u16"], [bk(ba)], banks[ba][:, 0:nn], lhsT=w1[:, k, :],
                           rhs=u16[:, k, OFF + n0:OFF + n0 + nn], start=(k == 0), stop=(k == NCH - 1))
                    for k in range(NCH):
                        OP(S, "pe", "matmul", [k2, "u16"], [bk(bb_)], banks[bb_][:, 0:nn], lhsT=w2[:, k, :],
                           rhs=u16[:, k, OFF + n0:OFF + n0 + nn], start=(k == 0), stop=(k == NCH - 1))
                    OP(S, "act", "activation", [bk(bb_)], [("sg", it % 2)], out=sg[it % 2][:, 0:nn], in_=banks[bb_][:, 0:nn],
                       func=AF.Sigmoid)
                    OP(S, "dve", "tensor_tensor", [bk(ba), ("sg", it % 2)], [("tg", it % 2)], out=tg[it % 2][:, 0:nn],
                       in0=banks[ba][:, 0:nn], in1=sg[it % 2][:, 0:nn], op=ALU.mult)
                    OP(S, "dve", "tensor_tensor", [("tg", it % 2)], [("x", mo, ti)], out=x[:, mo, n0:n0 + nn],
                       in0=x[:, mo, n0:n0 + nn], in1=tg[it % 2][:, 0:nn], op=ALU.add)
                    it += 1

    def attn_layer(li):
        ast = contextlib.ExitStack()

        def sba(name, shape, dt=F32):
            return ast.enter_context(nc.sbuf_tensor(name, list(shape), dt))
        QsT = sba("QsT", [128, NCH, TS], BF16)
        KsT = sba("KsT", [128, NCH, TS], BF16)
        vs_tok = sba("vs_tok", [TS, D], BF16)
        abias = sba("abias_sb", [128, 16])
        tri = sba("tri_sb", [128, 128], BF16)
        tmk = sba("tm_sb", [128, 1088], BF16)
        QT = [(0, 416), (416, 416), (832, 416), (1248, 416), (1664, 400)]
        with Phase() as ph:
            S = ph.S
            cst = ph.sb([128, 1088])
            DMA(S, abias[:], abias_d, [], ["abias"])
            DMA(S, cst[:, 0:128], tri_d, [], ["cst"])
            OP(S, "dve", "tensor_copy", ["cst"], ["tri"], out=tri[:], in_=cst[:, 0:128])
            DMA(S, cst[:], tm_d, ["cst"], ["cst"])
            OP(S, "dve", "tensor_copy", ["cst"], ["tmk"], out=tmk[:], in_=cst[:])
            ws = WStream(ph, 128, nst=2, nbf=4, name="wq")
            wos = ph.sb([128, D])
            wob = ph.sb([128, D], BF16)
            qT = ph.sb([128, T], BF16)
            kT = ph.sb([128, T], BF16)
            OT = ph.sb([128, TP], BF16)
            vtok = ph.sb([128, 17, 128], BF16)
            kst = [ph.sb([128, 4, 128]) for _ in range(2)]
            vst = [ph.sb([128, 4, 128]) for _ in range(2)]
            eb = [ph.sb([128, 416]) for _ in range(3)]
            spb = [ph.sb([128, 416], BF16) for _ in range(3)]
            wb = [ph.sb([128, 416], BF16) for _ in range(3)]
            Ss16 = ph.sb([128, 416], BF16)
            ntri = ph.sb([128, 128], BF16)
            nones = ph.sb([128, 128], BF16)
            OP(S, "dve", "tensor_scalar", ["tri"], ["ntri"], out=ntri[:], in0=tri[:], scalar1=-1.0, scalar2=None, op0=ALU.mult)
            OP(S, "dve", "memset", [], ["nones"], nones[:], -1.0)
            zc = 0
            oc = 0
            pc = 0
            sti = 0
            for c in range(NCH):
                wq, kq = ws.load(w_qkv[:, c * 128:(c + 1) * 128])
                wk, kk = ws.load(w_qkv[:, D + c * 128:D + (c + 1) * 128])
                wv, kv = ws.load(w_qkv[:, 2 * D + c * 128:2 * D + (c + 1) * 128])
                DMA(S, wos[:], w_o[c * 128:(c + 1) * 128, :], [], ["wos"])
                OP(S, "pool", "tensor_copy", ["wos"], ["wob"], out=wob[:], in_=wos[:])
                for w_, kw_, dst, dk in ((wq, kq, qT, "qT"), (wk, kk, kT, "kT")):
                    for ti, (n0, nn) in enumerate(NT):
                        b_ = 4 + pc % 2
                        pc += 1
                        for k in range(NCH):
                            OP(S, "pe", "matmul", [kw_, "u16"], [bk(b_)], banks[b_][:, 0:nn], lhsT=w_[:, k, :],
                               rhs=u16[:, k, OFF + n0:OFF + n0 + nn], start=(k == 0), stop=(k == NCH - 1))
                        OP(S, "act", "activation", [bk(b_)], [(dk, ti)], out=dst[:, n0:n0 + nn], in_=banks[b_][:, 0:nn],
                           func=AF.Copy, scale=(0.125 if dk == "qT" else 1.0))
                OP(S, "pool", "tensor_copy", [("qT", 4)], ["QsT"], out=QsT[:, c, :], in_=qT[:, TP:T])
                OP(S, "pool", "tensor_copy", [("kT", 4)], ["KsT"], out=KsT[:, c, :], in_=kT[:, TP:T])
                for g in range(5):
                    nq4 = 4 if g < 4 else 1
                    for w_, kw_, which in ((wk, kk, "k"), (wv, kv, "v")):
                        b_ = 4 + pc % 2
                        pc += 1
                        for q in range(nq4):
                            tb = 4 * g + q
                            nt = 128 if tb < 16 else 32
                            for k in range(NCH):
                                OP(S, "pe", "matmul", [kw_, "u16"], [bk(b_)], banks[b_][0:nt, q * 128:(q + 1) * 128],
                                   lhsT=u16[:, k, OFF + tb * 128:OFF + tb * 128 + nt], rhs=w_[:, k, :],
                                   start=(k == 0), stop=(k == NCH - 1))
                        np_ = 128 if g < 4 else 32
                        stg_ = (kst if which == "k" else vst)[sti % 2]
                        sk = (which + "st", sti % 2)
                        srcv = banks[b_][0:np_, 0:nq4 * 128].rearrange("p (q n) -> p q n", n=128)
                        OP(S, "act", "activation", [bk(b_)], [sk], out=stg_[0:np_, 0:nq4, :], in_=srcv, func=AF.Copy)
                        if which == "v":
                            OP(S, "act", "activation", [bk(b_)], [("vtok", g)], out=vtok[0:np_, 4 * g:4 * g + nq4, :], in_=srcv,
                               func=AF.Copy)
                        dstd = k_p if which == "k" else v_p
                        dsts = k_s if which == "k" else v_s
                        if g < 4:
                            DMA(S, dstd[g * 512:(g + 1) * 512, c * 128:(c + 1) * 128].rearrange("(q p) n -> p q n", p=128),
                                stg_[:, 0:4, :], [sk], [])
                        else:
                            DMA(S, dstd[2048:TP, c * 128:(c + 1) * 128], stg_[0:16, 0, :], [sk], [])
                            DMA(S, dsts[:, c * 128:(c + 1) * 128], stg_[16:32, 0, :], [sk], [])
                    sti += 1
                b_ = 4 + pc % 2
                pc += 1
                for k in range(NCH):
                    OP(S, "pe", "matmul", [kv, "u16"], [bk(b_)], banks[b_][0:TS, 0:128], lhsT=u16[:, k, OFF + TP:OFF + T],
                       rhs=wv[:, k, :], start=(k == 0), stop=(k == NCH - 1))
                OP(S, "act", "activation", [bk(b_)], ["vs_tok"], out=vs_tok[:, c * 128:(c + 1) * 128], in_=banks[b_][0:TS, 0:128],
                   func=AF.Copy)
                blks = []
                for hh in range(2):
                    for qi, (q0, nq) in enumerate(QT):
                        kbl = [sb_ for sb_ in range(17) if sb_ * 128 < q0 + nq]
                        bo = 6 + oc % 2
                        oc += 1
                        for bi_, sb_ in enumerate(reversed(kbl)):
                            s0 = sb_ * 128
                            ns = min(128, TP - s0)
                            blks.append(dict(hh=hh, qi=qi, q0=q0, nq=nq, sb=sb_, s0=s0, ns=ns, bo=bo, first=(bi_ == 0),
                                             last=(bi_ == len(kbl) - 1), mask=(s0 + ns - 1 >= q0)))
                nb = len(blks)

                def stageA(t):
                    B = blks[t]
                    r0 = 64 * B["hh"]
                    h = 2 * c + B["hh"]
                    ns, nq, q0, s0 = B["ns"], B["nq"], B["q0"], B["s0"]
                    bz = (zc0 + t) % 4
                    p3 = (zc0 + t) % 3
                    OP(S, "pe", "matmul", [("kT", t_) for t_ in range(5)] + [("qT", B["qi"])], [bk(bz)], banks[bz][0:ns, 0:nq],
                       lhsT=kT[r0:r0 + 64, s0:s0 + ns], rhs=qT[r0:r0 + 64, q0:q0 + nq], start=True, stop=True)
                    OP(S, "act", "activation", [bk(bz), "abias"], [("e", p3)], out=eb[p3][0:ns, 0:nq], in_=banks[bz][0:ns, 0:nq],
                       func=AF.Exp, scale=1.0, bias=abias[0:ns, h:h + 1])
                    OP(S, "act", "activation", [("e", p3)], [("sp", p3)], out=spb[p3][0:ns, 0:nq], in_=eb[p3][0:ns, 0:nq],
                       func=AF.Ln, bias=1.0, scale=1.0)
                    if B["mask"]:
                        mo_ = 544 - (s0 - q0)
                        OP(S, "dve", "tensor_tensor", [("sp", p3), "tmk"], [("sp", p3)], out=spb[p3][0:ns, 0:nq],
                           in0=spb[p3][0:ns, 0:nq], in1=tmk[0:ns, mo_:mo_ + nq], op=ALU.mult)

                def stageB(t):
                    B = blks[t]
                    h = 2 * c + B["hh"]
                    ns, nq, q0, s0 = B["ns"], B["nq"], B["q0"], B["s0"]
                    bz = (zc0 + t) % 4
                    p3 = (zc0 + t) % 3
                    OP(S, "pe", "matmul", ["ntri", ("sp", p3)], [bk(bz)], banks[bz][0:ns, 0:nq], lhsT=ntri[0:ns, 0:ns],
                       rhs=spb[p3][0:ns, 0:nq], start=False, stop=B["first"], skip_group_check=True)
                    if not B["first"]:
                        OP(S, "pe", "matmul", ["nones", "Ss16"], [bk(bz)], banks[bz][0:ns, 0:nq], lhsT=nones[:, 0:ns],
                           rhs=Ss16[:, 0:nq], start=False, stop=True, skip_group_check=True)
                    OP(S, "act", "activation", [bk(bz), "abias"], [("w", p3)], out=wb[p3][0:ns, 0:nq], in_=banks[bz][0:ns, 0:nq],
                       func=AF.Exp, scale=1.0, bias=abias[0:ns, h:h + 1])
                    if B["mask"]:
                        mo_ = 544 - (s0 - q0)
                        OP(S, "dve", "tensor_tensor", [("w", p3), "tmk"], [("w", p3)], out=wb[p3][0:ns, 0:nq],
                           in0=wb[p3][0:ns, 0:nq], in1=tmk[0:ns, mo_:mo_ + nq], op=ALU.mult)
                    if not B["last"]:
                        if B["first"]:
                            if ns < 128:
                                OP(S, "dve", "memset", [], ["Ss16"], Ss16[:], 0.0)
                            OP(S, "dve", "tensor_copy", [("sp", p3)], ["Ss16"], out=Ss16[0:ns, 0:nq], in_=spb[p3][0:ns, 0:nq])
                        else:
                            OP(S, "dve", "tensor_tensor", [("sp", p3), "Ss16"], ["Ss16"], out=Ss16[0:ns, 0:nq],
                               in0=Ss16[0:ns, 0:nq], in1=spb[p3][0:ns, 0:nq], op=ALU.add)

                def stageC(t):
                    B = blks[t]
                    r0 = 64 * B["hh"]
                    ns, nq, q0 = B["ns"], B["nq"], B["q0"]
                    p3 = (zc0 + t) % 3
                    bo = B["bo"]
                    OP(S, "pe", "matmul", [("vtok", B["sb"] // 4), ("w", p3)], [bk(bo)], banks[bo][r0:r0 + 64, 0:nq],
                       lhsT=vtok[0:ns, B["sb"], r0:r0 + 64], rhs=wb[p3][0:ns, 0:nq], start=B["first"], stop=B["last"],
                       tile_position=(0, r0))
                    if B["last"]:
                        OP(S, "act", "activation", [bk(bo)], [("OT", B["qi"])], out=OT[r0:r0 + 64, q0:q0 + nq],
                           in_=banks[bo][r0:r0 + 64, 0:nq], func=AF.Copy)
                zc0 = zc
                for t in range(nb + 2):
                    if t < nb:
                        stageA(t)
                    if 1 <= t <= nb:
                        stageB(t - 1)
                    if t >= 2:
                        stageC(t - 2)
                zc += nb
                for mo in range(NCH):
                    for qi, (q0, nq) in enumerate(QT):
                        b_ = 4 + pc % 2
                        pc += 1
                        OP(S, "pe", "matmul", ["wob", ("OT", qi)], [bk(b_)], banks[b_][:, 0:nq], lhsT=wob[:, mo * 128:(mo + 1) * 128],
                           rhs=OT[:, q0:q0 + nq], start=True, stop=True)
                        OP(S, "dve", "tensor_tensor", [bk(b_)], [("x", mo, qi)], out=x[:, mo, q0:q0 + nq], in0=banks[b_][:, 0:nq],
                           in1=x[:, mo, q0:q0 + nq], op=ALU.add)
        if ASTOP >= 1:
            attn_sample(QsT, KsT, vs_tok, abias, tri)
        ast.close()

    def attn_sample(QsT, KsT, vs_tok, abias, tri):
        with Phase() as ph:
            S = ph.S
            ptab = ph.sb([128, 256], I32)
            ptf = ph.sb([128, 256])
            idx = ph.sb([128, 256], I32)
            iot = ph.sb([128, 1])
            mcs = ph.sb([TS, 4, 64])
            mcur = ph.sb([TS, 4, 64], BF16)
            bt = ph.sb([128, 16, 4])
            Qblk = ph.sb([128, NCH, 64], BF16)
            kpg = [ph.sb([128, D]) for _ in range(3)]
            vpg = [ph.sb([128, D]) for _ in range(3)]
            KT = [ph.sb([128, NCH, 128], BF16) for _ in range(3)]
            Vb = [ph.sb([128, D], BF16) for _ in range(3)]
            zb = [ph.sb([128, 64]) for _ in range(3)]
            eb = [ph.sb([128, 64]) for _ in range(3)]
            gb = [ph.sb([128, 64]) for _ in range(3)]
            spb = [ph.sb([128, 64], BF16) for _ in range(3)]
            wb = [ph.sb([128, 64], BF16) for _ in range(3)]
            Ss32 = ph.sb([128, 64])
            Ss16 = ph.sb([128, 64], BF16)
            OsT = ph.sb([128, NCH, TS], BF16)
            wos = [ph.sb([128, D]) for _ in range(2)]
            wob = ph.sb([128, NCH, D], BF16)
            DMA(S, ptab[:], ptab_d, [], ["ptab"])
            DMA(S, iot[:], iota_d, [], ["iot"])
            DMA(S, mcs[:], mcur_d, [], ["mcs"])
            OP(S, "dve", "tensor_copy", ["mcs"], ["mcur"], out=mcur[:], in_=mcs[:])
            OP(S, "dve", "tensor_copy", ["ptab"], ["ptf"], out=ptf[:], in_=ptab[:])
            OP(S, "dve", "tensor_scalar", ["ptf", "iot"], ["ptf"], out=ptf[:], in0=ptf[:], scalar1=128.0, scalar2=iot[:, 0:1],
               op0=ALU.mult, op1=ALU.add)
            OP(S, "dve", "tensor_copy", ["ptf"], ["idx"], out=idx[:], in_=ptf[:])
            OP(S, "dve", "tensor_copy", ["abias"], ["bt"], out=bt[:], in_=abias[:].unsqueeze(2).broadcast_to([128, 16, 4]))
            btv = bt[:].rearrange("p h q -> p (h q)")
            pages = []
            for sq_ in range(4):
                plist = [-1] + list(range(63, -1, -1))
                for pi_, pg in enumerate(plist):
                    pages.append(dict(sq=sq_, pg=pg, first=(pi_ == 0), last=(pi_ == len(plist) - 1), ns=(TS if pg < 0 else 128)))
            npg = len(pages)

            def stA(t):
                Pg = pages[t]
                sq_, pg, ns = Pg["sq"], Pg["pg"], Pg["ns"]
                pz = t % 2
                p3 = t % 3
                bz = pz
                if Pg["first"]:
                    OP(S, "pool", "memset", [], ["Qblk"], Qblk[:], 0.0)
                    for c in range(NCH):
                        for hh in range(2):
                            h = 2 * c + hh
                            OP(S, "pool", "tensor_copy", ["Qblk"], ["Qblk"], out=Qblk[64 * hh:64 * hh + 64, c, 4 * h:4 * h + 4],
                               in_=QsT[64 * hh:64 * hh + 64, c, 4 * sq_:4 * sq_ + 4])
                if pg < 0:
                    for c in range(NCH):
                        OP(S, "pe", "matmul", ["Qblk"], [bk(bz)], banks[bz][0:ns, 0:64], lhsT=KsT[:, c, :], rhs=Qblk[:, c, :],
                           start=(c == 0), stop=(c == NCH - 1))
                else:
                    col = sq_ * 64 + pg
                    kp = kpg[p3]
                    vp = vpg[p3]
                    S.dma(lambda h_, kp=kp, col=col: h_.indirect_dma_start(
                        out=kp[:, :], out_offset=None, in_=cache_k,
                        in_offset=bass.IndirectOffsetOnAxis(ap=idx[:, col:col + 1], axis=0)), ["idx"], [("kpg", p3)], q="pool")
                    S.dma(lambda h_, vp=vp, col=col: h_.indirect_dma_start(
                        out=vp[:, :], out_offset=None, in_=cache_v,
                        in_offset=bass.IndirectOffsetOnAxis(ap=idx[:, col:col + 1], axis=0)), ["idx"], [("vpg", p3)], q="pool")
                    OP(S, "dve", "tensor_copy", [("vpg", p3)], [("Vb", p3)], out=Vb[p3][:], in_=vp[:])
                    for g in range(2):
                        b_ = 4 + g
                        for q in range(4):
                            c = 4 * g + q
                            OP(S, "pe", "transpose", [("kpg", p3), "idt"], [bk(b_)], out=banks[b_][:, q * 128:(q + 1) * 128],
                               in_=kp[:, c * 128:(c + 1) * 128], identity=idt[:])
                        OP(S, "act", "activation", [bk(b_)], [("KT", p3, g)],
                           out=KT[p3][:, 4 * g:4 * g + 4, :].rearrange("p c s -> p (c s)"), in_=banks[b_][:], func=AF.Copy)
                    for c in range(NCH):
                        OP(S, "pe", "matmul", ["Qblk", ("KT", p3, c // 4)], [bk(bz)], banks[bz][0:ns, 0:64], lhsT=KT[p3][:, c, :],
                           rhs=Qblk[:, c, :], start=(c == 0), stop=(c == NCH - 1))
                OP(S, "dve", "scalar_tensor_tensor", [bk(bz), "bt"], [("z", p3)], out=zb[p3][0:ns, :], in0=banks[bz][0:ns, 0:64],
                   scalar=1.0, in1=btv[0:ns, :], op0=ALU.mult, op1=ALU.add)
                OP(S, "act", "activation", [("z", p3)], [("e", p3)], out=eb[p3][0:ns, :], in_=zb[p3][0:ns, :], func=AF.Exp)
                OP(S, "act", "activation", [("e", p3)], [("sp", p3)], out=spb[p3][0:ns, :], in_=eb[p3][0:ns, :], func=AF.Ln,
                   bias=1.0, scale=1.0)
                if pg < 0:
                    OP(S, "dve", "tensor_tensor", [("sp", p3), "mcur"], [("sp", p3)], out=spb[p3][0:ns, :], in0=spb[p3][0:ns, :],
                       in1=mcur[:, sq_, :], op=ALU.mult)

            def stB(t):
                Pg = pages[t]
                sq_, pg, ns = Pg["sq"], Pg["pg"], Pg["ns"]
                pz = t % 2
                p3 = t % 3
                bc = 2 + pz
                first = Pg["first"]
                OP(S, "pe", "matmul", ["tri", ("sp", p3)], [bk(bc)], banks[bc][0:ns, 0:64], lhsT=tri[0:ns, 0:ns],
                   rhs=spb[p3][0:ns, :], start=True, stop=first)
                if not first:
                    OP(S, "pe", "matmul", ["ones_b", "Ss16"], [bk(bc)], banks[bc][0:ns, 0:64], lhsT=ones_b[:, 0:ns],
                       rhs=Ss16[:, :], start=False, stop=True)
                OP(S, "act", "activation", [bk(bc)], [("g", p3)], out=gb[p3][0:ns, :], in_=banks[bc][0:ns, 0:64], func=AF.Exp,
                   scale=-1.0)
                OP(S, "dve", "tensor_tensor", [("e", p3), ("g", p3)], [("w", p3)], out=wb[p3][0:ns, :], in0=eb[p3][0:ns, :],
                   in1=gb[p3][0:ns, :], op=ALU.mult)
                if pg < 0:
                    OP(S, "dve", "tensor_tensor", [("w", p3), "mcur"], [("w", p3)], out=wb[p3][0:ns, :], in0=wb[p3][0:ns, :],
                       in1=mcur[:, sq_, :], op=ALU.mult)
                if not Pg["last"]:
                    if first:
                        OP(S, "dve", "memset", [], ["Ss32"], Ss32[:], 0.0)
                    OP(S, "dve", "tensor_tensor", [("sp", p3), "Ss32"], ["Ss32"], out=Ss32[0:ns, :], in0=Ss32[0:ns, :],
                       in1=spb[p3][0:ns, :], op=ALU.add)
                    OP(S, "dve", "tensor_copy", ["Ss32"], ["Ss16"], out=Ss16[:], in_=Ss32[:])

            def stC(t):
                Pg = pages[t]
                sq_, pg, ns = Pg["sq"], Pg["pg"], Pg["ns"]
                p3 = t % 3
                bo = 6 + sq_ % 2
                for c in range(NCH):
                    if pg < 0:
                        lh = vs_tok[0:TS, c * 128:(c + 1) * 128]
                        vkey = "vs_tok"
                    else:
                        lh = Vb[p3][:, c * 128:(c + 1) * 128]
                        vkey = ("Vb", p3)
                    OP(S, "pe", "matmul", [vkey, ("w", p3)], [bk(bo)], banks[bo][:, c * 64:(c + 1) * 64], lhsT=lh,
                       rhs=wb[p3][0:ns, :], start=(Pg["first"] and c == 0), stop=(Pg["last"] and c == NCH - 1),
                       skip_group_check=True)
                if Pg["last"]:
                    for c in range(NCH):
                        for hh in range(2):
                            h = 2 * c + hh
                            OP(S, "act", "activation", [bk(bo)], ["OsT"], out=OsT[64 * hh:64 * hh + 64, c, 4 * sq_:4 * sq_ + 4],
                               in_=banks[bo][64 * hh:64 * hh + 64, c * 64 + 4 * h:c * 64 + 4 * h + 4], func=AF.Copy)
            for t in range(npg + 2):
                if t < npg:
                    stA(t)
                if 1 <= t <= npg:
                    stB(t - 1)
                if t >= 2:
                    stC(t - 2)
            for c in range(NCH):
                DMA(S, wos[c % 2][:], w_o[c * 128:(c + 1) * 128, :], [], [("wos", c % 2)])
                OP(S, "pool", "tensor_copy", [("wos", c % 2)], [("wob", c)], out=wob[:, c, :], in_=wos[c % 2][:])
            for mo in range(NCH):
                b_ = 4 + mo % 2
                for c in range(NCH):
                    OP(S, "pe", "matmul", [("wob", c), "OsT"], [bk(b_)], banks[b_][:, 0:TS], lhsT=wob[:, c, mo * 128:(mo + 1) * 128],
                       rhs=OsT[:, c, :], start=(c == 0), stop=(c == NCH - 1))
                OP(S, "dve", "tensor_tensor", [bk(b_)], [("xs", mo)], out=x[:, mo, TP:T], in0=banks[b_][:, 0:TS], in1=x[:, mo, TP:T],
                   op=ALU.add)

    def pool_layer(li):
        with Phase() as ph:
            S = ph.S
            rstd = norm_stats(ph)
            rk = [("rstd", ti) for ti in range(5)]
            E0 = ph.sb([128, 15 + TP])
            EA = ph.sb([128, 15 + TP])
            EB = ph.sb([128, 15 + TP])
            X0 = ph.sb([128, 4, 19])
            XA = ph.sb([128, 4, 19])
            XB = ph.sb([128, 4, 19])
            t16 = ph.sb([128, 16])
            pws = ph.sb([128, 4, 2, 256])
            pwb = ph.sb([128, 4, 2, 256], BF16)
            DMA(S, pws[:], pool_w.rearrange("g (ci p) e -> p g ci e", p=128), [], ["pws"])
            OP(S, "pool", "tensor_copy", ["pws"], ["pwb"], out=pwb[:], in_=pws[:])
            OP(S, "pool", "memset", [], ["E0"], E0[:, 0:15], 0.0)
            OP(S, "pool", "memset", [], ["EA"], EA[:, 0:15], 0.0)
            OP(S, "pool", "memset", [], ["EB"], EB[:, 0:15], 0.0)
            OP(S, "pool", "memset", [], ["XA"], XA[:], 0.0)
            OP(S, "pool", "memset", [], ["XB"], XB[:], 0.0)
            for c in range(NCH):
                gi = c // 2
                w = 2 << gi
                eng = "dve" if c % 2 == 0 else "pool"
                OP(S, "dve", "scalar_tensor_tensor", ["x"] + rk + ["E0"], ["E0"], out=E0[:, 15:15 + TP], in0=x[:, c, 0:TP],
                   scalar=gains[:, li, c:c + 1], in1=rstd[:, 0:TP], op0=ALU.mult, op1=ALU.mult)
                OP(S, "dve", "tensor_copy", ["spool"], ["X0"], out=X0[:, :, 0:15], in_=spool[:, c, :, :])
                OP(S, "dve", "scalar_tensor_tensor", ["x"] + rk + ["X0"], ["X0"], out=X0[:, :, 15:19],
                   in0=x[:, c, TP:T].rearrange("p (s t) -> p s t", t=4), scalar=gains[:, li, c:c + 1],
                   in1=rstd[:, TP:T].rearrange("p (s t) -> p s t", t=4), op0=ALU.mult, op1=ALU.mult)
                DMA(S, pool_p[:, c, :], E0[:, TP:TP + 15], ["E0"], [])
                DMA(S, pool_s[:, c, :, :], X0[:, :, 4:19], ["X0"], [])
                src, srck, xsrc, xsrck = E0, "E0", X0, "X0"
                bufs = [(EA, "EA", XA, "XA"), (EB, "EB", XB, "XB")]
                stp = 1
                bi = 0
                while stp < w:
                    dst, dstk, xdst, xdstk = bufs[bi % 2]
                    bi += 1
                    OP(S, eng, "tensor_tensor", [srck], [dstk], out=dst[:, stp:15 + TP], in0=src[:, stp:15 + TP],
                       in1=src[:, 0:15 + TP - stp], op=ALU.add)
                    OP(S, eng, "tensor_tensor", [xsrck], [xdstk], out=xdst[:, :, stp:19], in0=xsrc[:, :, stp:19],
                       in1=xsrc[:, :, 0:19 - stp], op=ALU.add)
                    src, srck, xsrc, xsrck = dst, dstk, xdst, xdstk
                    stp *= 2
                uk = ("u16", c)
                OP(S, "dve", "scalar_tensor_tensor", [srck, "E0"], [uk], out=u16[:, c, OFF:OFF + TP], in0=src[:, 15:15 + TP],
                   scalar=1.0 / w, in1=E0[:, 15:15 + TP], op0=ALU.mult, op1=ALU.subtract)
                OP(S, "dve", "tensor_tensor", [srck, "rcnt"], ["t16"], out=t16[:], in0=src[:, 15:31], in1=rcnt[:, gi, :], op=ALU.mult)
                OP(S, "dve", "tensor_tensor", ["t16", "E0"], [uk], out=u16[:, c, OFF:OFF + 16], in0=t16[:], in1=E0[:, 15:31],
                   op=ALU.subtract)
                OP(S, "dve", "scalar_tensor_tensor", [xsrck, "X0"], [uk],
                   out=u16[:, c, OFF + TP:OFF + T].rearrange("p (s t) -> p s t", t=4), in0=xsrc[:, :, 15:19],
                   scalar=1.0 / w, in1=X0[:, :, 15:19], op0=ALU.mult, op1=ALU.subtract)
            it = 0
            for gi in range(4):
                for co in range(2):
                    mo = 2 * gi + co
                    for ti, (n0, nn) in enumerate(NT):
                        b_ = it % 2
                        it += 1
                        for ci in range(2):
                            OP(S, "pe", "matmul", ["pwb", ("u16", 2 * gi + ci)], [bk(b_)], banks[b_][:, 0:nn],
                               lhsT=pwb[:, gi, ci, co * 128:(co + 1) * 128], rhs=u16[:, 2 * gi + ci, OFF + n0:OFF + n0 + nn],
                               start=(ci == 0), stop=(ci == 1))
                        OP(S, "dve", "scalar_tensor_tensor", [bk(b_), "pscale"], [("x", mo, ti)], out=x[:, mo, n0:n0 + nn],
                           in0=banks[b_][:, 0:nn], scalar=pscale[:, mo:mo + 1], in1=x[:, mo, n0:n0 + nn],
                           op0=ALU.mult, op1=ALU.add)

    def ffn_layer(li):
        groups = [list(range(0, 4)), list(range(4, 8)), list(range(8, 12)), list(range(12, 16)),
                  list(range(16, 19)), list(range(19, 22))]
        with Phase() as ph:
            S = ph.S
            ws = WStream(ph, 128, nst=2, nbf=4, name="wu")
            wds = [ph.sb([128, D]) for _ in range(2)]
            wdb = [ph.sb([128, D], BF16) for _ in range(8)]
            a_g = [ph.sb([128, T], BF16) for _ in range(4)]
            hbuf = [[ph.sb([128, 418]) for _ in range(2)] for _ in range(2)]
            tbuf = [[ph.sb([128, 416]) for _ in range(2)] for _ in range(2)]
            sgb = [ph.sb([128, 416]) for _ in range(2)]
            hse = [ph.sb([128, 4, 6]) for _ in range(2)]
            hlast = ph.sb([128, 44, 10])
            it = 0
            wdi = 0
            flat = [(gi_, cl, cc) for gi_, grp in enumerate(groups) for cl, cc in enumerate(grp)]
            loaded = {}

            def load_chunk(fi):
                gi_, cl, cc = flat[fi]
                wg_, kg = ws.load(w_up[li, :, cc * 128:(cc + 1) * 128])
                wv_, kv = ws.load(w_up[li, :, DFF + cc * 128:DFF + (cc + 1) * 128])
                sgi = fi % 2
                wslot = (gi_ % 2) * 4 + cl
                DMA(S, wds[sgi][:], w_down[li, cc * 128:(cc + 1) * 128, :], [], [("wds", sgi)])
                OP(S, "act", "activation", [("wds", sgi)], [("wdb", wslot)], out=wdb[wslot][:], in_=wds[sgi][:], func=AF.Copy)
                loaded[fi] = (wg_, kg, wv_, kv)
            load_chunk(0)
            fi = -1
            for gi_, grp in enumerate(groups):
                for cl, cc in enumerate(grp):
                    fi += 1
                    if fi + 1 < len(flat):
                        load_chunk(fi + 1)
                    wg_, kg, wv_, kv = loaded.pop(fi)
                    for ti, (n0, nn) in enumerate(NT):
                        par_ = it % 2
                        it += 1
                        for gv, (w_, kw_, chn) in enumerate(((wg_, kg, cc), (wv_, kv, 22 + cc))):
                            b_ = 2 * par_ + gv
                            hb_ = hbuf[gv][par_]
                            tb__ = tbuf[gv][par_]
                            kh = ("h", gv, par_)
                            kt = ("t", gv, par_)
                            for k in range(NCH):
                                OP(S, "pe", "matmul", [kw_, "u16"], [bk(b_)], banks[b_][:, 0:nn + 2], lhsT=w_[:, k, :],
                                   rhs=u16[:, k, n0:n0 + nn + 2], start=(k == 0), stop=(k == NCH - 1))
                            OP(S, "act", "activation", [bk(b_)], [kt], out=tb__[:, 0:nn], in_=banks[b_][:, 0:nn],
                               func=AF.Identity, scale=cw[:, li, 0, chn:chn + 1], bias=cb[:, li, chn:chn + 1])
                            OP(S, "dve", "scalar_tensor_tensor", [bk(b_), kt], [kt], out=tb__[:, 0:nn], in0=banks[b_][:, 1:nn + 1],
                               scalar=cw[:, li, 1, chn:chn + 1], in1=tb__[:, 0:nn], op0=ALU.mult, op1=ALU.add)
                            OP(S, "dve", "scalar_tensor_tensor", [bk(b_), kt], [kt], out=tb__[:, 0:nn], in0=banks[b_][:, 2:nn + 2],
                               scalar=cw[:, li, 2, chn:chn + 1], in1=tb__[:, 0:nn], op0=ALU.mult, op1=ALU.add)
                            if ti == 4:
                                he = hse[gv]
                                ke = ("hse", gv)
                                OP(S, "dve", "tensor_copy", [], [ke], out=he[:, :, 0:2],
                                   in_=cbuf[:, li, chn, :].rearrange("p (s r) -> p s r", r=2))
                                OP(S, "dve", "tensor_copy", [bk(b_)], [ke], out=he[:, :, 2:6],
                                   in_=banks[b_][:, 402:418].rearrange("p (s t) -> p s t", t=4))
                                OP(S, "dve", "tensor_copy", [bk(b_)], ["hlast"], out=hlast[:, chn, 0:2], in_=banks[b_][:, 400:402])
                                tv = tb__[:, 400:416].rearrange("p (s t) -> p s t", t=4)
                                OP(S, "dve", "tensor_scalar", [ke, kt], [kt], out=tv, in0=he[:, :, 0:4],
                                   scalar1=cw[:, li, 0, chn:chn + 1], scalar2=cb[:, li, chn:chn + 1], op0=ALU.mult, op1=ALU.add)
                                OP(S, "dve", "scalar_tensor_tensor", [ke, kt], [kt], out=tv, in0=he[:, :, 1:5],
                                   scalar=cw[:, li, 1, chn:chn + 1], in1=tv, op0=ALU.mult, op1=ALU.add)
                                OP(S, "dve", "scalar_tensor_tensor", [ke, kt], [kt], out=tv, in0=he[:, :, 2:6],
                                   scalar=cw[:, li, 2, chn:chn + 1], in1=tv, op0=ALU.mult, op1=ALU.add)
                                OP(S, "pool", "tensor_copy", [ke], ["hlast"],
                                   out=hlast[:, chn, 2:10].rearrange("p (s r) -> p s r", r=2), in_=he[:, :, 4:6])
                        ksg = ("sg", par_)
                        OP(S, "act", "activation", [("t", 0, par_)], [ksg], out=sgb[par_][:, 0:nn], in_=tbuf[0][par_][:, 0:nn],
                           func=AF.Silu)
                        OP(S, "pool", "tensor_tensor", [ksg, ("t", 1, par_)], [("a", cl, ti)], out=a_g[cl][:, n0:n0 + nn],
                           in0=sgb[par_][:, 0:nn], in1=tbuf[1][par_][:, 0:nn], op=ALU.mult)
                dit = 0
                for mo in range(NCH):
                    for ti, (n0, nn) in enumerate(NT):
                        b_ = 4 + dit % 2
                        dit += 1
                        for cl in range(len(grp)):
                            wsl = (gi_ % 2) * 4 + cl
                            OP(S, "pe", "matmul", [("wdb", wsl), ("a", cl, ti)], [bk(b_)], banks[b_][:, 0:nn],
                               lhsT=wdb[wsl][:, mo * 128:(mo + 1) * 128], rhs=a_g[cl][:, n0:n0 + nn],
                               start=(cl == 0), stop=(cl == len(grp) - 1))
                        OP(S, "dve", "tensor_tensor", [bk(b_)], [("x", mo, ti)], out=x[:, mo, n0:n0 + nn],
                           in0=banks[b_][:, 0:nn], in1=x[:, mo, n0:n0 + nn], op=ALU.add)
            DMA(S, conv_o[:, li, :, :], hlast[:], ["hlast"], [])

    def final_out():
        with Phase() as ph:
            S = ph.S
            rstd = norm_stats(ph)
            rk = [("rstd", ti) for ti in range(5)]
            for c in range(NCH):
                OP(S, "dve", "scalar_tensor_tensor", ["x"] + rk, [("xo", c)], out=x[:, c, :], in0=x[:, c, :],
                   scalar=gains[:, 8, c:c + 1], in1=rstd[:, :], op0=ALU.mult, op1=ALU.mult)
        with Phase() as ph:
            S = ph.S
            yst = [ph.sb([128, D]) for _ in range(2)]
            oblocks = [(y_p[b * 128:(b + 1) * 128, :], 128, 16 + b * 128) for b in range(16)]
            oblocks.append((y_s, 16, TP))
            for bi, (dst, nr, c0) in enumerate(oblocks):
                yt_ = yst[bi % 2]
                for g in range(2):
                    b_ = (bi * 2 + g) % 4
                    for jj in range(4):
                        c = g * 4 + jj
                        OP(S, "pe", "transpose", ["idt"], [bk(b_)], out=banks[b_][0:nr, jj * 128:(jj + 1) * 128],
                           in_=x[:, c, c0:c0 + nr], identity=idt[:])
                    if g == 0:
                        OP(S, "act", "activation", [bk(b_)], [("yst", bi % 2, g)], out=yt_[0:nr, g * 512:(g + 1) * 512],
                           in_=banks[b_][0:nr, :], func=AF.Copy)
                    else:
                        OP(S, "dve", "tensor_copy", [bk(b_)], [("yst", bi % 2, g)], out=yt_[0:nr, g * 512:(g + 1) * 512],
                           in_=banks[b_][0:nr, :])
                DMA(S, dst, yt_[0:nr, :], [("yst", bi % 2, 0), ("yst", bi % 2, 1)], [])

    for li in range(DEPTH):
        if li * 10 > STOP:
            break
        if li % 3 == 0:
            norm_to_u16(li)
            ssm_layer(li // 3, li)
        elif li % 3 == 1:
            pool_layer(li)
        else:
            norm_to_u16(li)
            attn_layer(li)
        if li * 10 + 0 >= STOP:
            break
        norm_to_u16(4 + li)
        ffn_layer(li)
        if li * 10 + 1 >= STOP:
            break
    final_out()
    pst.close()
    return nc


_NC = None


def prep_core(inp, c):
    f = np.float32

    def fm(v):
        v = np.asarray(v, f)
        lead = v.shape[:-1]
        return np.ascontiguousarray(np.moveaxis(v.reshape(lead + (NCH, 128)), -1, 0))
    m = {}
    m["xp"] = np.ascontiguousarray(inp["x_prompt"][c])
    m["xs"] = np.ascontiguousarray(inp["x_sample"][4 * c:4 * c + 4].reshape(TS, D))
    m["meta"] = np.ascontiguousarray(inp["meta_tokens"])
    m["ident"] = np.eye(128, dtype=f)
    g = np.concatenate([inp["norm_mix_g"], inp["norm_ffn_g"], inp["norm_final_g"][None]], 0)
    m["gains"] = fm(g)
    cwv = inp["ffn_conv_w"].reshape(DEPTH, 3, 44, 128)
    m["cw"] = np.ascontiguousarray(cwv.transpose(3, 0, 1, 2))
    m["cb"] = np.ascontiguousarray(inp["ffn_conv_b"].reshape(DEPTH, 44, 128).transpose(2, 0, 1))
    cbv = inp["state_ffn_conv"][:, 4 * c:4 * c + 4]
    m["cbuf"] = np.ascontiguousarray(cbv.reshape(DEPTH, 8, 44, 128).transpose(3, 0, 2, 1))
    m["ssmd"] = fm(inp["ssm_d"])
    lam = np.stack([inp["ssm_lambda_re"], inp["ssm_lambda_im"],
                    np.broadcast_to(inp["ssm_log_dt"][:, :, None], (2, 64, 64))], 0)
    m["par"] = np.ascontiguousarray(lam.reshape(3, 2, 32, 2, 64).transpose(3, 4, 0, 1, 2).reshape(128, 3, 2, 32))
    st = np.stack([inp["state_ssm_re"][:, 4 * c:4 * c + 4], inp["state_ssm_im"][:, 4 * c:4 * c + 4]], 0)
    m["sst"] = np.ascontiguousarray(st.reshape(2, 2, 4, 32, 2, 64).transpose(4, 5, 0, 1, 3, 2).reshape(128, 2, 2, 32, 4))
    B = np.stack([inp["ssm_b_re"], inp["ssm_b_im"]], 0).reshape(2, 2, 32, 2, 64, 16)
    bm = np.zeros((2, 64, 2, 2, 32, 2, 16), f)
    for g2 in range(2):
        bm[g2, :, :, :, :, g2, :] = B[:, :, :, g2].transpose(3, 0, 1, 2, 4)
    m["bm"] = bm.reshape(128, 2, 2, 32, 32)
    C = np.stack([inp["ssm_c_re"], inp["ssm_c_im"]], 0).reshape(2, 2, 32, 2, 16, 64)
    cm = np.zeros((2, 64, 2, 2, 32, 2, 16), f)
    for g2 in range(2):
        cm[g2, :, :, :, :, g2, :] = C[:, :, :, g2].transpose(4, 0, 1, 2, 3)
    m["cm"] = cm.reshape(128, 2, 2, 32, 32)
    rm = np.zeros((128, 4), f)
    for jj in range(4):
        rm[32 * jj:32 * jj + 32, jj] = 1.0
    m["rowmask"] = rm
    sp_ = inp["state_pool"][0, 4 * c:4 * c + 4]
    m["spool"] = np.ascontiguousarray(sp_.reshape(4, 15, NCH, 128).transpose(3, 2, 0, 1))
    m["pscale"] = fm(inp["pool_scale"][0])
    rc = np.zeros((128, 4, 16), f)
    for gi in range(4):
        for t in range(16):
            rc[:, gi, t] = 1.0 / min(2 << gi, t + 1)
    m["rcnt"] = rc
    m["pool_w"] = inp["pool_w"][0]
    m["w_qkv"] = inp["attn_w_qkv"][0]
    m["w_o"] = inp["attn_w_o"][0]
    m["abias"] = np.ascontiguousarray(np.broadcast_to(inp["attn_logit_bias"][0][None, :], (128, 16))).astype(f)
    pidx = np.arange(128)
    m["tri"] = (pidx[:, None] >= pidx[None, :]).astype(f)
    jx = np.arange(1088)
    m["tm"] = ((jx[None, :] - 544) > pidx[:, None]).astype(f)
    mc = np.zeros((16, 4, 16, 4), f)
    for sq_ in range(4):
        for r in range(4):
            for q_ in range(4):
                if r < q_:
                    mc[4 * sq_ + r, sq_, :, q_] = 1.0
    m["mcur"] = mc.reshape(16, 4, 64)
    pt = inp["page_table"][4 * c:4 * c + 4].reshape(1, 256).astype(np.int32)
    m["ptab"] = np.ascontiguousarray(np.broadcast_to(pt, (128, 256)))
    m["iota"] = np.arange(128, dtype=f).reshape(128, 1)
    if "cache_k" in inp:
        m["cache_k"] = inp["cache_k"].reshape(2560 * 128, D)
        m["cache_v"] = inp["cache_v"].reshape(2560 * 128, D)
    m["w_glu"] = inp["ssm_w_glu"]
    m["w_up"] = inp["ffn_w_up"]
    m["w_down"] = inp["ffn_w_down"]
    return m


def kernel(**inp):
    global _NC
    inp = {k: np.asarray(v) for k, v in inp.items()}
    if _NC is None:
        _NC = build_nc()
    nc = _NC
    in_maps = [prep_core(inp, c) for c in range(NCORES)]
    res = run_bass_kernel_spmd(nc, in_maps, core_ids=list(range(NCORES)))
    R = res.results
    y_prompt = np.stack([R[c]["y_p"] for c in range(NCORES)])
    y_sample = np.stack([R[c]["y_s"].reshape(4, 4, D) for c in range(NCORES)]).reshape(32, 4, D)
    sp = np.stack([R[c]["ssm_p"] for c in range(NCORES)])
    sp = sp.reshape(NCORES, 2, 64, 2, 2, 32).transpose(3, 4, 0, 5, 1, 2).reshape(2, 2, NCORES, 64, 64)
    ss = np.stack([R[c]["ssm_s"] for c in range(NCORES)])
    ss = ss.reshape(NCORES, 2, 64, 2, 2, 32, 4).transpose(3, 4, 0, 6, 5, 1, 2).reshape(2, 2, 32, 64, 64)
    co = np.stack([R[c]["conv_o"] for c in range(NCORES)])
    co = co.transpose(2, 0, 4, 3, 1).reshape(DEPTH, NCORES, 10, 2 * DFF)
    conv_prompt = np.ascontiguousarray(co[:, :, 0:2])
    conv_sample = np.ascontiguousarray(co[:, :, 2:10].reshape(DEPTH, NCORES, 4, 2, 2 * DFF).reshape(DEPTH, 32, 2, 2 * DFF))
    pp_ = np.stack([R[c]["pool_p"] for c in range(NCORES)])
    pool_prompt = np.ascontiguousarray(pp_.transpose(0, 3, 2, 1).reshape(1, NCORES, 15, D))
    ps_ = np.stack([R[c]["pool_s"] for c in range(NCORES)])
    pool_sample = np.ascontiguousarray(ps_.transpose(0, 3, 4, 2, 1).reshape(1, 32, 15, D))
    k_prompt = np.stack([R[c]["k_p"] for c in range(NCORES)]).reshape(1, NCORES, TP, 16, 64)
    v_prompt = np.stack([R[c]["v_p"] for c in range(NCORES)]).reshape(1, NCORES, TP, 16, 64)
    k_sample = np.stack([R[c]["k_s"] for c in range(NCORES)]).reshape(1, 32, 4, 16, 64)
    v_sample = np.stack([R[c]["v_s"] for c in range(NCORES)]).reshape(1, 32, 4, 16, 64)
    return (y_prompt, y_sample, k_prompt, v_prompt, k_sample, v_sample,
            np.ascontiguousarray(sp[0]), np.ascontiguousarray(sp[1]), np.ascontiguousarray(ss[0]), np.ascontiguousarray(ss[1]),
            pool_prompt, pool_sample, conv_prompt, conv_sample)
```

```python
import contextlib
import math
import os
import numpy as np
import concourse.bass as bass
import concourse.mybir as mybir
from concourse.bass_utils import run_bass_kernel_spmd

F32 = mybir.dt.float32
BF16 = mybir.dt.bfloat16
I32 = mybir.dt.int32
AF = mybir.ActivationFunctionType
ALU = mybir.AluOpType

NCORES = 8
D = 1024
NCH = 8
TP = 2064
TS = 16
T = TP + TS
OFF = 2
TW = T + OFF
DFF = 2816
NFF = 22
EPS = 1e-6
NT = [(0, 416), (416, 416), (832, 416), (1248, 416), (1664, 416)]
NKC = 129
TWO_PI = 2.0 * math.pi
DEPTH = 4


class Op:
    __slots__ = ("eng", "fn", "deps", "needed", "val", "sem", "is_dma")

    def __init__(self, eng, fn, is_dma=False):
        self.eng = eng
        self.fn = fn
        self.deps = []
        self.needed = False
        self.val = None
        self.sem = None
        self.is_dma = is_dma


class Sched:
    ENGS = ("sync", "act", "dve", "pool", "pe")
    UID = 0

    def __init__(self, nc):
        self.nc = nc
        self.ops = {e: [] for e in self.ENGS}
        self.last_w = {}
        self.readers = {}
        self.ndma = {"sync": 0, "pool": 0}
        self.npool = {"sync": 8, "pool": 6}
        self.dma_hist = {"sync": [], "pool": []}

    def _add(self, op, reads, writes):
        reads = list(reads)
        writes = list(writes)
        for k in list(reads):
            if isinstance(k, tuple) and k[0] == "bank":
                writes.append(("bankrd", k[1]))
        deps = []
        for k in reads:
            w = self.last_w.get(k)
            if w is not None:
                deps.append(w)
        for k in writes:
            w = self.last_w.get(k)
            if w is not None:
                deps.append(w)
            lastr = {}
            for r in self.readers.get(k, ()):
                if r.is_dma:
                    deps.append(r)
                else:
                    lastr[r.eng] = r
            deps.extend(lastr.values())
        if op.eng == "pe" and not op.is_dma:
            deps = [d for d in deps if not (d.eng == "pe" and not d.is_dma)]
        seen = set()
        for d in deps:
            if id(d) not in seen and d is not op:
                seen.add(id(d))
                op.deps.append(d)
                d.needed = True
        for k in writes:
            self.last_w[k] = op
            self.readers[k] = []
        for k in reads:
            self.readers.setdefault(k, []).append(op)
        self.ops[op.eng].append(op)
        return op

    def op(self, eng, fn, reads=(), writes=()):
        return self._add(Op(eng, fn), reads, writes)

    def dma(self, fn, reads=(), writes=(), q="sync"):
        op = Op(q, fn, is_dma=True)
        i = self.ndma[q]
        self.ndma[q] += 1
        n = self.npool[q]
        op.sem = (q, i % n)
        op.val = 16 * (i // n + 1)
        hist = self.dma_hist[q]
        if i >= n:
            op.deps.append(hist[i - n])
        hist.append(op)
        op.needed = True
        return self._add(op, reads, writes)

    def emit(self):
        nc = self.nc
        with contextlib.ExitStack() as st:
            Sched.UID += 1
            u = Sched.UID
            allsems = []

            def newsem(name):
                h_ = nc.alloc_semaphore(name=name)
                allsems.append(h_)
                return h_
            esem = {e: newsem("se%d_%s" % (u, e)) for e in self.ENGS}
            dsem = {}
            for q, n in self.npool.items():
                for j in range(min(n, self.ndma[q])):
                    dsem[(q, j)] = newsem("sd%d_%s%d" % (u, q, j))
            for e in self.ENGS:
                c = 0
                for op in self.ops[e]:
                    if op.is_dma:
                        continue
                    if op.needed:
                        c += 1
                        op.val = c
                        op.sem = ("e", e)

            def semof(op):
                return esem[op.sem[1]] if op.sem[0] == "e" else dsem[op.sem]

            block = st.enter_context(nc.Block())

            def replay(e, h):
                waited = {}
                for op in self.ops[e]:
                    need = {}
                    for d in op.deps:
                        if waited.get(d.sem, 0) >= d.val:
                            continue
                        if need.get(d.sem, (0, None))[0] < d.val:
                            need[d.sem] = (d.val, d)
                    for key, (v, d) in need.items():
                        waited[key] = v
                        h.wait_ge(semof(d), v)
                    ins = op.fn(h)
                    if op.is_dma:
                        ins.then_inc(semof(op), 16)
                    elif op.needed:
                        ins.then_inc(semof(op), 1)
                if e in self.dma_hist:
                    last = {}
                    for op in self.dma_hist[e]:
                        last[op.sem] = op
                    for key, op in last.items():
                        if waited.get(key, 0) < op.val:
                            h.wait_ge(semof(op), op.val)

            @block.sync
            def _(h):
                replay("sync", h)

            @block.scalar
            def _(h):
                replay("act", h)

            @block.vector
            def _(h):
                replay("dve", h)

            @block.gpsimd
            def _(h):
                replay("pool", h)

            @block.tensor
            def _(h):
                replay("pe", h)
            st.close()
            nc.clear_and_free_semaphores(allsems)
            nc.all_engine_barrier()


def OP(S, eng, method, reads, writes, *args, **kw):
    return S.op(eng, lambda h: getattr(h, method)(*args, **kw), reads, writes)


def DMA(S, out, in_, reads, writes, q="sync", **kw):
    return S.dma(lambda h: h.dma_start(out=out, in_=in_, **kw), reads, writes, q=q)


def build_nc():
    nc = bass.Bass("TRN2", target_bir_lowering=False)
    STOP = int(os.environ.get("KSTOP", "99"))
    SSTOP = int(os.environ.get("KSSTOP", "99"))
    ASTOP = int(os.environ.get("KASTOP", "99"))

    def din(name, shape, dt=F32):
        return nc.dram_tensor(name, list(shape), dt, kind="ExternalInput").ap()

    def dout(name, shape, dt=F32):
        return nc.dram_tensor(name, list(shape), dt, kind="ExternalOutput").ap()

    xp = din("xp", [2048, D])
    xs = din("xs", [TS, D])
    meta = din("meta", [16, D])
    ident = din("ident", [128, 128])
    gains_d = din("gains", [128, 9, NCH])
    cw_d = din("cw", [128, DEPTH, 3, 44])
    cb_d = din("cb", [128, DEPTH, 44])
    cbuf_d = din("cbuf", [128, DEPTH, 44, 8])
    ssmd_d = din("ssmd", [128, 2, NCH])
    par_d = din("par", [128, 3, 2, 32])
    sst_d = din("sst", [128, 2, 2, 32, 4])
    bm_d = din("bm", [128, 2, 2, 32, 32])
    cm_d = din("cm", [128, 2, 2, 32, 32])
    rowmask_d = din("rowmask", [128, 4])
    spool_d = din("spool", [128, NCH, 4, 15])
    pscale_d = din("pscale", [128, NCH])
    rcnt_d = din("rcnt", [128, 4, 16])
    pool_w = din("pool_w", [4, 256, 256])
    w_qkv = din("w_qkv", [D, 3 * D])
    w_o = din("w_o", [D, D])
    abias_d = din("abias", [128, 16])
    tri_d = din("tri", [128, 128])
    tm_d = din("tm", [128, 1088])
    mcur_d = din("mcur", [16, 4, 64])
    ptab_d = din("ptab", [128, 256], I32)
    iota_d = din("iota", [128, 1])
    cache_k = din("cache_k", [2560 * 128, D])
    cache_v = din("cache_v", [2560 * 128, D])
    w_glu = din("w_glu", [2, D, 2 * D])
    w_up = din("w_up", [DEPTH, D, 2 * DFF])
    w_down = din("w_down", [DEPTH, DFF, D])

    y_p = dout("y_p", [2048, D])
    y_s = dout("y_s", [TS, D])
    ssm_p = dout("ssm_p", [128, 2, 2, 32])
    ssm_s = dout("ssm_s", [128, 2, 2, 32, 4])
    conv_o = dout("conv_o", [128, DEPTH, 44, 10])
    k_p = dout("k_p", [TP, D])
    v_p = dout("v_p", [TP, D])
    k_s = dout("k_s", [TS, D])
    v_s = dout("v_s", [TS, D])
    pool_p = dout("pool_p", [128, NCH, 15])
    pool_s = dout("pool_s", [128, NCH, 4, 15])

    pst = contextlib.ExitStack()

    def sbp(name, shape, dt=F32):
        return pst.enter_context(nc.sbuf_tensor(name, list(shape), dt))

    x = sbp("x", [128, NCH, T])
    u16 = sbp("u16", [128, NCH, TW], BF16)
    idt = sbp("idt", [128, 128])
    ones_b = sbp("ones_b", [128, 128], BF16)
    gains = sbp("gains_sb", [128, 9, NCH])
    cw = sbp("cw_sb", [128, DEPTH, 3, 44])
    cb = sbp("cb_sb", [128, DEPTH, 44])
    cbuf = sbp("cbuf_sb", [128, DEPTH, 44, 8])
    ssmd = sbp("ssmd_sb", [128, 2, NCH])
    par = sbp("par_sb", [128, 3, 2, 32])
    sst = sbp("sst_sb", [128, 2, 2, 32, 4])
    rowmask = sbp("rowmask_sb", [128, 4])
    spool = sbp("spool_sb", [128, NCH, 4, 15])
    pscale = sbp("pscale_sb", [128, NCH])
    rcnt = sbp("rcnt_sb", [128, 4, 16])
    banks = [pst.enter_context(nc.psum_tensor("bank%d" % i, [128, 512], F32)) for i in range(8)]
    uid = [0]

    class Phase:
        def __enter__(self):
            self.S = Sched(nc)
            self.st = contextlib.ExitStack()

            def sb(shape, dt=F32):
                uid[0] += 1
                return self.st.enter_context(nc.sbuf_tensor("t%d" % uid[0], list(shape), dt))
            self.sb = sb
            return self

        def __exit__(self, *a):
            if a[0] is None:
                self.S.emit()
            self.st.close()
            return False

    def bk(i):
        return ("bank", i)

    with Phase() as ph:
        S = ph.S
        for dst, src, key in ((idt, ident, "idt"), (gains, gains_d, "gains"), (cw, cw_d, "cw"), (cb, cb_d, "cb"),
                              (cbuf, cbuf_d, "cbuf"), (ssmd, ssmd_d, "ssmd"), (par, par_d, "par"),
                              (sst, sst_d, "sst"), (rowmask, rowmask_d, "rowmask"),
                              (spool, spool_d, "spool"), (pscale, pscale_d, "pscale"), (rcnt, rcnt_d, "rcnt")):
            DMA(S, dst[:], src, [], [key])
        OP(S, "pool", "memset", [], ["ones_b"], ones_b[:], 1.0)
        OP(S, "pool", "memset", [], ["u16"], u16[:], 0.0)
        stg = [ph.sb([128, D]) for _ in range(2)]
        blocks = [(meta, 16, 0)]
        for b in range(16):
            blocks.append((xp[b * 128:(b + 1) * 128, :], 128, 16 + b * 128))
        blocks.append((xs, 16, TP))
        for bi, (src, nr, c0) in enumerate(blocks):
            sg = stg[bi % 2]
            DMA(S, sg[0:nr, :], src, [], [("stg", bi % 2)])
            for g in range(2):
                b_ = (bi * 2 + g) % 4
                for jj in range(4):
                    c = g * 4 + jj
                    OP(S, "pe", "transpose", [("stg", bi % 2), "idt"], [bk(b_)],
                       out=banks[b_][:, jj * 128:jj * 128 + nr], in_=sg[0:nr, c * 128:(c + 1) * 128],
                       identity=idt[0:nr, 0:nr])
                src_v = banks[b_][:].rearrange("p (j n) -> p j n", j=4)[:, :, 0:nr]
                dst_v = x[:, g * 4:(g + 1) * 4, c0:c0 + nr]
                if g == 0:
                    OP(S, "act", "activation", [bk(b_)], [("x", bi, g)], out=dst_v, in_=src_v, func=AF.Copy)
                else:
                    OP(S, "dve", "tensor_copy", [bk(b_)], [("x", bi, g)], out=dst_v, in_=src_v)

    def norm_stats(ph):
        S = ph.S
        sq = ph.sb([128, NCH, T], BF16)
        rstd = ph.sb([128, T])
        for c in range(NCH):
            OP(S, "act", "activation", ["x"], [("sq", c)], out=sq[:, c, :], in_=x[:, c, :], func=AF.Square)
        for ti, (n0, nn) in enumerate(NT):
            b_ = 6 + ti % 2
            for c in range(NCH):
                OP(S, "pe", "matmul", [("sq", c), "ones_b"], [bk(b_)], banks[b_][:, 0:nn], lhsT=ones_b[:],
                   rhs=sq[:, c, n0:n0 + nn], start=(c == 0), stop=(c == NCH - 1))
            OP(S, "act", "activation", [bk(b_)], [("rstd", ti)], out=rstd[:, n0:n0 + nn], in_=banks[b_][:, 0:nn],
               func=AF.Sqrt, bias=EPS, scale=1.0 / D)
            OP(S, "dve", "reciprocal", [("rstd", ti)], [("rstd", ti)], out=rstd[:, n0:n0 + nn], in_=rstd[:, n0:n0 + nn])
        return rstd

    def norm_to_u16(gi):
        with Phase() as ph:
            S = ph.S
            rstd = norm_stats(ph)
            rk = [("rstd", ti) for ti in range(5)]
            for c in range(NCH):
                OP(S, "dve", "scalar_tensor_tensor", ["x"] + rk, [("u16", c)], out=u16[:, c, OFF:OFF + T],
                   in0=x[:, c, :], scalar=gains[:, gi, c:c + 1], in1=rstd[:, :], op0=ALU.mult, op1=ALU.mult)

    class WStream:
        def __init__(self, ph, ncols, nst=2, nbf=2, name="w"):
            self.ph = ph
            self.stg = [ph.sb([128, 8, ncols]) for _ in range(nst)]
            self.bf = [ph.sb([128, 8, ncols], BF16) for _ in range(nbf)]
            self.i = 0
            self.name = name

        def load(self, src2d, cast_eng="pool"):
            S = self.ph.S
            i = self.i
            self.i += 1
            sg = self.stg[i % len(self.stg)]
            bf = self.bf[i % len(self.bf)]
            ks = (self.name + "s", i % len(self.stg))
            kb = (self.name + "b", i % len(self.bf))
            DMA(S, sg[:], src2d.rearrange("(k p) n -> p k n", p=128), [], [ks])
            OP(S, cast_eng, "tensor_copy", [ks], [kb], out=bf[:], in_=sg[:])
            return bf, kb

    def ssm_layer(j, li):
        sst_ = contextlib.ExitStack()

        def sbs(name, shape, dt=F32):
            return sst_.enter_context(nc.sbuf_tensor("%s_%d" % (name, j), list(shape), dt))
        S_r = sbs("S_r", [128, 32, NKC + 1])
        S_i = sbs("S_i", [128, 32, NKC + 1])
        pw_r = sbs("pw_r", [128, 17, 32])
        pw_i = sbs("pw_i", [128, 17, 32])
        wri = sbs("wri", [128, 2, 32])
        aa = sbs("aa", [128, 2, 32])
        bus = sbs("bus", [128, 2, 32, 16])
        hs = sbs("hs", [128, 2, 32, 4, 4])

        lamr = par[:, 0, j, :]
        lami = par[:, 1, j, :]
        ldt = par[:, 2, j, :]

        with Phase() as ph:
            S = ph.S
            pp = ph.sb([128, 16, 32])
            ki = ph.sb([128, 32], I32)

            def P(i):
                return pp[:, i, :]

            def K(i):
                return ("pp", i)
            dt_, mag, phi, kf, tmp, msk, sinv, cosv, den, am1, t1, t2, phc = range(13)
            OP(S, "act", "activation", [], [K(dt_)], out=P(dt_), in_=ldt, func=AF.Exp)
            OP(S, "dve", "tensor_tensor", [K(dt_)], [K(tmp)], out=P(tmp), in0=lamr, in1=P(dt_), op=ALU.mult)
            OP(S, "act", "activation", [K(tmp)], [K(mag)], out=P(mag), in_=P(tmp), func=AF.Exp)
            OP(S, "dve", "tensor_tensor", [K(dt_)], [K(phi)], out=P(phi), in0=lami, in1=P(dt_), op=ALU.mult)
            OP(S, "dve", "tensor_scalar", [K(phi)], ["ki"], out=ki[:], in0=P(phi), scalar1=1.0 / TWO_PI, scalar2=None,
               op0=ALU.mult)
            OP(S, "dve", "tensor_copy", ["ki"], [K(kf)], out=P(kf), in_=ki[:])
            OP(S, "dve", "scalar_tensor_tensor", [K(kf), K(phi)], [K(phi)], out=P(phi), in0=P(kf), scalar=-TWO_PI,
               in1=P(phi), op0=ALU.mult, op1=ALU.add)

            def fold(pi_):
                OP(S, "dve", "tensor_scalar", [K(pi_)], [K(msk)], out=P(msk), in0=P(pi_), scalar1=math.pi, scalar2=None,
                   op0=ALU.is_gt)
                OP(S, "dve", "scalar_tensor_tensor", [K(msk), K(pi_)], [K(pi_)], out=P(pi_), in0=P(msk), scalar=-TWO_PI,
                   in1=P(pi_), op0=ALU.mult, op1=ALU.add)
                OP(S, "dve", "tensor_scalar", [K(pi_)], [K(msk)], out=P(msk), in0=P(pi_), scalar1=-math.pi, scalar2=None,
                   op0=ALU.is_lt)
                OP(S, "dve", "scalar_tensor_tensor", [K(msk), K(pi_)], [K(pi_)], out=P(pi_), in0=P(msk), scalar=TWO_PI,
                   in1=P(pi_), op0=ALU.mult, op1=ALU.add)
            fold(phi)
            OP(S, "act", "activation", [K(phi)], [K(sinv)], out=P(sinv), in_=P(phi), func=AF.Sin)
            OP(S, "dve", "tensor_scalar", [K(phi)], [K(phc)], out=P(phc), in0=P(phi), scalar1=math.pi / 2, scalar2=None,
               op0=ALU.add)
            fold(phc)
            OP(S, "act", "activation", [K(phc)], [K(cosv)], out=P(cosv), in_=P(phc), func=AF.Sin)
            ar = aa[:, 0, :]
            ai = aa[:, 1, :]
            OP(S, "dve", "tensor_tensor", [K(mag), K(cosv)], ["ar"], out=ar, in0=P(mag), in1=P(cosv), op=ALU.mult)
            OP(S, "dve", "tensor_tensor", [K(mag), K(sinv)], ["ai"], out=ai, in0=P(mag), in1=P(sinv), op=ALU.mult)
            OP(S, "dve", "tensor_scalar", ["ar"], [K(am1)], out=P(am1), in0=ar, scalar1=-1.0, scalar2=None, op0=ALU.add)
            OP(S, "dve", "tensor_tensor", [], [K(t1)], out=P(t1), in0=lamr, in1=lamr, op=ALU.mult)
            OP(S, "dve", "tensor_tensor", [], [K(t2)], out=P(t2), in0=lami, in1=lami, op=ALU.mult)
            OP(S, "dve", "tensor_tensor", [K(t1), K(t2)], [K(den)], out=P(den), in0=P(t1), in1=P(t2), op=ALU.add)
            OP(S, "dve", "reciprocal", [K(den)], [K(den)], out=P(den), in_=P(den))
            OP(S, "dve", "tensor_tensor", [K(am1)], [K(t1)], out=P(t1), in0=P(am1), in1=lamr, op=ALU.mult)
            OP(S, "dve", "tensor_tensor", ["ai"], [K(t2)], out=P(t2), in0=ai, in1=lami, op=ALU.mult)
            OP(S, "dve", "tensor_tensor", [K(t1), K(t2)], [K(t1)], out=P(t1), in0=P(t1), in1=P(t2), op=ALU.add)
            OP(S, "dve", "tensor_tensor", [K(t1), K(den)], ["wr"], out=wri[:, 0, :], in0=P(t1), in1=P(den), op=ALU.mult)
            OP(S, "dve", "tensor_tensor", ["ai"], [K(t1)], out=P(t1), in0=ai, in1=lamr, op=ALU.mult)
            OP(S, "dve", "tensor_tensor", [K(am1)], [K(t2)], out=P(t2), in0=P(am1), in1=lami, op=ALU.mult)
            OP(S, "dve", "tensor_tensor", [K(t1), K(t2)], [K(t1)], out=P(t1), in0=P(t1), in1=P(t2), op=ALU.subtract)
            OP(S, "dve", "tensor_tensor", [K(t1), K(den)], ["wi"], out=wri[:, 1, :], in0=P(t1), in1=P(den), op=ALU.mult)
            OP(S, "dve", "memset", [], [("pwr", 0)], pw_r[:, 0, :], 1.0)
            OP(S, "dve", "memset", [], [("pwi", 0)], pw_i[:, 0, :], 0.0)
            for n in range(1, 17):
                OP(S, "dve", "tensor_tensor", ["ar", ("pwr", n - 1)], [K(t1)], out=P(t1), in0=ar, in1=pw_r[:, n - 1, :], op=ALU.mult)
                OP(S, "dve", "tensor_tensor", ["ai", ("pwi", n - 1)], [K(t2)], out=P(t2), in0=ai, in1=pw_i[:, n - 1, :], op=ALU.mult)
                OP(S, "dve", "tensor_tensor", [K(t1), K(t2)], [("pwr", n)], out=pw_r[:, n, :], in0=P(t1), in1=P(t2), op=ALU.subtract)
                OP(S, "dve", "tensor_tensor", ["ar", ("pwi", n - 1)], [K(t1)], out=P(t1), in0=ar, in1=pw_i[:, n - 1, :], op=ALU.mult)
                OP(S, "dve", "tensor_tensor", ["ai", ("pwr", n - 1)], [K(t2)], out=P(t2), in0=ai, in1=pw_r[:, n - 1, :], op=ALU.mult)
                OP(S, "dve", "tensor_tensor", [K(t1), K(t2)], [("pwi", n)], out=pw_i[:, n, :], in0=P(t1), in1=P(t2), op=ALU.add)

        def bbar(ph, m, eng="dve"):
            S = ph.S
            if not hasattr(ph, "bb_tiles"):
                ph.bb_tiles = ([ph.sb([128, 2, 4, 32]) for _ in range(2)], ph.sb([128, 2, 4, 32]), ph.sb([128, 2, 4, 32]))
            bms, tb, bb = ph.bb_tiles
            bmt = bms[m % 2]
            DMA(S, bmt[:], bm_d[:, :, j, 4 * m:4 * m + 4, :], [], [("bm", m % 2)])
            wr_b = wri[:, 0, 4 * m:4 * m + 4].unsqueeze(2).broadcast_to([128, 4, 32])
            wi_b = wri[:, 1, 4 * m:4 * m + 4].unsqueeze(2).broadcast_to([128, 4, 32])
            OP(S, eng, "tensor_tensor", [("bm", m % 2)], ["tb0"], out=tb[:, 0], in0=bmt[:, 0], in1=wr_b, op=ALU.mult)
            OP(S, eng, "tensor_tensor", [("bm", m % 2)], ["tb1"], out=tb[:, 1], in0=bmt[:, 1], in1=wi_b, op=ALU.mult)
            OP(S, eng, "tensor_tensor", ["tb0", "tb1"], ["bbr"], out=bb[:, 0], in0=tb[:, 0], in1=tb[:, 1], op=ALU.subtract)
            OP(S, eng, "tensor_tensor", [("bm", m % 2)], ["tb0"], out=tb[:, 0], in0=bmt[:, 1], in1=wr_b, op=ALU.mult)
            OP(S, eng, "tensor_tensor", [("bm", m % 2)], ["tb1"], out=tb[:, 1], in0=bmt[:, 0], in1=wi_b, op=ALU.mult)
            OP(S, eng, "tensor_tensor", ["tb0", "tb1"], ["bbi"], out=bb[:, 1], in0=tb[:, 0], in1=tb[:, 1], op=ALU.add)
            return bb

        with Phase() as ph:
            S = ph.S
            ta = ph.sb([128, 8, 128])
            td_ = ph.sb([128, 8, 128])
            Dr = ph.sb([128, 8, 128])
            Di = ph.sb([128, 8, 128])
            BT = [ph.sb([128, 16, 128], BF16) for _ in range(2)]
            um = [ph.sb([128, TW], BF16) for _ in range(4)]
            OP(S, "pool", "memset", [], [("S", 0, -1)], S_r[:, :, 0:1], 0.0)
            OP(S, "pool", "memset", [], [("S", 1, -1)], S_i[:, :, 0:1], 0.0)
            tcount = 0
            for m in range(NCH):
                bb = bbar(ph, m)
                for jj in range(4):
                    OP(S, "act", "activation", ["rowmask"], [("um", jj)], out=um[jj][:], in_=u16[:, m, :], func=AF.Identity,
                       scale=rowmask[:, jj:jj + 1], bias=0.0)
                for half in range(2):
                    n0 = 8 * half
                    pr = pw_r[:, n0:n0 + 8, 4 * m:4 * m + 4].unsqueeze(3).broadcast_to([128, 8, 4, 32])
                    pi_ = pw_i[:, n0:n0 + 8, 4 * m:4 * m + 4].unsqueeze(3).broadcast_to([128, 8, 4, 32])
                    br = bb[:, 0].unsqueeze(1).broadcast_to([128, 8, 4, 32])
                    bi = bb[:, 1].unsqueeze(1).broadcast_to([128, 8, 4, 32])

                    def v4(t):
                        return t[:].rearrange("p n (i c) -> p n i c", i=4)
                    OP(S, "dve", "tensor_tensor", ["bbr"], ["Dr"], out=v4(Dr), in0=pr, in1=br, op=ALU.mult)
                    OP(S, "dve", "tensor_tensor", ["bbi"], ["ta"], out=v4(ta), in0=pi_, in1=bi, op=ALU.mult)
                    OP(S, "dve", "tensor_tensor", ["Dr", "ta"], ["Dr"], out=Dr[:], in0=Dr[:], in1=ta[:], op=ALU.subtract)
                    OP(S, "pool", "tensor_tensor", ["bbi"], ["Di"], out=v4(Di), in0=pr, in1=bi, op=ALU.mult)
                    OP(S, "pool", "tensor_tensor", ["bbr"], ["td"], out=v4(td_), in0=pi_, in1=br, op=ALU.mult)
                    OP(S, "pool", "tensor_tensor", ["Di", "td"], ["Di"], out=Di[:], in0=Di[:], in1=td_[:], op=ALU.add)
                    for ri, (Dsrc, dk) in enumerate(((Dr, "Dr"), (Di, "Di"))):
                        for g in range(2):
                            b_ = tcount % 2
                            tcount += 1
                            for q in range(4):
                                OP(S, "pe", "transpose", [dk, "idt"], [bk(b_)], out=banks[b_][:, q * 128:(q + 1) * 128],
                                   in_=Dsrc[:, g * 4 + q, :], identity=idt[:])
                            OP(S, "act", "activation", [bk(b_)], [("BT", ri)],
                               out=BT[ri][:, n0 + g * 4:n0 + g * 4 + 4, :].rearrange("p n q -> p (n q)"),
                               in_=banks[b_][:], func=AF.Copy)
                pb = [2, 3, 4] if m % 2 == 0 else [5, 6, 7]
                slot = 0
                for jj in range(4):
                    for ri in range(2):
                        b_ = pb[slot // 3]
                        c0 = (slot % 3) * NKC
                        slot += 1
                        for tq in range(16):
                            OP(S, "pe", "matmul", [("BT", ri), ("um", jj)], [bk(b_)], banks[b_][:, c0:c0 + NKC],
                               lhsT=BT[ri][:, 15 - tq, :], rhs=um[jj][:, OFF + tq:OFF + tq + TP:16],
                               start=(tq == 0), stop=(tq == 15))
                b_ = pb[2]
                for jj in range(4):
                    for ri in range(2):
                        c0 = 258 + (jj * 2 + ri) * 16
                        OP(S, "pe", "matmul", [("BT", ri), ("um", jj)], [bk(b_)], banks[b_][:, c0:c0 + 16],
                           lhsT=BT[ri][:, 0, :], rhs=um[jj][:, OFF + TP:OFF + T], start=True, stop=True)
                slot = 0
                for jj in range(4):
                    for ri, Sdst in enumerate((S_r, S_i)):
                        b_ = pb[slot // 3]
                        c0 = (slot % 3) * NKC
                        slot += 1
                        OP(S, "act", "activation", [bk(b_)], [("S", ri, m)], out=Sdst[:, 4 * m + jj, 1:NKC + 1],
                           in_=banks[b_][:, c0:c0 + NKC], func=AF.Copy)
                OP(S, "act", "activation", [bk(pb[2])], [("bus", m)],
                   out=bus[:, :, 4 * m:4 * m + 4, :].rearrange("p r i c -> p i r c"),
                   in_=banks[pb[2]][:, 258:386].rearrange("p (i r c) -> p i r c", i=4, r=2), func=AF.Copy)

        if SSTOP <= 1:
            sst_.close()
            return
        with Phase() as ph:
            S = ph.S
            tt_ = ph.sb([128, 4, 32])
            ts_ = ph.sb([128, 4, 32, 4])
            A_r = pw_r[:, 16, :]
            A_i = pw_i[:, 16, :]
            ar_b = aa[:, 0, :].unsqueeze(2).broadcast_to([128, 32, 4])
            ai_b = aa[:, 1, :].unsqueeze(2).broadcast_to([128, 32, 4])
            for stp in range(4):
                if stp == 0:
                    hr_p = sst[:, 0, j, :, :]
                    hi_p = sst[:, 1, j, :, :]
                else:
                    hr_p = hs[:, 0, :, :, stp - 1]
                    hi_p = hs[:, 1, :, :, stp - 1]
                bur = bus[:, 0, :, :].rearrange("p i (s t) -> p i s t", t=4)[:, :, :, stp]
                bui = bus[:, 1, :, :].rearrange("p i (s t) -> p i s t", t=4)[:, :, :, stp]
                hr_n = hs[:, 0, :, :, stp]
                hi_n = hs[:, 1, :, :, stp]
                e = "pool"
                OP(S, e, "tensor_tensor", [("hs", stp - 1)], ["ts0"], out=ts_[:, 0], in0=hr_p, in1=ar_b, op=ALU.mult)
                OP(S, e, "tensor_tensor", [("hs", stp - 1)], ["ts1"], out=ts_[:, 1], in0=hi_p, in1=ai_b, op=ALU.mult)
                OP(S, e, "tensor_tensor", ["ts0", "ts1"], ["ts0"], out=ts_[:, 0], in0=ts_[:, 0], in1=ts_[:, 1], op=ALU.subtract)
                OP(S, e, "tensor_tensor", ["ts0"], [("hsr", stp)], out=hr_n, in0=ts_[:, 0], in1=bur, op=ALU.add)
                OP(S, e, "tensor_tensor", [("hs", stp - 1)], ["ts2"], out=ts_[:, 2], in0=hi_p, in1=ar_b, op=ALU.mult)
                OP(S, e, "tensor_tensor", [("hs", stp - 1)], ["ts3"], out=ts_[:, 3], in0=hr_p, in1=ai_b, op=ALU.mult)
                OP(S, e, "tensor_tensor", ["ts2", "ts3"], ["ts2"], out=ts_[:, 2], in0=ts_[:, 2], in1=ts_[:, 3], op=ALU.add)
                OP(S, e, "tensor_tensor", ["ts2", ("hsr", stp)], [("hs", stp)], out=hi_n, in0=ts_[:, 2], in1=bui, op=ALU.add)
            fin_s = ph.sb([128, 2, 32, 4])
            OP(S, "pool", "tensor_copy", [("hs", 3)], ["fin_s"], out=fin_s[:], in_=hs[:, :, :, :, 3])
            DMA(S, ssm_s[:, :, j, :, :], fin_s[:], ["fin_s"], [])
            for k in range(NKC):
                sr = S_r[:, :, k]
                si = S_i[:, :, k]
                nr = S_r[:, :, k + 1]
                ni = S_i[:, :, k + 1]
                kk = ("Sk", k)
                kn = ("Sk", k + 1)
                OP(S, "dve", "tensor_tensor", [kk], ["t0"], out=tt_[:, 0], in0=sr, in1=A_r, op=ALU.mult)
                OP(S, "dve", "tensor_tensor", [kk], ["t2"], out=tt_[:, 2], in0=si, in1=A_r, op=ALU.mult)
                OP(S, "dve", "tensor_tensor", [kk], ["t1"], out=tt_[:, 1], in0=si, in1=A_i, op=ALU.mult)
                OP(S, "dve", "tensor_tensor", [kk], ["t3"], out=tt_[:, 3], in0=sr, in1=A_i, op=ALU.mult)
                OP(S, "dve", "tensor_tensor", ["t0"], [("Skr", k + 1)], out=nr, in0=nr, in1=tt_[:, 0], op=ALU.add)
                OP(S, "dve", "tensor_tensor", ["t2"], [("Ski", k + 1)], out=ni, in0=ni, in1=tt_[:, 2], op=ALU.add)
                OP(S, "dve", "tensor_tensor", ["t1", ("Skr", k + 1)], [("Skr", k + 1)], out=nr, in0=nr, in1=tt_[:, 1], op=ALU.subtract)
                OP(S, "dve", "tensor_tensor", ["t3", ("Ski", k + 1), ("Skr", k + 1)], [kn], out=ni, in0=ni, in1=tt_[:, 3], op=ALU.add)
            fin_p = ph.sb([128, 2, 32])
            OP(S, "dve", "tensor_copy", [("Sk", NKC)], ["fin_p"], out=fin_p[:, 0, :], in_=S_r[:, :, NKC])
            OP(S, "dve", "tensor_copy", [("Sk", NKC)], ["fin_p"], out=fin_p[:, 1, :], in_=S_i[:, :, NKC])
            DMA(S, ssm_p[:, :, j, :], fin_p[:], ["fin_p"], [])

        if SSTOP <= 2:
            sst_.close()
            return
        with Phase() as ph:
            S = ph.S
            cms = [ph.sb([128, 2, 4, 32]) for _ in range(2)]
            t1 = ph.sb([128, 17, 4, 32])
            t2 = ph.sb([128, 17, 4, 32])
            Er = ph.sb([128, 17, 4, 32], BF16)
            nEi = ph.sb([128, 17, 4, 32], BF16)
            bb16 = ph.sb([128, 2, 4, 32], BF16)
            Kb = ph.sb([128, 16, 128], BF16)
            Sb = ph.sb([128, 2, 4, NKC + 1], BF16)
            hb = ph.sb([128, 2, 4, 16], BF16)
            yt = [ph.sb([128, 3, NKC]) for _ in range(2)]
            y2 = [ph.sb([128, 3, NKC]) for _ in range(2)]
            y3 = [ph.sb([128, 3, NKC]) for _ in range(2)]
            OP(S, "pool", "memset", [], ["Kb"], Kb[:], 0.0)
            for m in range(NCH):
                cmt = cms[m % 2]
                DMA(S, cmt[:], cm_d[:, :, j, 4 * m:4 * m + 4, :], [], [("cm", m % 2)])
                bb = bbar(ph, m, eng="pool")
                OP(S, "pool", "tensor_copy", ["bbr", "bbi"], ["bb16"], out=bb16[:], in_=bb[:])
                OP(S, "pool", "tensor_copy", [], ["Sb"], out=Sb[:, 0], in_=S_r[:, 4 * m:4 * m + 4, :])
                OP(S, "pool", "tensor_copy", [], ["Sb"], out=Sb[:, 1], in_=S_i[:, 4 * m:4 * m + 4, :])
                OP(S, "pool", "tensor_copy", [], ["hb"], out=hb[:].rearrange("p r i (s t) -> p r i s t", t=4),
                   in_=hs[:, :, 4 * m:4 * m + 4, :, :])
                pr = pw_r[:, :, 4 * m:4 * m + 4].unsqueeze(3).broadcast_to([128, 17, 4, 32])
                pi_ = pw_i[:, :, 4 * m:4 * m + 4].unsqueeze(3).broadcast_to([128, 17, 4, 32])
                cr = cmt[:, 0].unsqueeze(1).broadcast_to([128, 17, 4, 32])
                ci = cmt[:, 1].unsqueeze(1).broadcast_to([128, 17, 4, 32])
                ck = ("cm", m % 2)
                OP(S, "dve", "tensor_tensor", [ck], ["t1"], out=t1[:], in0=pr, in1=cr, op=ALU.mult)
                OP(S, "dve", "tensor_tensor", [ck], ["t2"], out=t2[:], in0=pi_, in1=ci, op=ALU.mult)
                OP(S, "dve", "tensor_tensor", ["t1", "t2"], ["Er"], out=Er[:], in0=t1[:], in1=t2[:], op=ALU.subtract)
                OP(S, "dve", "tensor_tensor", [ck], ["t1"], out=t1[:], in0=pr, in1=ci, op=ALU.mult)
                OP(S, "dve", "tensor_tensor", [ck], ["t2"], out=t2[:], in0=pi_, in1=cr, op=ALU.mult)
                OP(S, "dve", "scalar_tensor_tensor", ["t1", "t2"], ["nEi"], out=nEi[:].rearrange("p a b c -> p (a b c)"),
                   in0=t1[:].rearrange("p a b c -> p (a b c)"), scalar=-1.0, in1=t2[:].rearrange("p a b c -> p (a b c)"),
                   op0=ALU.mult, op1=ALU.subtract)
                for jj in range(4):
                    o = banks[6][32 * jj:32 * jj + 32, :]
                    OP(S, "pe", "matmul", ["bb16", "Er"], [bk(6)], o, lhsT=bb16[:, 0, jj, :], rhs=Er[:, 0:16, jj, :],
                       start=True, stop=False, tile_position=(0, 32 * jj))
                    OP(S, "pe", "matmul", ["bb16", "nEi"], [bk(6)], o, lhsT=bb16[:, 1, jj, :], rhs=nEi[:, 0:16, jj, :],
                       start=False, stop=True, tile_position=(0, 32 * jj))
                for jj in range(4):
                    OP(S, "act", "activation", [bk(6)], ["Kb"], out=Kb[32 * jj:32 * jj + 32, :, 32 * jj:32 * jj + 32],
                       in_=banks[6][32 * jj:32 * jj + 32, :].rearrange("p (d c) -> p d c", c=32), func=AF.Copy)
                ukey = ("u16", m)
                for tq in range(16):
                    b_ = tq // 3
                    c0 = (tq % 3) * NKC
                    for tp in range(tq + 1):
                        OP(S, "pe", "matmul", ["Kb", ukey], [bk(b_)], banks[b_][:, c0:c0 + NKC], lhsT=Kb[:, tq - tp, :],
                           rhs=u16[:, m, OFF + tp:OFF + tp + TP:16], start=(tp == 0), stop=False)
                    for jj in range(4):
                        o = banks[b_][32 * jj:32 * jj + 32, c0:c0 + NKC]
                        OP(S, "pe", "matmul", ["Er", "Sb"], [bk(b_)], o, lhsT=Er[:, tq + 1, jj, :], rhs=Sb[:, 0, jj, 0:NKC],
                           start=False, stop=False, tile_position=(0, 32 * jj))
                        OP(S, "pe", "matmul", ["nEi", "Sb"], [bk(b_)], o, lhsT=nEi[:, tq + 1, jj, :], rhs=Sb[:, 1, jj, 0:NKC],
                           start=False, stop=True, tile_position=(0, 32 * jj))
                for jj in range(4):
                    o = banks[7][32 * jj:32 * jj + 32, 0:16]
                    OP(S, "pe", "matmul", ["Er", "hb"], [bk(7)], o, lhsT=Er[:, 0, jj, :], rhs=hb[:, 0, jj, :],
                       start=True, stop=False, tile_position=(0, 32 * jj))
                    OP(S, "pe", "matmul", ["nEi", "hb"], [bk(7)], o, lhsT=nEi[:, 0, jj, :], rhs=hb[:, 1, jj, :],
                       start=False, stop=True, tile_position=(0, 32 * jj))
                dsc = ssmd[:, j, m:m + 1]
                uv = u16[:, m, OFF:OFF + TP].rearrange("p (k t) -> p t k", t=16)
                groups = [(b_, 3 if b_ < 5 else 1, uv[:, 3 * b_:3 * b_ + (3 if b_ < 5 else 1), :]) for b_ in range(6)]
                groups.append((7, 0, None))
                for gi, (b_, ntq, uview) in enumerate(groups):
                    a = yt[gi % 2]
                    b2 = y2[gi % 2]
                    b3 = y3[gi % 2]
                    if uview is None:
                        av = a[:, 0, 0:16]
                        b2v = b2[:, 0, 0:16]
                        b3v = b3[:, 0, 0:16]
                        pv = banks[7][:, 0:16]
                        uview = u16[:, m, OFF + TP:OFF + T]
                    else:
                        av = a[:, 0:ntq, :]
                        b2v = b2[:, 0:ntq, :]
                        b3v = b3[:, 0:ntq, :]
                        pv = banks[b_][:, 0:ntq * NKC].rearrange("p (t k) -> p t k", k=NKC)
                    ka = ("yt", gi % 2)
                    k2 = ("y2", gi % 2)
                    k3 = ("y3", gi % 2)
                    OP(S, "dve", "scalar_tensor_tensor", [bk(b_), ukey], [ka], out=av, in0=uview, scalar=dsc, in1=pv,
                       op0=ALU.mult, op1=ALU.add)
                    OP(S, "act", "activation", [ka], [k2], out=b2v, in_=av, func=AF.Square)
                    OP(S, "dve", "tensor_scalar", [k2], [k2], out=b2v, in0=b2v, scalar1=0.044715, scalar2=1.0,
                       op0=ALU.mult, op1=ALU.add)
                    OP(S, "dve", "tensor_tensor", [k2, ka], [k3], out=b3v, in0=b2v, in1=av, op=ALU.mult)
                    OP(S, "act", "activation", [k3], [k2], out=b2v, in_=b3v, func=AF.Sigmoid, scale=1.5957691216057308)
                    OP(S, "dve", "tensor_tensor", [k2, ka], [ukey], out=uview, in0=b2v, in1=av, op=ALU.mult)

        sst_.close()
        if SSTOP <= 3:
            return
        with Phase() as ph:
            S = ph.S
            ws = WStream(ph, 128, nst=4, nbf=4, name="wg")
            sg = [ph.sb([128, 416]) for _ in range(2)]
            tg = [ph.sb([128, 416]) for _ in range(2)]
            it = 0
            gl = {}

            def gload(mo_):
                gl[mo_] = ws.load(w_glu[j, :, mo_ * 128:(mo_ + 1) * 128]) + ws.load(w_glu[j, :, D + mo_ * 128:D + (mo_ + 1) * 128])
            gload(0)
            for mo in range(NCH):
                if mo + 1 < NCH:
                    gload(mo + 1)
                w1, k1, w2, k2 = gl.pop(mo)
                for ti, (n0, nn) in enumerate(NT):
                    ba = 2 * (it % 2)
                    bb_ = ba + 1
                    for k in range(NCH):
                        OP(S, "pe", "matmul", [k1, "u16"], [bk(ba)], banks[ba][:, 0:nn], lhsT=w1[:, k, :],
                           rhs=u16[:, k, OFF + n0:OFF + n0 + nn], start=(k == 0), stop=(k == NCH - 1))
                    for k in range(NCH):
                        OP(S, "pe", "matmul", [k2, "u16"], [bk(bb_)], banks[bb_][:, 0:nn], lhsT=w2[:, k, :],
                           rhs=u16[:, k, OFF + n0:OFF + n0 + nn], start=(k == 0), stop=(k == NCH - 1))
                    OP(S, "act", "activation", [bk(bb_)], [("sg", it % 2)], out=sg[it % 2][:, 0:nn], in_=banks[bb_][:, 0:nn],
                       func=AF.Sigmoid)
                    OP(S, "dve", "tensor_tensor", [bk(ba), ("sg", it % 2)], [("tg", it % 2)], out=tg[it % 2][:, 0:nn],
                       in0=banks[ba][:, 0:nn], in1=sg[it % 2][:, 0:nn], op=ALU.mult)
                    OP(S, "dve", "tensor_tensor", [("tg", it % 2)], [("x", mo, ti)], out=x[:, mo, n0:n0 + nn],
                       in0=x[:, mo, n0:n0 + nn], in1=tg[it % 2][:, 0:nn], op=ALU.add)
                    it += 1

    def attn_layer(li):
        ast = contextlib.ExitStack()

        def sba(name, shape, dt=F32):
            return ast.enter_context(nc.sbuf_tensor(name, list(shape), dt))
        QsT = sba("QsT", [128, NCH, TS], BF16)
        KsT = sba("KsT", [128, NCH, TS], BF16)
        vs_tok = sba("vs_tok", [TS, D], BF16)
        abias = sba("abias_sb", [128, 16])
        tri = sba("tri_sb", [128, 128], BF16)
        tmk = sba("tm_sb", [128, 1088], BF16)
        QT = [(0, 416), (416, 416), (832, 416), (1248, 416), (1664, 400)]
        with Phase() as ph:
            S = ph.S
            cst = ph.sb([128, 1088])
            DMA(S, abias[:], abias_d, [], ["abias"])
            DMA(S, cst[:, 0:128], tri_d, [], ["cst"])
            OP(S, "dve", "tensor_copy", ["cst"], ["tri"], out=tri[:], in_=cst[:, 0:128])
            DMA(S, cst[:], tm_d, ["cst"], ["cst"])
            OP(S, "dve", "tensor_copy", ["cst"], ["tmk"], out=tmk[:], in_=cst[:])
            ws = WStream(ph, 128, nst=2, nbf=4, name="wq")
            wos = ph.sb([128, D])
            wob = ph.sb([128, D], BF16)
            qT = ph.sb([128, T], BF16)
            kT = ph.sb([128, T], BF16)
            OT = ph.sb([128, TP], BF16)
            vtok = ph.sb([128, 17, 128], BF16)
            kst = [ph.sb([128, 4, 128]) for _ in range(2)]
            vst = [ph.sb([128, 4, 128]) for _ in range(2)]
            eb = [ph.sb([128, 416]) for _ in range(3)]
            spb = [ph.sb([128, 416], BF16) for _ in range(3)]
            wb = [ph.sb([128, 416], BF16) for _ in range(3)]
            Ss16 = ph.sb([128, 416], BF16)
            ntri = ph.sb([128, 128], BF16)
            nones = ph.sb([128, 128], BF16)
            OP(S, "dve", "tensor_scalar", ["tri"], ["ntri"], out=ntri[:], in0=tri[:], scalar1=-1.0, scalar2=None, op0=ALU.mult)
            OP(S, "dve", "memset", [], ["nones"], nones[:], -1.0)
            zc = 0
            oc = 0
            pc = 0
            sti = 0
            for c in range(NCH):
                wq, kq = ws.load(w_qkv[:, c * 128:(c + 1) * 128])
                wk, kk = ws.load(w_qkv[:, D + c * 128:D + (c + 1) * 128])
                wv, kv = ws.load(w_qkv[:, 2 * D + c * 128:2 * D + (c + 1) * 128])
                DMA(S, wos[:], w_o[c * 128:(c + 1) * 128, :], [], ["wos"])
                OP(S, "pool", "tensor_copy", ["wos"], ["wob"], out=wob[:], in_=wos[:])
                for w_, kw_, dst, dk in ((wq, kq, qT, "qT"), (wk, kk, kT, "kT")):
                    for ti, (n0, nn) in enumerate(NT):
                        b_ = 4 + pc % 2
                        pc += 1
                        for k in range(NCH):
                            OP(S, "pe", "matmul", [kw_, "u16"], [bk(b_)], banks[b_][:, 0:nn], lhsT=w_[:, k, :],
                               rhs=u16[:, k, OFF + n0:OFF + n0 + nn], start=(k == 0), stop=(k == NCH - 1))
                        OP(S, "act", "activation", [bk(b_)], [(dk, ti)], out=dst[:, n0:n0 + nn], in_=banks[b_][:, 0:nn],
                           func=AF.Copy, scale=(0.125 if dk == "qT" else 1.0))
                OP(S, "pool", "tensor_copy", [("qT", 4)], ["QsT"], out=QsT[:, c, :], in_=qT[:, TP:T])
                OP(S, "pool", "tensor_copy", [("kT", 4)], ["KsT"], out=KsT[:, c, :], in_=kT[:, TP:T])
                for g in range(5):
                    nq4 = 4 if g < 4 else 1
                    for w_, kw_, which in ((wk, kk, "k"), (wv, kv, "v")):
                        b_ = 4 + pc % 2
                        pc += 1
                        for q in range(nq4):
                            tb = 4 * g + q
                            nt = 128 if tb < 16 else 32
                            for k in range(NCH):
                                OP(S, "pe", "matmul", [kw_, "u16"], [bk(b_)], banks[b_][0:nt, q * 128:(q + 1) * 128],
                                   lhsT=u16[:, k, OFF + tb * 128:OFF + tb * 128 + nt], rhs=w_[:, k, :],
                                   start=(k == 0), stop=(k == NCH - 1))
                        np_ = 128 if g < 4 else 32
                        stg_ = (kst if which == "k" else vst)[sti % 2]
                        sk = (which + "st", sti % 2)
                        srcv = banks[b_][0:np_, 0:nq4 * 128].rearrange("p (q n) -> p q n", n=128)
                        OP(S, "act", "activation", [bk(b_)], [sk], out=stg_[0:np_, 0:nq4, :], in_=srcv, func=AF.Copy)
                        if which == "v":
                            OP(S, "act", "activation", [bk(b_)], [("vtok", g)], out=vtok[0:np_, 4 * g:4 * g + nq4, :], in_=srcv,
                               func=AF.Copy)
                        dstd = k_p if which == "k" else v_p
                        dsts = k_s if which == "k" else v_s
                        if g < 4:
                            DMA(S, dstd[g * 512:(g + 1) * 512, c * 128:(c + 1) * 128].rearrange("(q p) n -> p q n", p=128),
                                stg_[:, 0:4, :], [sk], [])
                        else:
                            DMA(S, dstd[2048:TP, c * 128:(c + 1) * 128], stg_[0:16, 0, :], [sk], [])
                            DMA(S, dsts[:, c * 128:(c + 1) * 128], stg_[16:32, 0, :], [sk], [])
                    sti += 1
                b_ = 4 + pc % 2
                pc += 1
                for k in range(NCH):
                    OP(S, "pe", "matmul", [kv, "u16"], [bk(b_)], banks[b_][0:TS, 0:128], lhsT=u16[:, k, OFF + TP:OFF + T],
                       rhs=wv[:, k, :], start=(k == 0), stop=(k == NCH - 1))
                OP(S, "act", "activation", [bk(b_)], ["vs_tok"], out=vs_tok[:, c * 128:(c + 1) * 128], in_=banks[b_][0:TS, 0:128],
                   func=AF.Copy)
                blks = []
                for hh in range(2):
                    for qi, (q0, nq) in enumerate(QT):
                        kbl = [sb_ for sb_ in range(17) if sb_ * 128 < q0 + nq]
                        bo = 6 + oc % 2
                        oc += 1
                        for bi_, sb_ in enumerate(reversed(kbl)):
                            s0 = sb_ * 128
                            ns = min(128, TP - s0)
                            blks.append(dict(hh=hh, qi=qi, q0=q0, nq=nq, sb=sb_, s0=s0, ns=ns, bo=bo, first=(bi_ == 0),
                                             last=(bi_ == len(kbl) - 1), mask=(s0 + ns - 1 >= q0)))
                nb = len(blks)

                def stageA(t):
                    B = blks[t]
                    r0 = 64 * B["hh"]
                    h = 2 * c + B["hh"]
                    ns, nq, q0, s0 = B["ns"], B["nq"], B["q0"], B["s0"]
                    bz = (zc0 + t) % 4
                    p3 = (zc0 + t) % 3
                    OP(S, "pe", "matmul", [("kT", t_) for t_ in range(5)] + [("qT", B["qi"])], [bk(bz)], banks[bz][0:ns, 0:nq],
                       lhsT=kT[r0:r0 + 64, s0:s0 + ns], rhs=qT[r0:r0 + 64, q0:q0 + nq], start=True, stop=True)
                    OP(S, "act", "activation", [bk(bz), "abias"], [("e", p3)], out=eb[p3][0:ns, 0:nq], in_=banks[bz][0:ns, 0:nq],
                       func=AF.Exp, scale=1.0, bias=abias[0:ns, h:h + 1])
                    OP(S, "act", "activation", [("e", p3)], [("sp", p3)], out=spb[p3][0:ns, 0:nq], in_=eb[p3][0:ns, 0:nq],
                       func=AF.Ln, bias=1.0, scale=1.0)
                    if B["mask"]:
                        mo_ = 544 - (s0 - q0)
                        OP(S, "dve", "tensor_tensor", [("sp", p3), "tmk"], [("sp", p3)], out=spb[p3][0:ns, 0:nq],
                           in0=spb[p3][0:ns, 0:nq], in1=tmk[0:ns, mo_:mo_ + nq], op=ALU.mult)

                def stageB(t):
                    B = blks[t]
                    h = 2 * c + B["hh"]
                    ns, nq, q0, s0 = B["ns"], B["nq"], B["q0"], B["s0"]
                    bz = (zc0 + t) % 4
                    p3 = (zc0 + t) % 3
                    OP(S, "pe", "matmul", ["ntri", ("sp", p3)], [bk(bz)], banks[bz][0:ns, 0:nq], lhsT=ntri[0:ns, 0:ns],
                       rhs=spb[p3][0:ns, 0:nq], start=False, stop=B["first"], skip_group_check=True)
                    if not B["first"]:
                        OP(S, "pe", "matmul", ["nones", "Ss16"], [bk(bz)], banks[bz][0:ns, 0:nq], lhsT=nones[:, 0:ns],
                           rhs=Ss16[:, 0:nq], start=False, stop=True, skip_group_check=True)
                    OP(S, "act", "activation", [bk(bz), "abias"], [("w", p3)], out=wb[p3][0:ns, 0:nq], in_=banks[bz][0:ns, 0:nq],
                       func=AF.Exp, scale=1.0, bias=abias[0:ns, h:h + 1])
                    if B["mask"]:
                        mo_ = 544 - (s0 - q0)
                        OP(S, "dve", "tensor_tensor", [("w", p3), "tmk"], [("w", p3)], out=wb[p3][0:ns, 0:nq],
                           in0=wb[p3][0:ns, 0:nq], in1=tmk[0:ns, mo_:mo_ + nq], op=ALU.mult)
                    if not B["last"]:
                        if B["first"]:
                            if ns < 128:
                                OP(S, "dve", "memset", [], ["Ss16"], Ss16[:], 0.0)
                            OP(S, "dve", "tensor_copy", [("sp", p3)], ["Ss16"], out=Ss16[0:ns, 0:nq], in_=spb[p3][0:ns, 0:nq])
                        else:
                            OP(S, "dve", "tensor_tensor", [("sp", p3), "Ss16"], ["Ss16"], out=Ss16[0:ns, 0:nq],
                               in0=Ss16[0:ns, 0:nq], in1=spb[p3][0:ns, 0:nq], op=ALU.add)

                def stageC(t):
                    B = blks[t]
                    r0 = 64 * B["hh"]
                    ns, nq, q0 = B["ns"], B["nq"], B["q0"]
                    p3 = (zc0 + t) % 3
                    bo = B["bo"]
                    OP(S, "pe", "matmul", [("vtok", B["sb"] // 4), ("w", p3)], [bk(bo)], banks[bo][r0:r0 + 64, 0:nq],
                       lhsT=vtok[0:ns, B["sb"], r0:r0 + 64], rhs=wb[p3][0:ns, 0:nq], start=B["first"], stop=B["last"],
                       tile_position=(0, r0))
                    if B["last"]:
                        OP(S, "act", "activation", [bk(bo)], [("OT", B["qi"])], out=OT[r0:r0 + 64, q0:q0 + nq],
                           in_=banks[bo][r0:r0 + 64, 0:nq], func=AF.Copy)
                zc0 = zc
                for t in range(nb + 2):
                    if t < nb:
                        stageA(t)
                    if 1 <= t <= nb:
                        stageB(t - 1)
                    if t >= 2:
                        stageC(t - 2)
                zc += nb
                for mo in range(NCH):
                    for qi, (q0, nq) in enumerate(QT):
                        b_ = 4 + pc % 2
                        pc += 1
                        OP(S, "pe", "matmul", ["wob", ("OT", qi)], [bk(b_)], banks[b_][:, 0:nq], lhsT=wob[:, mo * 128:(mo + 1) * 128],
                           rhs=OT[:, q0:q0 + nq], start=True, stop=True)
                        OP(S, "dve", "tensor_tensor", [bk(b_)], [("x", mo, qi)], out=x[:, mo, q0:q0 + nq], in0=banks[b_][:, 0:nq],
                           in1=x[:, mo, q0:q0 + nq], op=ALU.add)
        if ASTOP >= 1:
            attn_sample(QsT, KsT, vs_tok, abias, tri)
        ast.close()

    def attn_sample(QsT, KsT, vs_tok, abias, tri):
        with Phase() as ph:
            S = ph.S
            ptab = ph.sb([128, 256], I32)
            ptf = ph.sb([128, 256])
            idx = ph.sb([128, 256], I32)
            iot = ph.sb([128, 1])
            mcs = ph.sb([TS, 4, 64])
            mcur = ph.sb([TS, 4, 64], BF16)
            bt = ph.sb([128, 16, 4])
            Qblk = ph.sb([128, NCH, 64], BF16)
            kpg = [ph.sb([128, D]) for _ in range(3)]
            vpg = [ph.sb([128, D]) for _ in range(3)]
            KT = [ph.sb([128, NCH, 128], BF16) for _ in range(3)]
            Vb = [ph.sb([128, D], BF16) for _ in range(3)]
            zb = [ph.sb([128, 64]) for _ in range(3)]
            eb = [ph.sb([128, 64]) for _ in range(3)]
            gb = [ph.sb([128, 64]) for _ in range(3)]
            spb = [ph.sb([128, 64], BF16) for _ in range(3)]
            wb = [ph.sb([128, 64], BF16) for _ in range(3)]
            Ss32 = ph.sb([128, 64])
            Ss16 = ph.sb([128, 64], BF16)
            OsT = ph.sb([128, NCH, TS], BF16)
            wos = [ph.sb([128, D]) for _ in range(2)]
            wob = ph.sb([128, NCH, D], BF16)
            DMA(S, ptab[:], ptab_d, [], ["ptab"])
            DMA(S, iot[:], iota_d, [], ["iot"])
            DMA(S, mcs[:], mcur_d, [], ["mcs"])
            OP(S, "dve", "tensor_copy", ["mcs"], ["mcur"], out=mcur[:], in_=mcs[:])
            OP(S, "dve", "tensor_copy", ["ptab"], ["ptf"], out=ptf[:], in_=ptab[:])
            OP(S, "dve", "tensor_scalar", ["ptf", "iot"], ["ptf"], out=ptf[:], in0=ptf[:], scalar1=128.0, scalar2=iot[:, 0:1],
               op0=ALU.mult, op1=ALU.add)
            OP(S, "dve", "tensor_copy", ["ptf"], ["idx"], out=idx[:], in_=ptf[:])
            OP(S, "dve", "tensor_copy", ["abias"], ["bt"], out=bt[:], in_=abias[:].unsqueeze(2).broadcast_to([128, 16, 4]))
            btv = bt[:].rearrange("p h q -> p (h q)")
            pages = []
            for sq_ in range(4):
                plist = [-1] + list(range(63, -1, -1))
                for pi_, pg in enumerate(plist):
                    pages.append(dict(sq=sq_, pg=pg, first=(pi_ == 0), last=(pi_ == len(plist) - 1), ns=(TS if pg < 0 else 128)))
            npg = len(pages)

            def stA(t):
                Pg = pages[t]
                sq_, pg, ns = Pg["sq"], Pg["pg"], Pg["ns"]
                pz = t % 2
                p3 = t % 3
                bz = pz
                if Pg["first"]:
                    OP(S, "pool", "memset", [], ["Qblk"], Qblk[:], 0.0)
                    for c in range(NCH):
                        for hh in range(2):
                            h = 2 * c + hh
                            OP(S, "pool", "tensor_copy", ["Qblk"], ["Qblk"], out=Qblk[64 * hh:64 * hh + 64, c, 4 * h:4 * h + 4],
                               in_=QsT[64 * hh:64 * hh + 64, c, 4 * sq_:4 * sq_ + 4])
                if pg < 0:
                    for c in range(NCH):
                        OP(S, "pe", "matmul", ["Qblk"], [bk(bz)], banks[bz][0:ns, 0:64], lhsT=KsT[:, c, :], rhs=Qblk[:, c, :],
                           start=(c == 0), stop=(c == NCH - 1))
                else:
                    col = sq_ * 64 + pg
                    kp = kpg[p3]
                    vp = vpg[p3]
                    S.dma(lambda h_, kp=kp, col=col: h_.indirect_dma_start(
                        out=kp[:, :], out_offset=None, in_=cache_k,
                        in_offset=bass.IndirectOffsetOnAxis(ap=idx[:, col:col + 1], axis=0)), ["idx"], [("kpg", p3)], q="pool")
                    S.dma(lambda h_, vp=vp, col=col: h_.indirect_dma_start(
                        out=vp[:, :], out_offset=None, in_=cache_v,
                        in_offset=bass.IndirectOffsetOnAxis(ap=idx[:, col:col + 1], axis=0)), ["idx"], [("vpg", p3)], q="pool")
                    OP(S, "dve", "tensor_copy", [("vpg", p3)], [("Vb", p3)], out=Vb[p3][:], in_=vp[:])
                    for g in range(2):
                        b_ = 4 + g
                        for q in range(4):
                            c = 4 * g + q
                            OP(S, "pe", "transpose", [("kpg", p3), "idt"], [bk(b_)], out=banks[b_][:, q * 128:(q + 1) * 128],
                               in_=kp[:, c * 128:(c + 1) * 128], identity=idt[:])
                        OP(S, "act", "activation", [bk(b_)], [("KT", p3, g)],
                           out=KT[p3][:, 4 * g:4 * g + 4, :].rearrange("p c s -> p (c s)"), in_=banks[b_][:], func=AF.Copy)
                    for c in range(NCH):
                        OP(S, "pe", "matmul", ["Qblk", ("KT", p3, c // 4)], [bk(bz)], banks[bz][0:ns, 0:64], lhsT=KT[p3][:, c, :],
                           rhs=Qblk[:, c, :], start=(c == 0), stop=(c == NCH - 1))
                OP(S, "dve", "scalar_tensor_tensor", [bk(bz), "bt"], [("z", p3)], out=zb[p3][0:ns, :], in0=banks[bz][0:ns, 0:64],
                   scalar=1.0, in1=btv[0:ns, :], op0=ALU.mult, op1=ALU.add)
                OP(S, "act", "activation", [("z", p3)], [("e", p3)], out=eb[p3][0:ns, :], in_=zb[p3][0:ns, :], func=AF.Exp)
                OP(S, "act", "activation", [("e", p3)], [("sp", p3)], out=spb[p3][0:ns, :], in_=eb[p3][0:ns, :], func=AF.Ln,
                   bias=1.0, scale=1.0)
                if pg < 0:
                    OP(S, "dve", "tensor_tensor", [("sp", p3), "mcur"], [("sp", p3)], out=spb[p3][0:ns, :], in0=spb[p3][0:ns, :],
                       in1=mcur[:, sq_, :], op=ALU.mult)

            def stB(t):
                Pg = pages[t]
                sq_, pg, ns = Pg["sq"], Pg["pg"], Pg["ns"]
                pz = t % 2
                p3 = t % 3
                bc = 2 + pz
                first = Pg["first"]
                OP(S, "pe", "matmul", ["tri", ("sp", p3)], [bk(bc)], banks[bc][0:ns, 0:64], lhsT=tri[0:ns, 0:ns],
                   rhs=spb[p3][0:ns, :], start=True, stop=first)
                if not first:
                    OP(S, "pe", "matmul", ["ones_b", "Ss16"], [bk(bc)], banks[bc][0:ns, 0:64], lhsT=ones_b[:, 0:ns],
                       rhs=Ss16[:, :], start=False, stop=True)
                OP(S, "act", "activation", [bk(bc)], [("g", p3)], out=gb[p3][0:ns, :], in_=banks[bc][0:ns, 0:64], func=AF.Exp,
                   scale=-1.0)
                OP(S, "dve", "tensor_tensor", [("e", p3), ("g", p3)], [("w", p3)], out=wb[p3][0:ns, :], in0=eb[p3][0:ns, :],
                   in1=gb[p3][0:ns, :], op=ALU.mult)
                if pg < 0:
                    OP(S, "dve", "tensor_tensor", [("w", p3), "mcur"], [("w", p3)], out=wb[p3][0:ns, :], in0=wb[p3][0:ns, :],
                       in1=mcur[:, sq_, :], op=ALU.mult)
                if not Pg["last"]:
                    if first:
                        OP(S, "dve", "memset", [], ["Ss32"], Ss32[:], 0.0)
                    OP(S, "dve", "tensor_tensor", [("sp", p3), "Ss32"], ["Ss32"], out=Ss32[0:ns, :], in0=Ss32[0:ns, :],
                       in1=spb[p3][0:ns, :], op=ALU.add)
                    OP(S, "dve", "tensor_copy", ["Ss32"], ["Ss16"], out=Ss16[:], in_=Ss32[:])

            def stC(t):
                Pg = pages[t]
                sq_, pg, ns = Pg["sq"], Pg["pg"], Pg["ns"]
                p3 = t % 3
                bo = 6 + sq_ % 2
                for c in range(NCH):
                    if pg < 0:
                        lh = vs_tok[0:TS, c * 128:(c + 1) * 128]
                        vkey = "vs_tok"
                    else:
                        lh = Vb[p3][:, c * 128:(c + 1) * 128]
                        vkey = ("Vb", p3)
                    OP(S, "pe", "matmul", [vkey, ("w", p3)], [bk(bo)], banks[bo][:, c * 64:(c + 1) * 64], lhsT=lh,
                       rhs=wb[p3][0:ns, :], start=(Pg["first"] and c == 0), stop=(Pg["last"] and c == NCH - 1),
                       skip_group_check=True)
                if Pg["last"]:
                    for c in range(NCH):
                        for hh in range(2):
                            h = 2 * c + hh
                            OP(S, "act", "activation", [bk(bo)], ["OsT"], out=OsT[64 * hh:64 * hh + 64, c, 4 * sq_:4 * sq_ + 4],
                               in_=banks[bo][64 * hh:64 * hh + 64, c * 64 + 4 * h:c * 64 + 4 * h + 4], func=AF.Copy)
            for t in range(npg + 2):
                if t < npg:
                    stA(t)
                if 1 <= t <= npg:
                    stB(t - 1)
                if t >= 2:
                    stC(t - 2)
            for c in range(NCH):
                DMA(S, wos[c % 2][:], w_o[c * 128:(c + 1) * 128, :], [], [("wos", c % 2)])
                OP(S, "pool", "tensor_copy", [("wos", c % 2)], [("wob", c)], out=wob[:, c, :], in_=wos[c % 2][:])
            for mo in range(NCH):
                b_ = 4 + mo % 2
                for c in range(NCH):
                    OP(S, "pe", "matmul", [("wob", c), "OsT"], [bk(b_)], banks[b_][:, 0:TS], lhsT=wob[:, c, mo * 128:(mo + 1) * 128],
                       rhs=OsT[:, c, :], start=(c == 0), stop=(c == NCH - 1))
                OP(S, "dve", "tensor_tensor", [bk(b_)], [("xs", mo)], out=x[:, mo, TP:T], in0=banks[b_][:, 0:TS], in1=x[:, mo, TP:T],
                   op=ALU.add)

    def pool_layer(li):
        with Phase() as ph:
            S = ph.S
            rstd = norm_stats(ph)
            rk = [("rstd", ti) for ti in range(5)]
            E0 = ph.sb([128, 15 + TP])
            EA = ph.sb([128, 15 + TP])
            EB = ph.sb([128, 15 + TP])
            X0 = ph.sb([128, 4, 19])
            XA = ph.sb([128, 4, 19])
            XB = ph.sb([128, 4, 19])
            t16 = ph.sb([128, 16])
            pws = ph.sb([128, 4, 2, 256])
            pwb = ph.sb([128, 4, 2, 256], BF16)
            DMA(S, pws[:], pool_w.rearrange("g (ci p) e -> p g ci e", p=128), [], ["pws"])
            OP(S, "pool", "tensor_copy", ["pws"], ["pwb"], out=pwb[:], in_=pws[:])
            OP(S, "pool", "memset", [], ["E0"], E0[:, 0:15], 0.0)
            OP(S, "pool", "memset", [], ["EA"], EA[:, 0:15], 0.0)
            OP(S, "pool", "memset", [], ["EB"], EB[:, 0:15], 0.0)
            OP(S, "pool", "memset", [], ["XA"], XA[:], 0.0)
            OP(S, "pool", "memset", [], ["XB"], XB[:], 0.0)
            for c in range(NCH):
                gi = c // 2
                w = 2 << gi
                eng = "dve" if c % 2 == 0 else "pool"
                OP(S, "dve", "scalar_tensor_tensor", ["x"] + rk + ["E0"], ["E0"], out=E0[:, 15:15 + TP], in0=x[:, c, 0:TP],
                   scalar=gains[:, li, c:c + 1], in1=rstd[:, 0:TP], op0=ALU.mult, op1=ALU.mult)
                OP(S, "dve", "tensor_copy", ["spool"], ["X0"], out=X0[:, :, 0:15], in_=spool[:, c, :, :])
                OP(S, "dve", "scalar_tensor_tensor", ["x"] + rk + ["X0"], ["X0"], out=X0[:, :, 15:19],
                   in0=x[:, c, TP:T].rearrange("p (s t) -> p s t", t=4), scalar=gains[:, li, c:c + 1],
                   in1=rstd[:, TP:T].rearrange("p (s t) -> p s t", t=4), op0=ALU.mult, op1=ALU.mult)
                DMA(S, pool_p[:, c, :], E0[:, TP:TP + 15], ["E0"], [])
                DMA(S, pool_s[:, c, :, :], X0[:, :, 4:19], ["X0"], [])
                src, srck, xsrc, xsrck = E0, "E0", X0, "X0"
                bufs = [(EA, "EA", XA, "XA"), (EB, "EB", XB, "XB")]
                stp = 1
                bi = 0
                while stp < w:
                    dst, dstk, xdst, xdstk = bufs[bi % 2]
                    bi += 1
                    OP(S, eng, "tensor_tensor", [srck], [dstk], out=dst[:, stp:15 + TP], in0=src[:, stp:15 + TP],
                       in1=src[:, 0:15 + TP - stp], op=ALU.add)
                    OP(S, eng, "tensor_tensor", [xsrck], [xdstk], out=xdst[:, :, stp:19], in0=xsrc[:, :, stp:19],
                       in1=xsrc[:, :, 0:19 - stp], op=ALU.add)
                    src, srck, xsrc, xsrck = dst, dstk, xdst, xdstk
                    stp *= 2
                uk = ("u16", c)
                OP(S, "dve", "scalar_tensor_tensor", [srck, "E0"], [uk], out=u16[:, c, OFF:OFF + TP], in0=src[:, 15:15 + TP],
                   scalar=1.0 / w, in1=E0[:, 15:15 + TP], op0=ALU.mult, op1=ALU.subtract)
                OP(S, "dve", "tensor_tensor", [srck, "rcnt"], ["t16"], out=t16[:], in0=src[:, 15:31], in1=rcnt[:, gi, :], op=ALU.mult)
                OP(S, "dve", "tensor_tensor", ["t16", "E0"], [uk], out=u16[:, c, OFF:OFF + 16], in0=t16[:], in1=E0[:, 15:31],
                   op=ALU.subtract)
                OP(S, "dve", "scalar_tensor_tensor", [xsrck, "X0"], [uk],
                   out=u16[:, c, OFF + TP:OFF + T].rearrange("p (s t) -> p s t", t=4), in0=xsrc[:, :, 15:19],
                   scalar=1.0 / w, in1=X0[:, :, 15:19], op0=ALU.mult, op1=ALU.subtract)
            it = 0
            for gi in range(4):
                for co in range(2):
                    mo = 2 * gi + co
                    for ti, (n0, nn) in enumerate(NT):
                        b_ = it % 2
                        it += 1
                        for ci in range(2):
                            OP(S, "pe", "matmul", ["pwb", ("u16", 2 * gi + ci)], [bk(b_)], banks[b_][:, 0:nn],
                               lhsT=pwb[:, gi, ci, co * 128:(co + 1) * 128], rhs=u16[:, 2 * gi + ci, OFF + n0:OFF + n0 + nn],
                               start=(ci == 0), stop=(ci == 1))
                        OP(S, "dve", "scalar_tensor_tensor", [bk(b_), "pscale"], [("x", mo, ti)], out=x[:, mo, n0:n0 + nn],
                           in0=banks[b_][:, 0:nn], scalar=pscale[:, mo:mo + 1], in1=x[:, mo, n0:n0 + nn],
                           op0=ALU.mult, op1=ALU.add)

    def ffn_layer(li):
        groups = [list(range(0, 4)), list(range(4, 8)), list(range(8, 12)), list(range(12, 16)),
                  list(range(16, 19)), list(range(19, 22))]
        with Phase() as ph:
            S = ph.S
            ws = WStream(ph, 128, nst=2, nbf=4, name="wu")
            wds = [ph.sb([128, D]) for _ in range(2)]
            wdb = [ph.sb([128, D], BF16) for _ in range(8)]
            a_g = [ph.sb([128, T], BF16) for _ in range(4)]
            tbuf = [[ph.sb([128, 416]) for _ in range(3)] for _ in range(2)]
            sgb = [ph.sb([128, 416]) for _ in range(3)]
            hse = [ph.sb([128, 4, 6]) for _ in range(2)]
            hlast = ph.sb([128, 44, 10])
            it = 0
            wdi = 0
            flat = [(gi_, cl, cc) for gi_, grp in enumerate(groups) for cl, cc in enumerate(grp)]
            loaded = {}

            def load_chunk(fi):
                gi_, cl, cc = flat[fi]
                wg_, kg = ws.load(w_up[li, :, cc * 128:(cc + 1) * 128])
                wv_, kv = ws.load(w_up[li, :, DFF + cc * 128:DFF + (cc + 1) * 128])
                sgi = fi % 2
                wslot = (gi_ % 2) * 4 + cl
                DMA(S, wds[sgi][:], w_down[li, cc * 128:(cc + 1) * 128, :], [], [("wds", sgi)])
                OP(S, "act", "activation", [("wds", sgi)], [("wdb", wslot)], out=wdb[wslot][:], in_=wds[sgi][:], func=AF.Copy)
                loaded[fi] = (wg_, kg, wv_, kv)
            load_chunk(0)
            fi = -1
            for gi_, grp in enumerate(groups):
                for cl, cc in enumerate(grp):
                    fi += 1
                    if fi + 1 < len(flat):
                        load_chunk(fi + 1)
                    wg_, kg, wv_, kv = loaded.pop(fi)
                    for ti, (n0, nn) in enumerate(NT):
                        par_ = it % 3
                        it += 1
                        for gv, (w_, kw_, chn) in enumerate(((wg_, kg, cc), (wv_, kv, 22 + cc))):
                            b_ = 2 * par_ + gv
                            tb__ = tbuf[gv][par_]
                            kh = ("h", gv, par_)
                            kt = ("t", gv, par_)
                            for k in range(NCH):
                                OP(S, "pe", "matmul", [kw_, "u16"], [bk(b_)], banks[b_][:, 0:nn + 2], lhsT=w_[:, k, :],
                                   rhs=u16[:, k, n0:n0 + nn + 2], start=(k == 0), stop=(k == NCH - 1))
                            OP(S, "act", "activation", [bk(b_)], [kt], out=tb__[:, 0:nn], in_=banks[b_][:, 0:nn],
                               func=AF.Identity, scale=cw[:, li, 0, chn:chn + 1], bias=cb[:, li, chn:chn + 1])
                            OP(S, "dve", "scalar_tensor_tensor", [bk(b_), kt], [kt], out=tb__[:, 0:nn], in0=banks[b_][:, 1:nn + 1],
                               scalar=cw[:, li, 1, chn:chn + 1], in1=tb__[:, 0:nn], op0=ALU.mult, op1=ALU.add)
                            OP(S, "dve", "scalar_tensor_tensor", [bk(b_), kt], [kt], out=tb__[:, 0:nn], in0=banks[b_][:, 2:nn + 2],
                               scalar=cw[:, li, 2, chn:chn + 1], in1=tb__[:, 0:nn], op0=ALU.mult, op1=ALU.add)
                            if ti == 4:
                                he = hse[gv]
                                ke = ("hse", gv)
                                OP(S, "dve", "tensor_copy", [], [ke], out=he[:, :, 0:2],
                                   in_=cbuf[:, li, chn, :].rearrange("p (s r) -> p s r", r=2))
                                OP(S, "dve", "tensor_copy", [bk(b_)], [ke], out=he[:, :, 2:6],
                                   in_=banks[b_][:, 402:418].rearrange("p (s t) -> p s t", t=4))
                                OP(S, "dve", "tensor_copy", [bk(b_)], ["hlast"], out=hlast[:, chn, 0:2], in_=banks[b_][:, 400:402])
                                tv = tb__[:, 400:416].rearrange("p (s t) -> p s t", t=4)
                                OP(S, "dve", "tensor_scalar", [ke, kt], [kt], out=tv, in0=he[:, :, 0:4],
                                   scalar1=cw[:, li, 0, chn:chn + 1], scalar2=cb[:, li, chn:chn + 1], op0=ALU.mult, op1=ALU.add)
                                OP(S, "dve", "scalar_tensor_tensor", [ke, kt], [kt], out=tv, in0=he[:, :, 1:5],
                                   scalar=cw[:, li, 1, chn:chn + 1], in1=tv, op0=ALU.mult, op1=ALU.add)
                                OP(S, "dve", "scalar_tensor_tensor", [ke, kt], [kt], out=tv, in0=he[:, :, 2:6],
                                   scalar=cw[:, li, 2, chn:chn + 1], in1=tv, op0=ALU.mult, op1=ALU.add)
                                OP(S, "pool", "tensor_copy", [ke], ["hlast"],
                                   out=hlast[:, chn, 2:10].rearrange("p (s r) -> p s r", r=2), in_=he[:, :, 4:6])
                        ksg = ("sg", par_)
                        OP(S, "act", "activation", [("t", 0, par_)], [ksg], out=sgb[par_][:, 0:nn], in_=tbuf[0][par_][:, 0:nn],
                           func=AF.Silu)
                        OP(S, "pool", "tensor_tensor", [ksg, ("t", 1, par_)], [("a", cl, ti)], out=a_g[cl][:, n0:n0 + nn],
                           in0=sgb[par_][:, 0:nn], in1=tbuf[1][par_][:, 0:nn], op=ALU.mult)
                dit = 0
                for mo in range(NCH):
                    for ti, (n0, nn) in enumerate(NT):
                        b_ = 6 + dit % 2
                        dit += 1
                        for cl in range(len(grp)):
                            wsl = (gi_ % 2) * 4 + cl
                            OP(S, "pe", "matmul", [("wdb", wsl), ("a", cl, ti)], [bk(b_)], banks[b_][:, 0:nn],
                               lhsT=wdb[wsl][:, mo * 128:(mo + 1) * 128], rhs=a_g[cl][:, n0:n0 + nn],
                               start=(cl == 0), stop=(cl == len(grp) - 1))
                        OP(S, "dve", "tensor_tensor", [bk(b_)], [("x", mo, ti)], out=x[:, mo, n0:n0 + nn],
                           in0=banks[b_][:, 0:nn], in1=x[:, mo, n0:n0 + nn], op=ALU.add)
            DMA(S, conv_o[:, li, :, :], hlast[:], ["hlast"], [])

    def final_out():
        with Phase() as ph:
            S = ph.S
            rstd = norm_stats(ph)
            rk = [("rstd", ti) for ti in range(5)]
            for c in range(NCH):
                OP(S, "dve", "scalar_tensor_tensor", ["x"] + rk, [("xo", c)], out=x[:, c, :], in0=x[:, c, :],
                   scalar=gains[:, 8, c:c + 1], in1=rstd[:, :], op0=ALU.mult, op1=ALU.mult)
        with Phase() as ph:
            S = ph.S
            yst = [ph.sb([128, D]) for _ in range(2)]
            oblocks = [(y_p[b * 128:(b + 1) * 128, :], 128, 16 + b * 128) for b in range(16)]
            oblocks.append((y_s, 16, TP))
            for bi, (dst, nr, c0) in enumerate(oblocks):
                yt_ = yst[bi % 2]
                for g in range(2):
                    b_ = (bi * 2 + g) % 4
                    for jj in range(4):
                        c = g * 4 + jj
                        OP(S, "pe", "transpose", ["idt"], [bk(b_)], out=banks[b_][0:nr, jj * 128:(jj + 1) * 128],
                           in_=x[:, c, c0:c0 + nr], identity=idt[:])
                    if g == 0:
                        OP(S, "act", "activation", [bk(b_)], [("yst", bi % 2, g)], out=yt_[0:nr, g * 512:(g + 1) * 512],
                           in_=banks[b_][0:nr, :], func=AF.Copy)
                    else:
                        OP(S, "dve", "tensor_copy", [bk(b_)], [("yst", bi % 2, g)], out=yt_[0:nr, g * 512:(g + 1) * 512],
                           in_=banks[b_][0:nr, :])
                DMA(S, dst, yt_[0:nr, :], [("yst", bi % 2, 0), ("yst", bi % 2, 1)], [])

    for li in range(DEPTH):
        if li * 10 > STOP:
            break
        if li % 3 == 0:
            norm_to_u16(li)
            ssm_layer(li // 3, li)
        elif li % 3 == 1:
            pool_layer(li)
        else:
            norm_to_u16(li)
            attn_layer(li)
        if li * 10 + 0 >= STOP:
            break
        norm_to_u16(4 + li)
        ffn_layer(li)
        if li * 10 + 1 >= STOP:
            break
    final_out()
    pst.close()
    return nc


_NC = None


def prep_core(inp, c):
    f = np.float32

    def fm(v):
        v = np.asarray(v, f)
        lead = v.shape[:-1]
        return np.ascontiguousarray(np.moveaxis(v.reshape(lead + (NCH, 128)), -1, 0))
    m = {}
    m["xp"] = np.ascontiguousarray(inp["x_prompt"][c])
    m["xs"] = np.ascontiguousarray(inp["x_sample"][4 * c:4 * c + 4].reshape(TS, D))
    m["meta"] = np.ascontiguousarray(inp["meta_tokens"])
    m["ident"] = np.eye(128, dtype=f)
    g = np.concatenate([inp["norm_mix_g"], inp["norm_ffn_g"], inp["norm_final_g"][None]], 0)
    m["gains"] = fm(g)
    cwv = inp["ffn_conv_w"].reshape(DEPTH, 3, 44, 128)
    m["cw"] = np.ascontiguousarray(cwv.transpose(3, 0, 1, 2))
    m["cb"] = np.ascontiguousarray(inp["ffn_conv_b"].reshape(DEPTH, 44, 128).transpose(2, 0, 1))
    cbv = inp["state_ffn_conv"][:, 4 * c:4 * c + 4]
    m["cbuf"] = np.ascontiguousarray(cbv.reshape(DEPTH, 8, 44, 128).transpose(3, 0, 2, 1))
    m["ssmd"] = fm(inp["ssm_d"])
    lam = np.stack([inp["ssm_lambda_re"], inp["ssm_lambda_im"],
                    np.broadcast_to(inp["ssm_log_dt"][:, :, None], (2, 64, 64))], 0)
    m["par"] = np.ascontiguousarray(lam.reshape(3, 2, 32, 2, 64).transpose(3, 4, 0, 1, 2).reshape(128, 3, 2, 32))
    st = np.stack([inp["state_ssm_re"][:, 4 * c:4 * c + 4], inp["state_ssm_im"][:, 4 * c:4 * c + 4]], 0)
    m["sst"] = np.ascontiguousarray(st.reshape(2, 2, 4, 32, 2, 64).transpose(4, 5, 0, 1, 3, 2).reshape(128, 2, 2, 32, 4))
    B = np.stack([inp["ssm_b_re"], inp["ssm_b_im"]], 0).reshape(2, 2, 32, 2, 64, 16)
    bm = np.zeros((2, 64, 2, 2, 32, 2, 16), f)
    for g2 in range(2):
        bm[g2, :, :, :, :, g2, :] = B[:, :, :, g2].transpose(3, 0, 1, 2, 4)
    m["bm"] = bm.reshape(128, 2, 2, 32, 32)
    C = np.stack([inp["ssm_c_re"], inp["ssm_c_im"]], 0).reshape(2, 2, 32, 2, 16, 64)
    cm = np.zeros((2, 64, 2, 2, 32, 2, 16), f)
    for g2 in range(2):
        cm[g2, :, :, :, :, g2, :] = C[:, :, :, g2].transpose(4, 0, 1, 2, 3)
    m["cm"] = cm.reshape(128, 2, 2, 32, 32)
    rm = np.zeros((128, 4), f)
    for jj in range(4):
        rm[32 * jj:32 * jj + 32, jj] = 1.0
    m["rowmask"] = rm
    sp_ = inp["state_pool"][0, 4 * c:4 * c + 4]
    m["spool"] = np.ascontiguousarray(sp_.reshape(4, 15, NCH, 128).transpose(3, 2, 0, 1))
    m["pscale"] = fm(inp["pool_scale"][0])
    rc = np.zeros((128, 4, 16), f)
    for gi in range(4):
        for t in range(16):
            rc[:, gi, t] = 1.0 / min(2 << gi, t + 1)
    m["rcnt"] = rc
    m["pool_w"] = inp["pool_w"][0]
    m["w_qkv"] = inp["attn_w_qkv"][0]
    m["w_o"] = inp["attn_w_o"][0]
    m["abias"] = np.ascontiguousarray(np.broadcast_to(inp["attn_logit_bias"][0][None, :], (128, 16))).astype(f)
    pidx = np.arange(128)
    m["tri"] = (pidx[:, None] >= pidx[None, :]).astype(f)
    jx = np.arange(1088)
    m["tm"] = ((jx[None, :] - 544) > pidx[:, None]).astype(f)
    mc = np.zeros((16, 4, 16, 4), f)
    for sq_ in range(4):
        for r in range(4):
            for q_ in range(4):
                if r < q_:
                    mc[4 * sq_ + r, sq_, :, q_] = 1.0
    m["mcur"] = mc.reshape(16, 4, 64)
    pt = inp["page_table"][4 * c:4 * c + 4].reshape(1, 256).astype(np.int32)
    m["ptab"] = np.ascontiguousarray(np.broadcast_to(pt, (128, 256)))
    m["iota"] = np.arange(128, dtype=f).reshape(128, 1)
    if "cache_k" in inp:
        m["cache_k"] = inp["cache_k"].reshape(2560 * 128, D)
        m["cache_v"] = inp["cache_v"].reshape(2560 * 128, D)
    m["w_glu"] = inp["ssm_w_glu"]
    m["w_up"] = inp["ffn_w_up"]
    m["w_down"] = inp["ffn_w_down"]
    return m


def kernel(**inp):
    global _NC
    inp = {k: np.asarray(v) for k, v in inp.items()}
    if _NC is None:
        _NC = build_nc()
    nc = _NC
    in_maps = [prep_core(inp, c) for c in range(NCORES)]
    res = run_bass_kernel_spmd(nc, in_maps, core_ids=list(range(NCORES)))
    R = res.results
    y_prompt = np.stack([R[c]["y_p"] for c in range(NCORES)])
    y_sample = np.stack([R[c]["y_s"].reshape(4, 4, D) for c in range(NCORES)]).reshape(32, 4, D)
    sp = np.stack([R[c]["ssm_p"] for c in range(NCORES)])
    sp = sp.reshape(NCORES, 2, 64, 2, 2, 32).transpose(3, 4, 0, 5, 1, 2).reshape(2, 2, NCORES, 64, 64)
    ss = np.stack([R[c]["ssm_s"] for c in range(NCORES)])
    ss = ss.reshape(NCORES, 2, 64, 2, 2, 32, 4).transpose(3, 4, 0, 6, 5, 1, 2).reshape(2, 2, 32, 64, 64)
    co = np.stack([R[c]["conv_o"] for c in range(NCORES)])
    co = co.transpose(2, 0, 4, 3, 1).reshape(DEPTH, NCORES, 10, 2 * DFF)
    conv_prompt = np.ascontiguousarray(co[:, :, 0:2])
    conv_sample = np.ascontiguousarray(co[:, :, 2:10].reshape(DEPTH, NCORES, 4, 2, 2 * DFF).reshape(DEPTH, 32, 2, 2 * DFF))
    pp_ = np.stack([R[c]["pool_p"] for c in range(NCORES)])
    pool_prompt = np.ascontiguousarray(pp_.transpose(0, 3, 2, 1).reshape(1, NCORES, 15, D))
    ps_ = np.stack([R[c]["pool_s"] for c in range(NCORES)])
    pool_sample = np.ascontiguousarray(ps_.transpose(0, 3, 4, 2, 1).reshape(1, 32, 15, D))
    k_prompt = np.stack([R[c]["k_p"] for c in range(NCORES)]).reshape(1, NCORES, TP, 16, 64)
    v_prompt = np.stack([R[c]["v_p"] for c in range(NCORES)]).reshape(1, NCORES, TP, 16, 64)
    k_sample = np.stack([R[c]["k_s"] for c in range(NCORES)]).reshape(1, 32, 4, 16, 64)
    v_sample = np.stack([R[c]["v_s"] for c in range(NCORES)]).reshape(1, 32, 4, 16, 64)
    return (y_prompt, y_sample, k_prompt, v_prompt, k_sample, v_sample,
            np.ascontiguousarray(sp[0]), np.ascontiguousarray(sp[1]), np.ascontiguousarray(ss[0]), np.ascontiguousarray(ss[1]),
            pool_prompt, pool_sample, conv_prompt, conv_sample)
```

```python
import contextlib
import math
import os
import numpy as np
import concourse.bass as bass
import concourse.mybir as mybir
from concourse.bass_utils import run_bass_kernel_spmd

F32 = mybir.dt.float32
BF16 = mybir.dt.bfloat16
I32 = mybir.dt.int32
AF = mybir.ActivationFunctionType
ALU = mybir.AluOpType

NCORES = 8
D = 1024
NCH = 8
TP = 2064
TS = 16
T = TP + TS
OFF = 2
TW = T + OFF
DFF = 2816
NFF = 22
EPS = 1e-6
NT = [(0, 416), (416, 416), (832, 416), (1248, 416), (1664, 416)]
NKC = 129
TWO_PI = 2.0 * math.pi
DEPTH = 4


class Op:
    __slots__ = ("eng", "fn", "deps", "needed", "val", "sem", "is_dma")

    def __init__(self, eng, fn, is_dma=False):
        self.eng = eng
        self.fn = fn
        self.deps = []
        self.needed = False
        self.val = None
        self.sem = None
        self.is_dma = is_dma


class Sched:
    ENGS = ("sync", "act", "dve", "pool", "pe")
    UID = 0

    def __init__(self, nc):
        self.nc = nc
        self.ops = {e: [] for e in self.ENGS}
        self.last_w = {}
        self.readers = {}
        self.ndma = {"sync": 0, "pool": 0}
        self.npool = {"sync": 8, "pool": 6}
        self.dma_hist = {"sync": [], "pool": []}

    def _add(self, op, reads, writes):
        reads = list(reads)
        writes = list(writes)
        for k in list(reads):
            if isinstance(k, tuple) and k[0] == "bank":
                writes.append(("bankrd", k[1]))
        deps = []
        for k in reads:
            w = self.last_w.get(k)
            if w is not None:
                deps.append(w)
        for k in writes:
            w = self.last_w.get(k)
            if w is not None:
                deps.append(w)
            lastr = {}
            for r in self.readers.get(k, ()):
                if r.is_dma:
                    deps.append(r)
                else:
                    lastr[r.eng] = r
            deps.extend(lastr.values())
        if op.eng == "pe" and not op.is_dma:
            deps = [d for d in deps if not (d.eng == "pe" and not d.is_dma)]
        seen = set()
        for d in deps:
            if id(d) not in seen and d is not op:
                seen.add(id(d))
                op.deps.append(d)
                d.needed = True
        for k in writes:
            self.last_w[k] = op
            self.readers[k] = []
        for k in reads:
            self.readers.setdefault(k, []).append(op)
        self.ops[op.eng].append(op)
        return op

    def op(self, eng, fn, reads=(), writes=()):
        return self._add(Op(eng, fn), reads, writes)

    def dma(self, fn, reads=(), writes=(), q="sync"):
        op = Op(q, fn, is_dma=True)
        i = self.ndma[q]
        self.ndma[q] += 1
        n = self.npool[q]
        op.sem = (q, i % n)
        op.val = 16 * (i // n + 1)
        hist = self.dma_hist[q]
        if i >= n:
            op.deps.append(hist[i - n])
        hist.append(op)
        op.needed = True
        return self._add(op, reads, writes)

    def emit(self):
        nc = self.nc
        with contextlib.ExitStack() as st:
            Sched.UID += 1
            u = Sched.UID
            allsems = []

            def newsem(name):
                h_ = nc.alloc_semaphore(name=name)
                allsems.append(h_)
                return h_
            esem = {e: newsem("se%d_%s" % (u, e)) for e in self.ENGS}
            dsem = {}
            for q, n in self.npool.items():
                for j in range(min(n, self.ndma[q])):
                    dsem[(q, j)] = newsem("sd%d_%s%d" % (u, q, j))
            for e in self.ENGS:
                c = 0
                for op in self.ops[e]:
                    if op.is_dma:
                        continue
                    if op.needed:
                        c += 1
                        op.val = c
                        op.sem = ("e", e)

            def semof(op):
                return esem[op.sem[1]] if op.sem[0] == "e" else dsem[op.sem]

            block = st.enter_context(nc.Block())

            def replay(e, h):
                waited = {}
                for op in self.ops[e]:
                    need = {}
                    for d in op.deps:
                        if waited.get(d.sem, 0) >= d.val:
                            continue
                        if need.get(d.sem, (0, None))[0] < d.val:
                            need[d.sem] = (d.val, d)
                    for key, (v, d) in need.items():
                        waited[key] = v
                        h.wait_ge(semof(d), v)
                    ins = op.fn(h)
                    if op.is_dma:
                        ins.then_inc(semof(op), 16)
                    elif op.needed:
                        ins.then_inc(semof(op), 1)
                if e in self.dma_hist:
                    last = {}
                    for op in self.dma_hist[e]:
                        last[op.sem] = op
                    for key, op in last.items():
                        if waited.get(key, 0) < op.val:
                            h.wait_ge(semof(op), op.val)

            @block.sync
            def _(h):
                replay("sync", h)

            @block.scalar
            def _(h):
                replay("act", h)

            @block.vector
            def _(h):
                replay("dve", h)

            @block.gpsimd
            def _(h):
                replay("pool", h)

            @block.tensor
            def _(h):
                replay("pe", h)
            st.close()
            nc.clear_and_free_semaphores(allsems)
            nc.all_engine_barrier()


def OP(S, eng, method, reads, writes, *args, **kw):
    return S.op(eng, lambda h: getattr(h, method)(*args, **kw), reads, writes)


def DMA(S, out, in_, reads, writes, q="sync", **kw):
    return S.dma(lambda h: h.dma_start(out=out, in_=in_, **kw), reads, writes, q=q)


def build_nc():
    nc = bass.Bass("TRN2", target_bir_lowering=False)
    STOP = int(os.environ.get("KSTOP", "99"))
    SSTOP = int(os.environ.get("KSSTOP", "99"))
    ASTOP = int(os.environ.get("KASTOP", "99"))

    def din(name, shape, dt=F32):
        return nc.dram_tensor(name, list(shape), dt, kind="ExternalInput").ap()

    def dout(name, shape, dt=F32):
        return nc.dram_tensor(name, list(shape), dt, kind="ExternalOutput").ap()

    xp = din("xp", [2048, D])
    xs = din("xs", [TS, D])
    meta = din("meta", [16, D])
    ident = din("ident", [128, 128])
    gains_d = din("gains", [128, 9, NCH])
    cw_d = din("cw", [128, DEPTH, 3, 44])
    cb_d = din("cb", [128, DEPTH, 44])
    cbuf_d = din("cbuf", [128, DEPTH, 44, 8])
    ssmd_d = din("ssmd", [128, 2, NCH])
    par_d = din("par", [128, 3, 2, 32])
    sst_d = din("sst", [128, 2, 2, 32, 4])
    bm_d = din("bm", [128, 2, 2, 32, 32])
    cm_d = din("cm", [128, 2, 2, 32, 32])
    rowmask_d = din("rowmask", [128, 4])
    spool_d = din("spool", [128, NCH, 4, 15])
    pscale_d = din("pscale", [128, NCH])
    rcnt_d = din("rcnt", [128, 4, 16])
    pool_w = din("pool_w", [4, 256, 256])
    w_qkv = din("w_qkv", [D, 3 * D])
    w_o = din("w_o", [D, D])
    abias_d = din("abias", [128, 16])
    tri_d = din("tri", [128, 128])
    tm_d = din("tm", [128, 1088])
    mcur_d = din("mcur", [16, 4, 64])
    ptab_d = din("ptab", [128, 256], I32)
    iota_d = din("iota", [128, 1])
    cache_k = din("cache_k", [2560 * 128, D])
    cache_v = din("cache_v", [2560 * 128, D])
    w_glu = din("w_glu", [2, D, 2 * D])
    w_up = din("w_up", [DEPTH, D, 2 * DFF])
    w_down = din("w_down", [DEPTH, DFF, D])

    y_p = dout("y_p", [2048, D])
    y_s = dout("y_s", [TS, D])
    ssm_p = dout("ssm_p", [128, 2, 2, 32])
    ssm_s = dout("ssm_s", [128, 2, 2, 32, 4])
    conv_o = dout("conv_o", [128, DEPTH, 44, 10])
    k_p = dout("k_p", [TP, D])
    v_p = dout("v_p", [TP, D])
    k_s = dout("k_s", [TS, D])
    v_s = dout("v_s", [TS, D])
    pool_p = dout("pool_p", [128, NCH, 15])
    pool_s = dout("pool_s", [128, NCH, 4, 15])

    pst = contextlib.ExitStack()

    def sbp(name, shape, dt=F32):
        return pst.enter_context(nc.sbuf_tensor(name, list(shape), dt))

    x = sbp("x", [128, NCH, T])
    u16 = sbp("u16", [128, NCH, TW], BF16)
    idt = sbp("idt", [128, 128])
    ones_b = sbp("ones_b", [128, 128], BF16)
    gains = sbp("gains_sb", [128, 9, NCH])
    cw = sbp("cw_sb", [128, DEPTH, 3, 44])
    cb = sbp("cb_sb", [128, DEPTH, 44])
    cbuf = sbp("cbuf_sb", [128, DEPTH, 44, 8])
    ssmd = sbp("ssmd_sb", [128, 2, NCH])
    par = sbp("par_sb", [128, 3, 2, 32])
    sst = sbp("sst_sb", [128, 2, 2, 32, 4])
    rowmask = sbp("rowmask_sb", [128, 4])
    spool = sbp("spool_sb", [128, NCH, 4, 15])
    pscale = sbp("pscale_sb", [128, NCH])
    rcnt = sbp("rcnt_sb", [128, 4, 16])
    banks = [pst.enter_context(nc.psum_tensor("bank%d" % i, [128, 512], F32)) for i in range(8)]
    uid = [0]

    class Phase:
        def __enter__(self):
            self.S = Sched(nc)
            self.st = contextlib.ExitStack()

            def sb(shape, dt=F32):
                uid[0] += 1
                return self.st.enter_context(nc.sbuf_tensor("t%d" % uid[0], list(shape), dt))
            self.sb = sb
            return self

        def __exit__(self, *a):
            if a[0] is None:
                self.S.emit()
            self.st.close()
            return False

    def bk(i):
        return ("bank", i)

    with Phase() as ph:
        S = ph.S
        for dst, src, key in ((idt, ident, "idt"), (gains, gains_d, "gains"), (cw, cw_d, "cw"), (cb, cb_d, "cb"),
                              (cbuf, cbuf_d, "cbuf"), (ssmd, ssmd_d, "ssmd"), (par, par_d, "par"),
                              (sst, sst_d, "sst"), (rowmask, rowmask_d, "rowmask"),
                              (spool, spool_d, "spool"), (pscale, pscale_d, "pscale"), (rcnt, rcnt_d, "rcnt")):
            DMA(S, dst[:], src, [], [key])
        OP(S, "pool", "memset", [], ["ones_b"], ones_b[:], 1.0)
        OP(S, "pool", "memset", [], ["u16"], u16[:], 0.0)
        stg = [ph.sb([128, D]) for _ in range(2)]
        blocks = [(meta, 16, 0)]
        for b in range(16):
            blocks.append((xp[b * 128:(b + 1) * 128, :], 128, 16 + b * 128))
        blocks.append((xs, 16, TP))
        for bi, (src, nr, c0) in enumerate(blocks):
            sg = stg[bi % 2]
            DMA(S, sg[0:nr, :], src, [], [("stg", bi % 2)])
            for g in range(2):
                b_ = (bi * 2 + g) % 4
                for jj in range(4):
                    c = g * 4 + jj
                    OP(S, "pe", "transpose", [("stg", bi % 2), "idt"], [bk(b_)],
                       out=banks[b_][:, jj * 128:jj * 128 + nr], in_=sg[0:nr, c * 128:(c + 1) * 128],
                       identity=idt[0:nr, 0:nr])
                src_v = banks[b_][:].rearrange("p (j n) -> p j n", j=4)[:, :, 0:nr]
                dst_v = x[:, g * 4:(g + 1) * 4, c0:c0 + nr]
                if g == 0:
                    OP(S, "act", "activation", [bk(b_)], [("x", bi, g)], out=dst_v, in_=src_v, func=AF.Copy)
                else:
                    OP(S, "dve", "tensor_copy", [bk(b_)], [("x", bi, g)], out=dst_v, in_=src_v)

    def norm_stats(ph):
        S = ph.S
        sq = ph.sb([128, NCH, T], BF16)
        rstd = ph.sb([128, T])
        for c in range(NCH):
            OP(S, "act", "activation", ["x"], [("sq", c)], out=sq[:, c, :], in_=x[:, c, :], func=AF.Square)
        for ti, (n0, nn) in enumerate(NT):
            b_ = 6 + ti % 2
            for c in range(NCH):
                OP(S, "pe", "matmul", [("sq", c), "ones_b"], [bk(b_)], banks[b_][:, 0:nn], lhsT=ones_b[:],
                   rhs=sq[:, c, n0:n0 + nn], start=(c == 0), stop=(c == NCH - 1))
            OP(S, "act", "activation", [bk(b_)], [("rstd", ti)], out=rstd[:, n0:n0 + nn], in_=banks[b_][:, 0:nn],
               func=AF.Sqrt, bias=EPS, scale=1.0 / D)
            OP(S, "dve", "reciprocal", [("rstd", ti)], [("rstd", ti)], out=rstd[:, n0:n0 + nn], in_=rstd[:, n0:n0 + nn])
        return rstd

    def norm_to_u16(gi):
        with Phase() as ph:
            S = ph.S
            rstd = norm_stats(ph)
            rk = [("rstd", ti) for ti in range(5)]
            for c in range(NCH):
                OP(S, "dve", "scalar_tensor_tensor", ["x"] + rk, [("u16", c)], out=u16[:, c, OFF:OFF + T],
                   in0=x[:, c, :], scalar=gains[:, gi, c:c + 1], in1=rstd[:, :], op0=ALU.mult, op1=ALU.mult)

    class WStream:
        def __init__(self, ph, ncols, nst=2, nbf=2, name="w"):
            self.ph = ph
            self.stg = [ph.sb([128, 8, ncols]) for _ in range(nst)]
            self.bf = [ph.sb([128, 8, ncols], BF16) for _ in range(nbf)]
            self.i = 0
            self.name = name

        def load(self, src2d, cast_eng="pool"):
            S = self.ph.S
            i = self.i
            self.i += 1
            sg = self.stg[i % len(self.stg)]
            bf = self.bf[i % len(self.bf)]
            ks = (self.name + "s", i % len(self.stg))
            kb = (self.name + "b", i % len(self.bf))
            DMA(S, sg[:], src2d.rearrange("(k p) n -> p k n", p=128), [], [ks])
            OP(S, cast_eng, "tensor_copy", [ks], [kb], out=bf[:], in_=sg[:])
            return bf, kb

    def ssm_layer(j, li):
        sst_ = contextlib.ExitStack()

        def sbs(name, shape, dt=F32):
            return sst_.enter_context(nc.sbuf_tensor("%s_%d" % (name, j), list(shape), dt))
        S_r = sbs("S_r", [128, 32, NKC + 1])
        S_i = sbs("S_i", [128, 32, NKC + 1])
        pw_r = sbs("pw_r", [128, 17, 32])
        pw_i = sbs("pw_i", [128, 17, 32])
        wri = sbs("wri", [128, 2, 32])
        aa = sbs("aa", [128, 2, 32])
        bus = sbs("bus", [128, 2, 32, 16])
        hs = sbs("hs", [128, 2, 32, 4, 4])

        lamr = par[:, 0, j, :]
        lami = par[:, 1, j, :]
        ldt = par[:, 2, j, :]

        with Phase() as ph:
            S = ph.S
            pp = ph.sb([128, 16, 32])
            ki = ph.sb([128, 32], I32)

            def P(i):
                return pp[:, i, :]

            def K(i):
                return ("pp", i)
            dt_, mag, phi, kf, tmp, msk, sinv, cosv, den, am1, t1, t2, phc = range(13)
            OP(S, "act", "activation", [], [K(dt_)], out=P(dt_), in_=ldt, func=AF.Exp)
            OP(S, "dve", "tensor_tensor", [K(dt_)], [K(tmp)], out=P(tmp), in0=lamr, in1=P(dt_), op=ALU.mult)
            OP(S, "act", "activation", [K(tmp)], [K(mag)], out=P(mag), in_=P(tmp), func=AF.Exp)
            OP(S, "dve", "tensor_tensor", [K(dt_)], [K(phi)], out=P(phi), in0=lami, in1=P(dt_), op=ALU.mult)
            OP(S, "dve", "tensor_scalar", [K(phi)], ["ki"], out=ki[:], in0=P(phi), scalar1=1.0 / TWO_PI, scalar2=None,
               op0=ALU.mult)
            OP(S, "dve", "tensor_copy", ["ki"], [K(kf)], out=P(kf), in_=ki[:])
            OP(S, "dve", "scalar_tensor_tensor", [K(kf), K(phi)], [K(phi)], out=P(phi), in0=P(kf), scalar=-TWO_PI,
               in1=P(phi), op0=ALU.mult, op1=ALU.add)

            def fold(pi_):
                OP(S, "dve", "tensor_scalar", [K(pi_)], [K(msk)], out=P(msk), in0=P(pi_), scalar1=math.pi, scalar2=None,
                   op0=ALU.is_gt)
                OP(S, "dve", "scalar_tensor_tensor", [K(msk), K(pi_)], [K(pi_)], out=P(pi_), in0=P(msk), scalar=-TWO_PI,
                   in1=P(pi_), op0=ALU.mult, op1=ALU.add)
                OP(S, "dve", "tensor_scalar", [K(pi_)], [K(msk)], out=P(msk), in0=P(pi_), scalar1=-math.pi, scalar2=None,
                   op0=ALU.is_lt)
                OP(S, "dve", "scalar_tensor_tensor", [K(msk), K(pi_)], [K(pi_)], out=P(pi_), in0=P(msk), scalar=TWO_PI,
                   in1=P(pi_), op0=ALU.mult, op1=ALU.add)
            fold(phi)
            OP(S, "act", "activation", [K(phi)], [K(sinv)], out=P(sinv), in_=P(phi), func=AF.Sin)
            OP(S, "dve", "tensor_scalar", [K(phi)], [K(phc)], out=P(phc), in0=P(phi), scalar1=math.pi / 2, scalar2=None,
               op0=ALU.add)
            fold(phc)
            OP(S, "act", "activation", [K(phc)], [K(cosv)], out=P(cosv), in_=P(phc), func=AF.Sin)
            ar = aa[:, 0, :]
            ai = aa[:, 1, :]
            OP(S, "dve", "tensor_tensor", [K(mag), K(cosv)], ["ar"], out=ar, in0=P(mag), in1=P(cosv), op=ALU.mult)
            OP(S, "dve", "tensor_tensor", [K(mag), K(sinv)], ["ai"], out=ai, in0=P(mag), in1=P(sinv), op=ALU.mult)
            OP(S, "dve", "tensor_scalar", ["ar"], [K(am1)], out=P(am1), in0=ar, scalar1=-1.0, scalar2=None, op0=ALU.add)
            OP(S, "dve", "tensor_tensor", [], [K(t1)], out=P(t1), in0=lamr, in1=lamr, op=ALU.mult)
            OP(S, "dve", "tensor_tensor", [], [K(t2)], out=P(t2), in0=lami, in1=lami, op=ALU.mult)
            OP(S, "dve", "tensor_tensor", [K(t1), K(t2)], [K(den)], out=P(den), in0=P(t1), in1=P(t2), op=ALU.add)
            OP(S, "dve", "reciprocal", [K(den)], [K(den)], out=P(den), in_=P(den))
            OP(S, "dve", "tensor_tensor", [K(am1)], [K(t1)], out=P(t1), in0=P(am1), in1=lamr, op=ALU.mult)
            OP(S, "dve", "tensor_tensor", ["ai"], [K(t2)], out=P(t2), in0=ai, in1=lami, op=ALU.mult)
            OP(S, "dve", "tensor_tensor", [K(t1), K(t2)], [K(t1)], out=P(t1), in0=P(t1), in1=P(t2), op=ALU.add)
            OP(S, "dve", "tensor_tensor", [K(t1), K(den)], ["wr"], out=wri[:, 0, :], in0=P(t1), in1=P(den), op=ALU.mult)
            OP(S, "dve", "tensor_tensor", ["ai"], [K(t1)], out=P(t1), in0=ai, in1=lamr, op=ALU.mult)
            OP(S, "dve", "tensor_tensor", [K(am1)], [K(t2)], out=P(t2), in0=P(am1), in1=lami, op=ALU.mult)
            OP(S, "dve", "tensor_tensor", [K(t1), K(t2)], [K(t1)], out=P(t1), in0=P(t1), in1=P(t2), op=ALU.subtract)
            OP(S, "dve", "tensor_tensor", [K(t1), K(den)], ["wi"], out=wri[:, 1, :], in0=P(t1), in1=P(den), op=ALU.mult)
            OP(S, "dve", "memset", [], [("pwr", 0)], pw_r[:, 0, :], 1.0)
            OP(S, "dve", "memset", [], [("pwi", 0)], pw_i[:, 0, :], 0.0)
            for n in range(1, 17):
                OP(S, "dve", "tensor_tensor", ["ar", ("pwr", n - 1)], [K(t1)], out=P(t1), in0=ar, in1=pw_r[:, n - 1, :], op=ALU.mult)
                OP(S, "dve", "tensor_tensor", ["ai", ("pwi", n - 1)], [K(t2)], out=P(t2), in0=ai, in1=pw_i[:, n - 1, :], op=ALU.mult)
                OP(S, "dve", "tensor_tensor", [K(t1), K(t2)], [("pwr", n)], out=pw_r[:, n, :], in0=P(t1), in1=P(t2), op=ALU.subtract)
                OP(S, "dve", "tensor_tensor", ["ar", ("pwi", n - 1)], [K(t1)], out=P(t1), in0=ar, in1=pw_i[:, n - 1, :], op=ALU.mult)
                OP(S, "dve", "tensor_tensor", ["ai", ("pwr", n - 1)], [K(t2)], out=P(t2), in0=ai, in1=pw_r[:, n - 1, :], op=ALU.mult)
                OP(S, "dve", "tensor_tensor", [K(t1), K(t2)], [("pwi", n)], out=pw_i[:, n, :], in0=P(t1), in1=P(t2), op=ALU.add)

        def bbar(ph, m, eng="dve"):
            S = ph.S
            if not hasattr(ph, "bb_tiles"):
                ph.bb_tiles = ([ph.sb([128, 2, 4, 32]) for _ in range(2)], ph.sb([128, 2, 4, 32]), ph.sb([128, 2, 4, 32]))
            bms, tb, bb = ph.bb_tiles
            bmt = bms[m % 2]
            DMA(S, bmt[:], bm_d[:, :, j, 4 * m:4 * m + 4, :], [], [("bm", m % 2)])
            wr_b = wri[:, 0, 4 * m:4 * m + 4].unsqueeze(2).broadcast_to([128, 4, 32])
            wi_b = wri[:, 1, 4 * m:4 * m + 4].unsqueeze(2).broadcast_to([128, 4, 32])
            OP(S, eng, "tensor_tensor", [("bm", m % 2)], ["tb0"], out=tb[:, 0], in0=bmt[:, 0], in1=wr_b, op=ALU.mult)
            OP(S, eng, "tensor_tensor", [("bm", m % 2)], ["tb1"], out=tb[:, 1], in0=bmt[:, 1], in1=wi_b, op=ALU.mult)
            OP(S, eng, "tensor_tensor", ["tb0", "tb1"], ["bbr"], out=bb[:, 0], in0=tb[:, 0], in1=tb[:, 1], op=ALU.subtract)
            OP(S, eng, "tensor_tensor", [("bm", m % 2)], ["tb0"], out=tb[:, 0], in0=bmt[:, 1], in1=wr_b, op=ALU.mult)
            OP(S, eng, "tensor_tensor", [("bm", m % 2)], ["tb1"], out=tb[:, 1], in0=bmt[:, 0], in1=wi_b, op=ALU.mult)
            OP(S, eng, "tensor_tensor", ["tb0", "tb1"], ["bbi"], out=bb[:, 1], in0=tb[:, 0], in1=tb[:, 1], op=ALU.add)
            return bb

        with Phase() as ph:
            S = ph.S
            ta = ph.sb([128, 8, 128])
            td_ = ph.sb([128, 8, 128])
            Dr = ph.sb([128, 8, 128])
            Di = ph.sb([128, 8, 128])
            BT = [ph.sb([128, 16, 128], BF16) for _ in range(2)]
            um = [ph.sb([128, TW], BF16) for _ in range(4)]
            OP(S, "pool", "memset", [], [("S", 0, -1)], S_r[:, :, 0:1], 0.0)
            OP(S, "pool", "memset", [], [("S", 1, -1)], S_i[:, :, 0:1], 0.0)
            tcount = 0
            for m in range(NCH):
                bb = bbar(ph, m)
                for jj in range(4):
                    OP(S, "act", "activation", ["rowmask"], [("um", jj)], out=um[jj][:], in_=u16[:, m, :], func=AF.Identity,
                       scale=rowmask[:, jj:jj + 1], bias=0.0)
                for half in range(2):
                    n0 = 8 * half
                    pr = pw_r[:, n0:n0 + 8, 4 * m:4 * m + 4].unsqueeze(3).broadcast_to([128, 8, 4, 32])
                    pi_ = pw_i[:, n0:n0 + 8, 4 * m:4 * m + 4].unsqueeze(3).broadcast_to([128, 8, 4, 32])
                    br = bb[:, 0].unsqueeze(1).broadcast_to([128, 8, 4, 32])
                    bi = bb[:, 1].unsqueeze(1).broadcast_to([128, 8, 4, 32])

                    def v4(t):
                        return t[:].rearrange("p n (i c) -> p n i c", i=4)
                    OP(S, "dve", "tensor_tensor", ["bbr"], ["Dr"], out=v4(Dr), in0=pr, in1=br, op=ALU.mult)
                    OP(S, "dve", "tensor_tensor", ["bbi"], ["ta"], out=v4(ta), in0=pi_, in1=bi, op=ALU.mult)
                    OP(S, "dve", "tensor_tensor", ["Dr", "ta"], ["Dr"], out=Dr[:], in0=Dr[:], in1=ta[:], op=ALU.subtract)
                    OP(S, "pool", "tensor_tensor", ["bbi"], ["Di"], out=v4(Di), in0=pr, in1=bi, op=ALU.mult)
                    OP(S, "pool", "tensor_tensor", ["bbr"], ["td"], out=v4(td_), in0=pi_, in1=br, op=ALU.mult)
                    OP(S, "pool", "tensor_tensor", ["Di", "td"], ["Di"], out=Di[:], in0=Di[:], in1=td_[:], op=ALU.add)
                    for ri, (Dsrc, dk) in enumerate(((Dr, "Dr"), (Di, "Di"))):
                        for g in range(2):
                            b_ = tcount % 2
                            tcount += 1
                            for q in range(4):
                                OP(S, "pe", "transpose", [dk, "idt"], [bk(b_)], out=banks[b_][:, q * 128:(q + 1) * 128],
                                   in_=Dsrc[:, g * 4 + q, :], identity=idt[:])
                            OP(S, "act", "activation", [bk(b_)], [("BT", ri)],
                               out=BT[ri][:, n0 + g * 4:n0 + g * 4 + 4, :].rearrange("p n q -> p (n q)"),
                               in_=banks[b_][:], func=AF.Copy)
                pb = [2, 3, 4] if m % 2 == 0 else [5, 6, 7]
                slot = 0
                for jj in range(4):
                    for ri in range(2):
                        b_ = pb[slot // 3]
                        c0 = (slot % 3) * NKC
                        slot += 1
                        for tq in range(16):
                            OP(S, "pe", "matmul", [("BT", ri), ("um", jj)], [bk(b_)], banks[b_][:, c0:c0 + NKC],
                               lhsT=BT[ri][:, 15 - tq, :], rhs=um[jj][:, OFF + tq:OFF + tq + TP:16],
                               start=(tq == 0), stop=(tq == 15))
                b_ = pb[2]
                for jj in range(4):
                    for ri in range(2):
                        c0 = 258 + (jj * 2 + ri) * 16
                        OP(S, "pe", "matmul", [("BT", ri), ("um", jj)], [bk(b_)], banks[b_][:, c0:c0 + 16],
                           lhsT=BT[ri][:, 0, :], rhs=um[jj][:, OFF + TP:OFF + T], start=True, stop=True)
                slot = 0
                for jj in range(4):
                    for ri, Sdst in enumerate((S_r, S_i)):
                        b_ = pb[slot // 3]
                        c0 = (slot % 3) * NKC
                        slot += 1
                        OP(S, "act", "activation", [bk(b_)], [("S", ri, m)], out=Sdst[:, 4 * m + jj, 1:NKC + 1],
                           in_=banks[b_][:, c0:c0 + NKC], func=AF.Copy)
                OP(S, "act", "activation", [bk(pb[2])], [("bus", m)],
                   out=bus[:, :, 4 * m:4 * m + 4, :].rearrange("p r i c -> p i r c"),
                   in_=banks[pb[2]][:, 258:386].rearrange("p (i r c) -> p i r c", i=4, r=2), func=AF.Copy)

        if SSTOP <= 1:
            sst_.close()
            return
        with Phase() as ph:
            S = ph.S
            tt_ = ph.sb([128, 4, 32])
            ts_ = ph.sb([128, 4, 32, 4])
            A_r = pw_r[:, 16, :]
            A_i = pw_i[:, 16, :]
            ar_b = aa[:, 0, :].unsqueeze(2).broadcast_to([128, 32, 4])
            ai_b = aa[:, 1, :].unsqueeze(2).broadcast_to([128, 32, 4])
            for stp in range(4):
                if stp == 0:
                    hr_p = sst[:, 0, j, :, :]
                    hi_p = sst[:, 1, j, :, :]
                else:
                    hr_p = hs[:, 0, :, :, stp - 1]
                    hi_p = hs[:, 1, :, :, stp - 1]
                bur = bus[:, 0, :, :].rearrange("p i (s t) -> p i s t", t=4)[:, :, :, stp]
                bui = bus[:, 1, :, :].rearrange("p i (s t) -> p i s t", t=4)[:, :, :, stp]
                hr_n = hs[:, 0, :, :, stp]
                hi_n = hs[:, 1, :, :, stp]
                e = "pool"
                OP(S, e, "tensor_tensor", [("hs", stp - 1)], ["ts0"], out=ts_[:, 0], in0=hr_p, in1=ar_b, op=ALU.mult)
                OP(S, e, "tensor_tensor", [("hs", stp - 1)], ["ts1"], out=ts_[:, 1], in0=hi_p, in1=ai_b, op=ALU.mult)
                OP(S, e, "tensor_tensor", ["ts0", "ts1"], ["ts0"], out=ts_[:, 0], in0=ts_[:, 0], in1=ts_[:, 1], op=ALU.subtract)
                OP(S, e, "tensor_tensor", ["ts0"], [("hsr", stp)], out=hr_n, in0=ts_[:, 0], in1=bur, op=ALU.add)
                OP(S, e, "tensor_tensor", [("hs", stp - 1)], ["ts2"], out=ts_[:, 2], in0=hi_p, in1=ar_b, op=ALU.mult)
                OP(S, e, "tensor_tensor", [("hs", stp - 1)], ["ts3"], out=ts_[:, 3], in0=hr_p, in1=ai_b, op=ALU.mult)
                OP(S, e, "tensor_tensor", ["ts2", "ts3"], ["ts2"], out=ts_[:, 2], in0=ts_[:, 2], in1=ts_[:, 3], op=ALU.add)
                OP(S, e, "tensor_tensor", ["ts2", ("hsr", stp)], [("hs", stp)], out=hi_n, in0=ts_[:, 2], in1=bui, op=ALU.add)
            fin_s = ph.sb([128, 2, 32, 4])
            OP(S, "pool", "tensor_copy", [("hs", 3)], ["fin_s"], out=fin_s[:], in_=hs[:, :, :, :, 3])
            DMA(S, ssm_s[:, :, j, :, :], fin_s[:], ["fin_s"], [])
            for k in range(NKC):
                sr = S_r[:, :, k]
                si = S_i[:, :, k]
                nr = S_r[:, :, k + 1]
                ni = S_i[:, :, k + 1]
                kk = ("Sk", k)
                kn = ("Sk", k + 1)
                OP(S, "dve", "tensor_tensor", [kk], ["t0"], out=tt_[:, 0], in0=sr, in1=A_r, op=ALU.mult)
                OP(S, "dve", "tensor_tensor", [kk], ["t2"], out=tt_[:, 2], in0=si, in1=A_r, op=ALU.mult)
                OP(S, "dve", "tensor_tensor", [kk], ["t1"], out=tt_[:, 1], in0=si, in1=A_i, op=ALU.mult)
                OP(S, "dve", "tensor_tensor", [kk], ["t3"], out=tt_[:, 3], in0=sr, in1=A_i, op=ALU.mult)
                OP(S, "dve", "tensor_tensor", ["t0"], [("Skr", k + 1)], out=nr, in0=nr, in1=tt_[:, 0], op=ALU.add)
                OP(S, "dve", "tensor_tensor", ["t2"], [("Ski", k + 1)], out=ni, in0=ni, in1=tt_[:, 2], op=ALU.add)
                OP(S, "dve", "tensor_tensor", ["t1", ("Skr", k + 1)], [("Skr", k + 1)], out=nr, in0=nr, in1=tt_[:, 1], op=ALU.subtract)
                OP(S, "dve", "tensor_tensor", ["t3", ("Ski", k + 1), ("Skr", k + 1)], [kn], out=ni, in0=ni, in1=tt_[:, 3], op=ALU.add)
            fin_p = ph.sb([128, 2, 32])
            OP(S, "dve", "tensor_copy", [("Sk", NKC)], ["fin_p"], out=fin_p[:, 0, :], in_=S_r[:, :, NKC])
            OP(S, "dve", "tensor_copy", [("Sk", NKC)], ["fin_p"], out=fin_p[:, 1, :], in_=S_i[:, :, NKC])
            DMA(S, ssm_p[:, :, j, :], fin_p[:], ["fin_p"], [])

        if SSTOP <= 2:
            sst_.close()
            return
        with Phase() as ph:
            S = ph.S
            cms = [ph.sb([128, 2, 4, 32]) for _ in range(2)]
            t1 = ph.sb([128, 17, 4, 32])
            t2 = ph.sb([128, 17, 4, 32])
            Er = ph.sb([128, 17, 4, 32], BF16)
            nEi = ph.sb([128, 17, 4, 32], BF16)
            bb16 = ph.sb([128, 2, 4, 32], BF16)
            Kb = ph.sb([128, 16, 128], BF16)
            Sb = ph.sb([128, 2, 4, NKC + 1], BF16)
            hb = ph.sb([128, 2, 4, 16], BF16)
            yt = [ph.sb([128, 3, NKC]) for _ in range(2)]
            y2 = [ph.sb([128, 3, NKC]) for _ in range(2)]
            y3 = [ph.sb([128, 3, NKC]) for _ in range(2)]
            OP(S, "pool", "memset", [], ["Kb"], Kb[:], 0.0)
            for m in range(NCH):
                cmt = cms[m % 2]
                DMA(S, cmt[:], cm_d[:, :, j, 4 * m:4 * m + 4, :], [], [("cm", m % 2)])
                bb = bbar(ph, m, eng="pool")
                OP(S, "pool", "tensor_copy", ["bbr", "bbi"], ["bb16"], out=bb16[:], in_=bb[:])
                OP(S, "pool", "tensor_copy", [], ["Sb"], out=Sb[:, 0], in_=S_r[:, 4 * m:4 * m + 4, :])
                OP(S, "pool", "tensor_copy", [], ["Sb"], out=Sb[:, 1], in_=S_i[:, 4 * m:4 * m + 4, :])
                OP(S, "pool", "tensor_copy", [], ["hb"], out=hb[:].rearrange("p r i (s t) -> p r i s t", t=4),
                   in_=hs[:, :, 4 * m:4 * m + 4, :, :])
                pr = pw_r[:, :, 4 * m:4 * m + 4].unsqueeze(3).broadcast_to([128, 17, 4, 32])
                pi_ = pw_i[:, :, 4 * m:4 * m + 4].unsqueeze(3).broadcast_to([128, 17, 4, 32])
                cr = cmt[:, 0].unsqueeze(1).broadcast_to([128, 17, 4, 32])
                ci = cmt[:, 1].unsqueeze(1).broadcast_to([128, 17, 4, 32])
                ck = ("cm", m % 2)
                OP(S, "dve", "tensor_tensor", [ck], ["t1"], out=t1[:], in0=pr, in1=cr, op=ALU.mult)
                OP(S, "dve", "tensor_tensor", [ck], ["t2"], out=t2[:], in0=pi_, in1=ci, op=ALU.mult)
                OP(S, "dve", "tensor_tensor", ["t1", "t2"], ["Er"], out=Er[:], in0=t1[:], in1=t2[:], op=ALU.subtract)
                OP(S, "dve", "tensor_tensor", [ck], ["t1"], out=t1[:], in0=pr, in1=ci, op=ALU.mult)
                OP(S, "dve", "tensor_tensor", [ck], ["t2"], out=t2[:], in0=pi_, in1=cr, op=ALU.mult)
                OP(S, "dve", "scalar_tensor_tensor", ["t1", "t2"], ["nEi"], out=nEi[:].rearrange("p a b c -> p (a b c)"),
                   in0=t1[:].rearrange("p a b c -> p (a b c)"), scalar=-1.0, in1=t2[:].rearrange("p a b c -> p (a b c)"),
                   op0=ALU.mult, op1=ALU.subtract)
                for jj in range(4):
                    o = banks[6][32 * jj:32 * jj + 32, :]
                    OP(S, "pe", "matmul", ["bb16", "Er"], [bk(6)], o, lhsT=bb16[:, 0, jj, :], rhs=Er[:, 0:16, jj, :],
                       start=True, stop=False, tile_position=(0, 32 * jj))
                    OP(S, "pe", "matmul", ["bb16", "nEi"], [bk(6)], o, lhsT=bb16[:, 1, jj, :], rhs=nEi[:, 0:16, jj, :],
                       start=False, stop=True, tile_position=(0, 32 * jj))
                for jj in range(4):
                    OP(S, "act", "activation", [bk(6)], ["Kb"], out=Kb[32 * jj:32 * jj + 32, :, 32 * jj:32 * jj + 32],
                       in_=banks[6][32 * jj:32 * jj + 32, :].rearrange("p (d c) -> p d c", c=32), func=AF.Copy)
                ukey = ("u16", m)
                for tq in range(16):
                    b_ = tq // 3
                    c0 = (tq % 3) * NKC
                    for tp in range(tq + 1):
                        OP(S, "pe", "matmul", ["Kb", ukey], [bk(b_)], banks[b_][:, c0:c0 + NKC], lhsT=Kb[:, tq - tp, :],
                           rhs=u16[:, m, OFF + tp:OFF + tp + TP:16], start=(tp == 0), stop=False)
                    for jj in range(4):
                        o = banks[b_][32 * jj:32 * jj + 32, c0:c0 + NKC]
                        OP(S, "pe", "matmul", ["Er", "Sb"], [bk(b_)], o, lhsT=Er[:, tq + 1, jj, :], rhs=Sb[:, 0, jj, 0:NKC],
                           start=False, stop=False, tile_position=(0, 32 * jj))
                        OP(S, "pe", "matmul", ["nEi", "Sb"], [bk(b_)], o, lhsT=nEi[:, tq + 1, jj, :], rhs=Sb[:, 1, jj, 0:NKC],
                           start=False, stop=True, tile_position=(0, 32 * jj))
                for jj in range(4):
                    o = banks[7][32 * jj:32 * jj + 32, 0:16]
                    OP(S, "pe", "matmul", ["Er", "hb"], [bk(7)], o, lhsT=Er[:, 0, jj, :], rhs=hb[:, 0, jj, :],
                       start=True, stop=False, tile_position=(0, 32 * jj))
                    OP(S, "pe", "matmul", ["nEi", "hb"], [bk(7)], o, lhsT=nEi[:, 0, jj, :], rhs=hb[:, 1, jj, :],
                       start=False, stop=True, tile_position=(0, 32 * jj))
                dsc = ssmd[:, j, m:m + 1]
                uv = u16[:, m, OFF:OFF + TP].rearrange("p (k t) -> p t k", t=16)
                groups = [(b_, 3 if b_ < 5 else 1, uv[:, 3 * b_:3 * b_ + (3 if b_ < 5 else 1), :]) for b_ in range(6)]
                groups.append((7, 0, None))
                for gi, (b_, ntq, uview) in enumerate(groups):
                    a = yt[gi % 2]
                    b2 = y2[gi % 2]
                    b3 = y3[gi % 2]
                    if uview is None:
                        av = a[:, 0, 0:16]
                        b2v = b2[:, 0, 0:16]
                        b3v = b3[:, 0, 0:16]
                        pv = banks[7][:, 0:16]
                        uview = u16[:, m, OFF + TP:OFF + T]
                    else:
                        av = a[:, 0:ntq, :]
                        b2v = b2[:, 0:ntq, :]
                        b3v = b3[:, 0:ntq, :]
                        pv = banks[b_][:, 0:ntq * NKC].rearrange("p (t k) -> p t k", k=NKC)
                    ka = ("yt", gi % 2)
                    k2 = ("y2", gi % 2)
                    k3 = ("y3", gi % 2)
                    OP(S, "dve", "scalar_tensor_tensor", [bk(b_), ukey], [ka], out=av, in0=uview, scalar=dsc, in1=pv,
                       op0=ALU.mult, op1=ALU.add)
                    OP(S, "act", "activation", [ka], [k2], out=b2v, in_=av, func=AF.Square)
                    OP(S, "dve", "tensor_scalar", [k2], [k2], out=b2v, in0=b2v, scalar1=0.044715, scalar2=1.0,
                       op0=ALU.mult, op1=ALU.add)
                    OP(S, "dve", "tensor_tensor", [k2, ka], [k3], out=b3v, in0=b2v, in1=av, op=ALU.mult)
                    OP(S, "act", "activation", [k3], [k2], out=b2v, in_=b3v, func=AF.Sigmoid, scale=1.5957691216057308)
                    OP(S, "dve", "tensor_tensor", [k2, ka], [ukey], out=uview, in0=b2v, in1=av, op=ALU.mult)

        sst_.close()
        if SSTOP <= 3:
            return
        with Phase() as ph:
            S = ph.S
            ws = WStream(ph, 128, nst=4, nbf=4, name="wg")
            sg = [ph.sb([128, 416]) for _ in range(2)]
            tg = [ph.sb([128, 416]) for _ in range(2)]
            it = 0
            gl = {}

            def gload(mo_):
                gl[mo_] = ws.load(w_glu[j, :, mo_ * 128:(mo_ + 1) * 128]) + ws.load(w_glu[j, :, D + mo_ * 128:D + (mo_ + 1) * 128])
            gload(0)
            for mo in range(NCH):
                if mo + 1 < NCH:
                    gload(mo + 1)
                w1, k1, w2, k2 = gl.pop(mo)
                for ti, (n0, nn) in enumerate(NT):
                    ba = 2 * (it % 2)
                    bb_ = ba + 1
                    for k in range(NCH):
                        OP(S, "pe", "matmul", [k1, "u16"], [bk(ba)], banks[ba][:, 0:nn], lhsT=w1[:, k, :],
                           rhs=u16[:, k, OFF + n0:OFF + n0 + nn], start=(k == 0), stop=(k == NCH - 1))
                    for k in range(NCH):
                        OP(S, "pe", "matmul", [k2, "u16"], [bk(bb_)], banks[bb_][:, 0:nn], lhsT=w2[:, k, :],
                           rhs=u16[:, k, OFF + n0:OFF + n0 + nn], start=(k == 0), stop=(k == NCH - 1))
                    OP(S, "act", "activation", [bk(bb_)], [("sg", it % 2)], out=sg[it % 2][:, 0:nn], in_=banks[bb_][:, 0:nn],
                       func=AF.Sigmoid)
                    OP(S, "dve", "tensor_tensor", [bk(ba), ("sg", it % 2)], [("tg", it % 2)], out=tg[it % 2][:, 0:nn],
                       in0=banks[ba][:, 0:nn], in1=sg[it % 2][:, 0:nn], op=ALU.mult)
                    OP(S, "dve", "tensor_tensor", [("tg", it % 2)], [("x", mo, ti)], out=x[:, mo, n0:n0 + nn],
                       in0=x[:, mo, n0:n0 + nn], in1=tg[it % 2][:, 0:nn], op=ALU.add)
                    it += 1

    def attn_layer(li):
        ast = contextlib.ExitStack()

        def sba(name, shape, dt=F32):
            return ast.enter_context(nc.sbuf_tensor(name, list(shape), dt))
        QsT = sba("QsT", [128, NCH, TS], BF16)
        KsT = sba("KsT", [128, NCH, TS], BF16)
        vs_tok = sba("vs_tok", [TS, D], BF16)
        abias = sba("abias_sb", [128, 16])
        tri = sba("tri_sb", [128, 128], BF16)
        tmk = sba("tm_sb", [128, 1088], BF16)
        QT = [(0, 416), (416, 416), (832, 416), (1248, 416), (1664, 400)]
        with Phase() as ph:
            S = ph.S
            cst = ph.sb([128, 1088])
            DMA(S, abias[:], abias_d, [], ["abias"])
            DMA(S, cst[:, 0:128], tri_d, [], ["cst"])
            OP(S, "dve", "tensor_copy", ["cst"], ["tri"], out=tri[:], in_=cst[:, 0:128])
            DMA(S, cst[:], tm_d, ["cst"], ["cst"])
            OP(S, "dve", "tensor_copy", ["cst"], ["tmk"], out=tmk[:], in_=cst[:])
            ws = WStream(ph, 128, nst=2, nbf=4, name="wq")
            wos = ph.sb([128, D])
            wob = ph.sb([128, D], BF16)
            qT = ph.sb([128, T], BF16)
            kT = ph.sb([128, T], BF16)
            OT = ph.sb([128, TP], BF16)
            vtok = ph.sb([128, 17, 128], BF16)
            kst = [ph.sb([128, 4, 128]) for _ in range(2)]
            vst = [ph.sb([128, 4, 128]) for _ in range(2)]
            eb = [ph.sb([128, 416]) for _ in range(3)]
            spb = [ph.sb([128, 416], BF16) for _ in range(3)]
            wb = [ph.sb([128, 416], BF16) for _ in range(3)]
            Ss16 = ph.sb([128, 416], BF16)
            ntri = ph.sb([128, 128], BF16)
            nones = ph.sb([128, 128], BF16)
            OP(S, "dve", "tensor_scalar", ["tri"], ["ntri"], out=ntri[:], in0=tri[:], scalar1=-1.0, scalar2=None, op0=ALU.mult)
            OP(S, "dve", "memset", [], ["nones"], nones[:], -1.0)
            zc = 0
            oc = 0
            pc = 0
            sti = 0
            for c in range(NCH):
                wq, kq = ws.load(w_qkv[:, c * 128:(c + 1) * 128])
                wk, kk = ws.load(w_qkv[:, D + c * 128:D + (c + 1) * 128])
                wv, kv = ws.load(w_qkv[:, 2 * D + c * 128:2 * D + (c + 1) * 128])
                DMA(S, wos[:], w_o[c * 128:(c + 1) * 128, :], [], ["wos"])
                OP(S, "pool", "tensor_copy", ["wos"], ["wob"], out=wob[:], in_=wos[:])
                for w_, kw_, dst, dk in ((wq, kq, qT, "qT"), (wk, kk, kT, "kT")):
                    for ti, (n0, nn) in enumerate(NT):
                        b_ = 4 + pc % 2
                        pc += 1
                        for k in range(NCH):
                            OP(S, "pe", "matmul", [kw_, "u16"], [bk(b_)], banks[b_][:, 0:nn], lhsT=w_[:, k, :],
                               rhs=u16[:, k, OFF + n0:OFF + n0 + nn], start=(k == 0), stop=(k == NCH - 1))
                        OP(S, "act", "activation", [bk(b_)], [(dk, ti)], out=dst[:, n0:n0 + nn], in_=banks[b_][:, 0:nn],
                           func=AF.Copy, scale=(0.125 if dk == "qT" else 1.0))
                OP(S, "pool", "tensor_copy", [("qT", 4)], ["QsT"], out=QsT[:, c, :], in_=qT[:, TP:T])
                OP(S, "pool", "tensor_copy", [("kT", 4)], ["KsT"], out=KsT[:, c, :], in_=kT[:, TP:T])
                for g in range(5):
                    nq4 = 4 if g < 4 else 1
                    for w_, kw_, which in ((wk, kk, "k"), (wv, kv, "v")):
                        b_ = 4 + pc % 2
                        pc += 1
                        for q in range(nq4):
                            tb = 4 * g + q
                            nt = 128 if tb < 16 else 32
                            for k in range(NCH):
                                OP(S, "pe", "matmul", [kw_, "u16"], [bk(b_)], banks[b_][0:nt, q * 128:(q + 1) * 128],
                                   lhsT=u16[:, k, OFF + tb * 128:OFF + tb * 128 + nt], rhs=w_[:, k, :],
                                   start=(k == 0), stop=(k == NCH - 1))
                        np_ = 128 if g < 4 else 32
                        stg_ = (kst if which == "k" else vst)[sti % 2]
                        sk = (which + "st", sti % 2)
                        srcv = banks[b_][0:np_, 0:nq4 * 128].rearrange("p (q n) -> p q n", n=128)
                        OP(S, "act", "activation", [bk(b_)], [sk], out=stg_[0:np_, 0:nq4, :], in_=srcv, func=AF.Copy)
                        if which == "v":
                            OP(S, "act", "activation", [bk(b_)], [("vtok", g)], out=vtok[0:np_, 4 * g:4 * g + nq4, :], in_=srcv,
                               func=AF.Copy)
                        dstd = k_p if which == "k" else v_p
                        dsts = k_s if which == "k" else v_s
                        if g < 4:
                            DMA(S, dstd[g * 512:(g + 1) * 512, c * 128:(c + 1) * 128].rearrange("(q p) n -> p q n", p=128),
                                stg_[:, 0:4, :], [sk], [])
                        else:
                            DMA(S, dstd[2048:TP, c * 128:(c + 1) * 128], stg_[0:16, 0, :], [sk], [])
                            DMA(S, dsts[:, c * 128:(c + 1) * 128], stg_[16:32, 0, :], [sk], [])
                    sti += 1
                b_ = 4 + pc % 2
                pc += 1
                for k in range(NCH):
                    OP(S, "pe", "matmul", [kv, "u16"], [bk(b_)], banks[b_][0:TS, 0:128], lhsT=u16[:, k, OFF + TP:OFF + T],
                       rhs=wv[:, k, :], start=(k == 0), stop=(k == NCH - 1))
                OP(S, "act", "activation", [bk(b_)], ["vs_tok"], out=vs_tok[:, c * 128:(c + 1) * 128], in_=banks[b_][0:TS, 0:128],
                   func=AF.Copy)
                blks = []
                for hh in range(2):
                    for qi, (q0, nq) in enumerate(QT):
                        kbl = [sb_ for sb_ in range(17) if sb_ * 128 < q0 + nq]
                        bo = 6 + oc % 2
                        oc += 1
                        for bi_, sb_ in enumerate(reversed(kbl)):
                            s0 = sb_ * 128
                            ns = min(128, TP - s0)
                            blks.append(dict(hh=hh, qi=qi, q0=q0, nq=nq, sb=sb_, s0=s0, ns=ns, bo=bo, first=(bi_ == 0),
                                             last=(bi_ == len(kbl) - 1), mask=(s0 + ns - 1 >= q0)))
                nb = len(blks)

                def stageA(t):
                    B = blks[t]
                    r0 = 64 * B["hh"]
                    h = 2 * c + B["hh"]
                    ns, nq, q0, s0 = B["ns"], B["nq"], B["q0"], B["s0"]
                    bz = (zc0 + t) % 4
                    p3 = (zc0 + t) % 3
                    OP(S, "pe", "matmul", [("kT", t_) for t_ in range(5)] + [("qT", B["qi"])], [bk(bz)], banks[bz][0:ns, 0:nq],
                       lhsT=kT[r0:r0 + 64, s0:s0 + ns], rhs=qT[r0:r0 + 64, q0:q0 + nq], start=True, stop=True)
                    OP(S, "act", "activation", [bk(bz), "abias"], [("e", p3)], out=eb[p3][0:ns, 0:nq], in_=banks[bz][0:ns, 0:nq],
                       func=AF.Exp, scale=1.0, bias=abias[0:ns, h:h + 1])
                    OP(S, "act", "activation", [("e", p3)], [("sp", p3)], out=spb[p3][0:ns, 0:nq], in_=eb[p3][0:ns, 0:nq],
                       func=AF.Ln, bias=1.0, scale=1.0)
                    if B["mask"]:
                        mo_ = 544 - (s0 - q0)
                        OP(S, "dve", "tensor_tensor", [("sp", p3), "tmk"], [("sp", p3)], out=spb[p3][0:ns, 0:nq],
                           in0=spb[p3][0:ns, 0:nq], in1=tmk[0:ns, mo_:mo_ + nq], op=ALU.mult)

                def stageB(t):
                    B = blks[t]
                    h = 2 * c + B["hh"]
                    ns, nq, q0, s0 = B["ns"], B["nq"], B["q0"], B["s0"]
                    bz = (zc0 + t) % 4
                    p3 = (zc0 + t) % 3
                    OP(S, "pe", "matmul", ["ntri", ("sp", p3)], [bk(bz)], banks[bz][0:ns, 0:nq], lhsT=ntri[0:ns, 0:ns],
                       rhs=spb[p3][0:ns, 0:nq], start=False, stop=B["first"], skip_group_check=True)
                    if not B["first"]:
                        OP(S, "pe", "matmul", ["nones", "Ss16"], [bk(bz)], banks[bz][0:ns, 0:nq], lhsT=nones[:, 0:ns],
                           rhs=Ss16[:, 0:nq], start=False, stop=True, skip_group_check=True)
                    OP(S, "act", "activation", [bk(bz), "abias"], [("w", p3)], out=wb[p3][0:ns, 0:nq], in_=banks[bz][0:ns, 0:nq],
                       func=AF.Exp, scale=1.0, bias=abias[0:ns, h:h + 1])
                    if B["mask"]:
                        mo_ = 544 - (s0 - q0)
                        OP(S, "dve", "tensor_tensor", [("w", p3), "tmk"], [("w", p3)], out=wb[p3][0:ns, 0:nq],
                           in0=wb[p3][0:ns, 0:nq], in1=tmk[0:ns, mo_:mo_ + nq], op=ALU.mult)
                    if not B["last"]:
                        if B["first"]:
                            if ns < 128:
                                OP(S, "dve", "memset", [], ["Ss16"], Ss16[:], 0.0)
                            OP(S, "dve", "tensor_copy", [("sp", p3)], ["Ss16"], out=Ss16[0:ns, 0:nq], in_=spb[p3][0:ns, 0:nq])
                        else:
                            OP(S, "dve", "tensor_tensor", [("sp", p3), "Ss16"], ["Ss16"], out=Ss16[0:ns, 0:nq],
                               in0=Ss16[0:ns, 0:nq], in1=spb[p3][0:ns, 0:nq], op=ALU.add)

                def stageC(t):
                    B = blks[t]
                    r0 = 64 * B["hh"]
                    ns, nq, q0 = B["ns"], B["nq"], B["q0"]
                    p3 = (zc0 + t) % 3
                    bo = B["bo"]
                    OP(S, "pe", "matmul", [("vtok", B["sb"] // 4), ("w", p3)], [bk(bo)], banks[bo][r0:r0 + 64, 0:nq],
                       lhsT=vtok[0:ns, B["sb"], r0:r0 + 64], rhs=wb[p3][0:ns, 0:nq], start=B["first"], stop=B["last"],
                       tile_position=(0, r0))
                    if B["last"]:
                        OP(S, "act", "activation", [bk(bo)], [("OT", B["qi"])], out=OT[r0:r0 + 64, q0:q0 + nq],
                           in_=banks[bo][r0:r0 + 64, 0:nq], func=AF.Copy)
                zc0 = zc
                for t in range(nb + 2):
                    if t < nb:
                        stageA(t)
                    if 1 <= t <= nb:
                        stageB(t - 1)
                    if t >= 2:
                        stageC(t - 2)
                zc += nb
                for mo in range(NCH):
                    for qi, (q0, nq) in enumerate(QT):
                        b_ = 4 + pc % 2
                        pc += 1
                        OP(S, "pe", "matmul", ["wob", ("OT", qi)], [bk(b_)], banks[b_][:, 0:nq], lhsT=wob[:, mo * 128:(mo + 1) * 128],
                           rhs=OT[:, q0:q0 + nq], start=True, stop=True)
                        OP(S, "dve", "tensor_tensor", [bk(b_)], [("x", mo, qi)], out=x[:, mo, q0:q0 + nq], in0=banks[b_][:, 0:nq],
                           in1=x[:, mo, q0:q0 + nq], op=ALU.add)
        if ASTOP >= 1:
            attn_sample(QsT, KsT, vs_tok, abias, tri)
        ast.close()

    def attn_sample(QsT, KsT, vs_tok, abias, tri):
        with Phase() as ph:
            S = ph.S
            ptab = ph.sb([128, 256], I32)
            ptf = ph.sb([128, 256])
            idx = ph.sb([128, 256], I32)
            iot = ph.sb([128, 1])
            mcs = ph.sb([TS, 4, 64])
            mcur = ph.sb([TS, 4, 64], BF16)
            bt = ph.sb([128, 16, 4])
            Qblk = ph.sb([128, NCH, 64], BF16)
            kpg = [ph.sb([128, D]) for _ in range(3)]
            vpg = [ph.sb([128, D]) for _ in range(3)]
            KT = [ph.sb([128, NCH, 128], BF16) for _ in range(3)]
            Vb = [ph.sb([128, D], BF16) for _ in range(3)]
            zb = [ph.sb([128, 64]) for _ in range(3)]
            eb = [ph.sb([128, 64]) for _ in range(3)]
            gb = [ph.sb([128, 64]) for _ in range(3)]
            spb = [ph.sb([128, 64], BF16) for _ in range(3)]
            wb = [ph.sb([128, 64], BF16) for _ in range(3)]
            Ss32 = ph.sb([128, 64])
            Ss16 = ph.sb([128, 64], BF16)
            OsT = ph.sb([128, NCH, TS], BF16)
            wos = [ph.sb([128, D]) for _ in range(2)]
            wob = ph.sb([128, NCH, D], BF16)
            DMA(S, ptab[:], ptab_d, [], ["ptab"])
            DMA(S, iot[:], iota_d, [], ["iot"])
            DMA(S, mcs[:], mcur_d, [], ["mcs"])
            OP(S, "dve", "tensor_copy", ["mcs"], ["mcur"], out=mcur[:], in_=mcs[:])
            OP(S, "dve", "tensor_copy", ["ptab"], ["ptf"], out=ptf[:], in_=ptab[:])
            OP(S, "dve", "tensor_scalar", ["ptf", "iot"], ["ptf"], out=ptf[:], in0=ptf[:], scalar1=128.0, scalar2=iot[:, 0:1],
               op0=ALU.mult, op1=ALU.add)
            OP(S, "dve", "tensor_copy", ["ptf"], ["idx"], out=idx[:], in_=ptf[:])
            OP(S, "dve", "tensor_copy", ["abias"], ["bt"], out=bt[:], in_=abias[:].unsqueeze(2).broadcast_to([128, 16, 4]))
            btv = bt[:].rearrange("p h q -> p (h q)")
            pages = []
            for sq_ in range(4):
                plist = [-1] + list(range(63, -1, -1))
                for pi_, pg in enumerate(plist):
                    pages.append(dict(sq=sq_, pg=pg, first=(pi_ == 0), last=(pi_ == len(plist) - 1), ns=(TS if pg < 0 else 128)))
            npg = len(pages)

            def stA(t):
                Pg = pages[t]
                sq_, pg, ns = Pg["sq"], Pg["pg"], Pg["ns"]
                pz = t % 2
                p3 = t % 3
                bz = pz
                if Pg["first"]:
                    OP(S, "pool", "memset", [], ["Qblk"], Qblk[:], 0.0)
                    for c in range(NCH):
                        for hh in range(2):
                            h = 2 * c + hh
                            OP(S, "pool", "tensor_copy", ["Qblk"], ["Qblk"], out=Qblk[64 * hh:64 * hh + 64, c, 4 * h:4 * h + 4],
                               in_=QsT[64 * hh:64 * hh + 64, c, 4 * sq_:4 * sq_ + 4])
                if pg < 0:
                    for c in range(NCH):
                        OP(S, "pe", "matmul", ["Qblk"], [bk(bz)], banks[bz][0:ns, 0:64], lhsT=KsT[:, c, :], rhs=Qblk[:, c, :],
                           start=(c == 0), stop=(c == NCH - 1))
                else:
                    col = sq_ * 64 + pg
                    kp = kpg[p3]
                    vp = vpg[p3]
                    S.dma(lambda h_, kp=kp, col=col: h_.indirect_dma_start(
                        out=kp[:, :], out_offset=None, in_=cache_k,
                        in_offset=bass.IndirectOffsetOnAxis(ap=idx[:, col:col + 1], axis=0)), ["idx"], [("kpg", p3)], q="pool")
                    S.dma(lambda h_, vp=vp, col=col: h_.indirect_dma_start(
                        out=vp[:, :], out_offset=None, in_=cache_v,
                        in_offset=bass.IndirectOffsetOnAxis(ap=idx[:, col:col + 1], axis=0)), ["idx"], [("vpg", p3)], q="pool")
                    OP(S, "dve", "tensor_copy", [("vpg", p3)], [("Vb", p3)], out=Vb[p3][:], in_=vp[:])
                    for g in range(2):
                        b_ = 4 + g
                        for q in range(4):
                            c = 4 * g + q
                            OP(S, "pe", "transpose", [("kpg", p3), "idt"], [bk(b_)], out=banks[b_][:, q * 128:(q + 1) * 128],
                               in_=kp[:, c * 128:(c + 1) * 128], identity=idt[:])
                        OP(S, "act", "activation", [bk(b_)], [("KT", p3, g)],
                           out=KT[p3][:, 4 * g:4 * g + 4, :].rearrange("p c s -> p (c s)"), in_=banks[b_][:], func=AF.Copy)
                    for c in range(NCH):
                        OP(S, "pe", "matmul", ["Qblk", ("KT", p3, c // 4)], [bk(bz)], banks[bz][0:ns, 0:64], lhsT=KT[p3][:, c, :],
                           rhs=Qblk[:, c, :], start=(c == 0), stop=(c == NCH - 1))
                OP(S, "dve", "scalar_tensor_tensor", [bk(bz), "bt"], [("z", p3)], out=zb[p3][0:ns, :], in0=banks[bz][0:ns, 0:64],
                   scalar=1.0, in1=btv[0:ns, :], op0=ALU.mult, op1=ALU.add)
                OP(S, "act", "activation", [("z", p3)], [("e", p3)], out=eb[p3][0:ns, :], in_=zb[p3][0:ns, :], func=AF.Exp)
                OP(S, "act", "activation", [("e", p3)], [("sp", p3)], out=spb[p3][0:ns, :], in_=eb[p3][0:ns, :], func=AF.Ln,
                   bias=1.0, scale=1.0)
                if pg < 0:
                    OP(S, "dve", "tensor_tensor", [("sp", p3), "mcur"], [("sp", p3)], out=spb[p3][0:ns, :], in0=spb[p3][0:ns, :],
                       in1=mcur[:, sq_, :], op=ALU.mult)

            def stB(t):
                Pg = pages[t]
                sq_, pg, ns = Pg["sq"], Pg["pg"], Pg["ns"]
                pz = t % 2
                p3 = t % 3
                bc = 2 + pz
                first = Pg["first"]
                OP(S, "pe", "matmul", ["tri", ("sp", p3)], [bk(bc)], banks[bc][0:ns, 0:64], lhsT=tri[0:ns, 0:ns],
                   rhs=spb[p3][0:ns, :], start=True, stop=first)
                if not first:
                    OP(S, "pe", "matmul", ["ones_b", "Ss16"], [bk(bc)], banks[bc][0:ns, 0:64], lhsT=ones_b[:, 0:ns],
                       rhs=Ss16[:, :], start=False, stop=True)
                OP(S, "act", "activation", [bk(bc)], [("g", p3)], out=gb[p3][0:ns, :], in_=banks[bc][0:ns, 0:64], func=AF.Exp,
                   scale=-1.0)
                OP(S, "dve", "tensor_tensor", [("e", p3), ("g", p3)], [("w", p3)], out=wb[p3][0:ns, :], in0=eb[p3][0:ns, :],
                   in1=gb[p3][0:ns, :], op=ALU.mult)
                if pg < 0:
                    OP(S, "dve", "tensor_tensor", [("w", p3), "mcur"], [("w", p3)], out=wb[p3][0:ns, :], in0=wb[p3][0:ns, :],
                       in1=mcur[:, sq_, :], op=ALU.mult)
                if not Pg["last"]:
                    if first:
                        OP(S, "dve", "memset", [], ["Ss32"], Ss32[:], 0.0)
                    OP(S, "dve", "tensor_tensor", [("sp", p3), "Ss32"], ["Ss32"], out=Ss32[0:ns, :], in0=Ss32[0:ns, :],
                       in1=spb[p3][0:ns, :], op=ALU.add)
                    OP(S, "dve", "tensor_copy", ["Ss32"], ["Ss16"], out=Ss16[:], in_=Ss32[:])

            def stC(t):
                Pg = pages[t]
                sq_, pg, ns = Pg["sq"], Pg["pg"], Pg["ns"]
                p3 = t % 3
                bo = 6 + sq_ % 2
                for c in range(NCH):
                    if pg < 0:
                        lh = vs_tok[0:TS, c * 128:(c + 1) * 128]
                        vkey = "vs_tok"
                    else:
                        lh = Vb[p3][:, c * 128:(c + 1) * 128]
                        vkey = ("Vb", p3)
                    OP(S, "pe", "matmul", [vkey, ("w", p3)], [bk(bo)], banks[bo][:, c * 64:(c + 1) * 64], lhsT=lh,
                       rhs=wb[p3][0:ns, :], start=(Pg["first"] and c == 0), stop=(Pg["last"] and c == NCH - 1),
                       skip_group_check=True)
                if Pg["last"]:
                    for c in range(NCH):
                        for hh in range(2):
                            h = 2 * c + hh
                            OP(S, "act", "activation", [bk(bo)], ["OsT"], out=OsT[64 * hh:64 * hh + 64, c, 4 * sq_:4 * sq_ + 4],
                               in_=banks[bo][64 * hh:64 * hh + 64, c * 64 + 4 * h:c * 64 + 4 * h + 4], func=AF.Copy)
            for t in range(npg + 2):
                if t < npg:
                    stA(t)
                if 1 <= t <= npg:
                    stB(t - 1)
                if t >= 2:
                    stC(t - 2)
            for c in range(NCH):
                DMA(S, wos[c % 2][:], w_o[c * 128:(c + 1) * 128, :], [], [("wos", c % 2)])
                OP(S, "pool", "tensor_copy", [("wos", c % 2)], [("wob", c)], out=wob[:, c, :], in_=wos[c % 2][:])
            for mo in range(NCH):
                b_ = 4 + mo % 2
                for c in range(NCH):
                    OP(S, "pe", "matmul", [("wob", c), "OsT"], [bk(b_)], banks[b_][:, 0:TS], lhsT=wob[:, c, mo * 128:(mo + 1) * 128],
                       rhs=OsT[:, c, :], start=(c == 0), stop=(c == NCH - 1))
                OP(S, "dve", "tensor_tensor", [bk(b_)], [("xs", mo)], out=x[:, mo, TP:T], in0=banks[b_][:, 0:TS], in1=x[:, mo, TP:T],
                   op=ALU.add)

    def pool_layer(li):
        with Phase() as ph:
            S = ph.S
            rstd = norm_stats(ph)
            rk = [("rstd", ti) for ti in range(5)]
            E0 = ph.sb([128, 15 + TP])
            EA = ph.sb([128, 15 + TP])
            EB = ph.sb([128, 15 + TP])
            X0 = ph.sb([128, 4, 19])
            XA = ph.sb([128, 4, 19])
            XB = ph.sb([128, 4, 19])
            t16 = ph.sb([128, 16])
            pws = ph.sb([128, 4, 2, 256])
            pwb = ph.sb([128, 4, 2, 256], BF16)
            DMA(S, pws[:], pool_w.rearrange("g (ci p) e -> p g ci e", p=128), [], ["pws"])
            OP(S, "pool", "tensor_copy", ["pws"], ["pwb"], out=pwb[:], in_=pws[:])
            OP(S, "pool", "memset", [], ["E0"], E0[:, 0:15], 0.0)
            OP(S, "pool", "memset", [], ["EA"], EA[:, 0:15], 0.0)
            OP(S, "pool", "memset", [], ["EB"], EB[:, 0:15], 0.0)
            OP(S, "pool", "memset", [], ["XA"], XA[:], 0.0)
            OP(S, "pool", "memset", [], ["XB"], XB[:], 0.0)
            for c in range(NCH):
                gi = c // 2
                w = 2 << gi
                eng = "dve" if c % 2 == 0 else "pool"
                OP(S, "dve", "scalar_tensor_tensor", ["x"] + rk + ["E0"], ["E0"], out=E0[:, 15:15 + TP], in0=x[:, c, 0:TP],
                   scalar=gains[:, li, c:c + 1], in1=rstd[:, 0:TP], op0=ALU.mult, op1=ALU.mult)
                OP(S, "dve", "tensor_copy", ["spool"], ["X0"], out=X0[:, :, 0:15], in_=spool[:, c, :, :])
                OP(S, "dve", "scalar_tensor_tensor", ["x"] + rk + ["X0"], ["X0"], out=X0[:, :, 15:19],
                   in0=x[:, c, TP:T].rearrange("p (s t) -> p s t", t=4), scalar=gains[:, li, c:c + 1],
                   in1=rstd[:, TP:T].rearrange("p (s t) -> p s t", t=4), op0=ALU.mult, op1=ALU.mult)
                DMA(S, pool_p[:, c, :], E0[:, TP:TP + 15], ["E0"], [])
                DMA(S, pool_s[:, c, :, :], X0[:, :, 4:19], ["X0"], [])
                src, srck, xsrc, xsrck = E0, "E0", X0, "X0"
                bufs = [(EA, "EA", XA, "XA"), (EB, "EB", XB, "XB")]
                stp = 1
                bi = 0
                while stp < w:
                    dst, dstk, xdst, xdstk = bufs[bi % 2]
                    bi += 1
                    OP(S, eng, "tensor_tensor", [srck], [dstk], out=dst[:, stp:15 + TP], in0=src[:, stp:15 + TP],
                       in1=src[:, 0:15 + TP - stp], op=ALU.add)
                    OP(S, eng, "tensor_tensor", [xsrck], [xdstk], out=xdst[:, :, stp:19], in0=xsrc[:, :, stp:19],
                       in1=xsrc[:, :, 0:19 - stp], op=ALU.add)
                    src, srck, xsrc, xsrck = dst, dstk, xdst, xdstk
                    stp *= 2
                uk = ("u16", c)
                OP(S, "dve", "scalar_tensor_tensor", [srck, "E0"], [uk], out=u16[:, c, OFF:OFF + TP], in0=src[:, 15:15 + TP],
                   scalar=1.0 / w, in1=E0[:, 15:15 + TP], op0=ALU.mult, op1=ALU.subtract)
                OP(S, "dve", "tensor_tensor", [srck, "rcnt"], ["t16"], out=t16[:], in0=src[:, 15:31], in1=rcnt[:, gi, :], op=ALU.mult)
                OP(S, "dve", "tensor_tensor", ["t16", "E0"], [uk], out=u16[:, c, OFF:OFF + 16], in0=t16[:], in1=E0[:, 15:31],
                   op=ALU.subtract)
                OP(S, "dve", "scalar_tensor_tensor", [xsrck, "X0"], [uk],
                   out=u16[:, c, OFF + TP:OFF + T].rearrange("p (s t) -> p s t", t=4), in0=xsrc[:, :, 15:19],
                   scalar=1.0 / w, in1=X0[:, :, 15:19], op0=ALU.mult, op1=ALU.subtract)
            it = 0
            for gi in range(4):
                for co in range(2):
                    mo = 2 * gi + co
                    for ti, (n0, nn) in enumerate(NT):
                        b_ = it % 2
                        it += 1
                        for ci in range(2):
                            OP(S, "pe", "matmul", ["pwb", ("u16", 2 * gi + ci)], [bk(b_)], banks[b_][:, 0:nn],
                               lhsT=pwb[:, gi, ci, co * 128:(co + 1) * 128], rhs=u16[:, 2 * gi + ci, OFF + n0:OFF + n0 + nn],
                               start=(ci == 0), stop=(ci == 1))
                        OP(S, "dve", "scalar_tensor_tensor", [bk(b_), "pscale"], [("x", mo, ti)], out=x[:, mo, n0:n0 + nn],
                           in0=banks[b_][:, 0:nn], scalar=pscale[:, mo:mo + 1], in1=x[:, mo, n0:n0 + nn],
                           op0=ALU.mult, op1=ALU.add)

    def ffn_layer(li):
        groups = [list(range(0, 6)), list(range(6, 12)), list(range(12, 17)), list(range(17, 22))]
        with Phase() as ph:
            S = ph.S
            ws = WStream(ph, 128, nst=2, nbf=4, name="wu")
            wds = [ph.sb([128, D]) for _ in range(2)]
            wdb = [ph.sb([128, D], BF16) for _ in range(12)]
            a_g = [ph.sb([128, T], BF16) for _ in range(6)]
            tbuf = [[ph.sb([128, 416]) for _ in range(3)] for _ in range(2)]
            sgb = [ph.sb([128, 416]) for _ in range(3)]
            hse = [ph.sb([128, 4, 6]) for _ in range(2)]
            hlast = ph.sb([128, 44, 10])
            it = 0
            wdi = 0
            flat = [(gi_, cl, cc) for gi_, grp in enumerate(groups) for cl, cc in enumerate(grp)]
            loaded = {}

            def load_chunk(fi):
                gi_, cl, cc = flat[fi]
                wg_, kg = ws.load(w_up[li, :, cc * 128:(cc + 1) * 128])
                wv_, kv = ws.load(w_up[li, :, DFF + cc * 128:DFF + (cc + 1) * 128])
                sgi = fi % 2
                wslot = (gi_ % 2) * 6 + cl
                DMA(S, wds[sgi][:], w_down[li, cc * 128:(cc + 1) * 128, :], [], [("wds", sgi)])
                OP(S, "act", "activation", [("wds", sgi)], [("wdb", wslot)], out=wdb[wslot][:], in_=wds[sgi][:], func=AF.Copy)
                loaded[fi] = (wg_, kg, wv_, kv)
            load_chunk(0)
            fi = -1
            for gi_, grp in enumerate(groups):
                for cl, cc in enumerate(grp):
                    fi += 1
                    if fi + 1 < len(flat):
                        load_chunk(fi + 1)
                    wg_, kg, wv_, kv = loaded.pop(fi)
                    for ti, (n0, nn) in enumerate(NT):
                        par_ = it % 3
                        it += 1
                        for gv, (w_, kw_, chn) in enumerate(((wg_, kg, cc), (wv_, kv, 22 + cc))):
                            b_ = 2 * par_ + gv
                            tb__ = tbuf[gv][par_]
                            kh = ("h", gv, par_)
                            kt = ("t", gv, par_)
                            for k in range(NCH):
                                OP(S, "pe", "matmul", [kw_, "u16"], [bk(b_)], banks[b_][:, 0:nn + 2], lhsT=w_[:, k, :],
                                   rhs=u16[:, k, n0:n0 + nn + 2], start=(k == 0), stop=(k == NCH - 1))
                            OP(S, "act", "activation", [bk(b_)], [kt], out=tb__[:, 0:nn], in_=banks[b_][:, 0:nn],
                               func=AF.Identity, scale=cw[:, li, 0, chn:chn + 1], bias=cb[:, li, chn:chn + 1])
                            OP(S, "dve", "scalar_tensor_tensor", [bk(b_), kt], [kt], out=tb__[:, 0:nn], in0=banks[b_][:, 1:nn + 1],
                               scalar=cw[:, li, 1, chn:chn + 1], in1=tb__[:, 0:nn], op0=ALU.mult, op1=ALU.add)
                            OP(S, "dve", "scalar_tensor_tensor", [bk(b_), kt], [kt], out=tb__[:, 0:nn], in0=banks[b_][:, 2:nn + 2],
                               scalar=cw[:, li, 2, chn:chn + 1], in1=tb__[:, 0:nn], op0=ALU.mult, op1=ALU.add)
                            if ti == 4:
                                he = hse[gv]
                                ke = ("hse", gv)
                                OP(S, "dve", "tensor_copy", [], [ke], out=he[:, :, 0:2],
                                   in_=cbuf[:, li, chn, :].rearrange("p (s r) -> p s r", r=2))
                                OP(S, "dve", "tensor_copy", [bk(b_)], [ke], out=he[:, :, 2:6],
                                   in_=banks[b_][:, 402:418].rearrange("p (s t) -> p s t", t=4))
                                OP(S, "dve", "tensor_copy", [bk(b_)], ["hlast"], out=hlast[:, chn, 0:2], in_=banks[b_][:, 400:402])
                                tv = tb__[:, 400:416].rearrange("p (s t) -> p s t", t=4)
                                OP(S, "dve", "tensor_scalar", [ke, kt], [kt], out=tv, in0=he[:, :, 0:4],
                                   scalar1=cw[:, li, 0, chn:chn + 1], scalar2=cb[:, li, chn:chn + 1], op0=ALU.mult, op1=ALU.add)
                                OP(S, "dve", "scalar_tensor_tensor", [ke, kt], [kt], out=tv, in0=he[:, :, 1:5],
                                   scalar=cw[:, li, 1, chn:chn + 1], in1=tv, op0=ALU.mult, op1=ALU.add)
                                OP(S, "dve", "scalar_tensor_tensor", [ke, kt], [kt], out=tv, in0=he[:, :, 2:6],
                                   scalar=cw[:, li, 2, chn:chn + 1], in1=tv, op0=ALU.mult, op1=ALU.add)
                                OP(S, "pool", "tensor_copy", [ke], ["hlast"],
                                   out=hlast[:, chn, 2:10].rearrange("p (s r) -> p s r", r=2), in_=he[:, :, 4:6])
                        ksg = ("sg", par_)
                        OP(S, "act", "activation", [("t", 0, par_)], [ksg], out=sgb[par_][:, 0:nn], in_=tbuf[0][par_][:, 0:nn],
                           func=AF.Silu)
                        OP(S, "pool", "tensor_tensor", [ksg, ("t", 1, par_)], [("a", cl, ti)], out=a_g[cl][:, n0:n0 + nn],
                           in0=sgb[par_][:, 0:nn], in1=tbuf[1][par_][:, 0:nn], op=ALU.mult)
                dit = 0
                for mo in range(NCH):
                    for ti, (n0, nn) in enumerate(NT):
                        b_ = 6 + dit % 2
                        dit += 1
                        for cl in range(len(grp)):
                            wsl = (gi_ % 2) * 6 + cl
                            OP(S, "pe", "matmul", [("wdb", wsl), ("a", cl, ti)], [bk(b_)], banks[b_][:, 0:nn],
                               lhsT=wdb[wsl][:, mo * 128:(mo + 1) * 128], rhs=a_g[cl][:, n0:n0 + nn],
                               start=(cl == 0), stop=(cl == len(grp) - 1))
                        OP(S, "dve", "tensor_tensor", [bk(b_)], [("x", mo, ti)], out=x[:, mo, n0:n0 + nn],
                           in0=banks[b_][:, 0:nn], in1=x[:, mo, n0:n0 + nn], op=ALU.add)
            DMA(S, conv_o[:, li, :, :], hlast[:], ["hlast"], [])

    def final_out():
        with Phase() as ph:
            S = ph.S
            rstd = norm_stats(ph)
            rk = [("rstd", ti) for ti in range(5)]
            for c in range(NCH):
                OP(S, "dve", "scalar_tensor_tensor", ["x"] + rk, [("xo", c)], out=x[:, c, :], in0=x[:, c, :],
                   scalar=gains[:, 8, c:c + 1], in1=rstd[:, :], op0=ALU.mult, op1=ALU.mult)
        with Phase() as ph:
            S = ph.S
            yst = [ph.sb([128, D]) for _ in range(2)]
            oblocks = [(y_p[b * 128:(b + 1) * 128, :], 128, 16 + b * 128) for b in range(16)]
            oblocks.append((y_s, 16, TP))
            for bi, (dst, nr, c0) in enumerate(oblocks):
                yt_ = yst[bi % 2]
                for g in range(2):
                    b_ = (bi * 2 + g) % 4
                    for jj in range(4):
                        c = g * 4 + jj
                        OP(S, "pe", "transpose", ["idt"], [bk(b_)], out=banks[b_][0:nr, jj * 128:(jj + 1) * 128],
                           in_=x[:, c, c0:c0 + nr], identity=idt[:])
                    if g == 0:
                        OP(S, "act", "activation", [bk(b_)], [("yst", bi % 2, g)], out=yt_[0:nr, g * 512:(g + 1) * 512],
                           in_=banks[b_][0:nr, :], func=AF.Copy)
                    else:
                        OP(S, "dve", "tensor_copy", [bk(b_)], [("yst", bi % 2, g)], out=yt_[0:nr, g * 512:(g + 1) * 512],
                           in_=banks[b_][0:nr, :])
                DMA(S, dst, yt_[0:nr, :], [("yst", bi % 2, 0), ("yst", bi % 2, 1)], [])

    for li in range(DEPTH):
        if li * 10 > STOP:
            break
        if li % 3 == 0:
            norm_to_u16(li)
            ssm_layer(li // 3, li)
        elif li % 3 == 1:
            pool_layer(li)
        else:
            norm_to_u16(li)
            attn_layer(li)
        if li * 10 + 0 >= STOP:
            break
        norm_to_u16(4 + li)
        ffn_layer(li)
        if li * 10 + 1 >= STOP:
            break
    final_out()
    pst.close()
    return nc


_NC = None


def prep_core(inp, c):
    f = np.float32

    def fm(v):
        v = np.asarray(v, f)
        lead = v.shape[:-1]
        return np.ascontiguousarray(np.moveaxis(v.reshape(lead + (NCH, 128)), -1, 0))
    m = {}
    m["xp"] = np.ascontiguousarray(inp["x_prompt"][c])
    m["xs"] = np.ascontiguousarray(inp["x_sample"][4 * c:4 * c + 4].reshape(TS, D))
    m["meta"] = np.ascontiguousarray(inp["meta_tokens"])
    m["ident"] = np.eye(128, dtype=f)
    g = np.concatenate([inp["norm_mix_g"], inp["norm_ffn_g"], inp["norm_final_g"][None]], 0)
    m["gains"] = fm(g)
    cwv = inp["ffn_conv_w"].reshape(DEPTH, 3, 44, 128)
    m["cw"] = np.ascontiguousarray(cwv.transpose(3, 0, 1, 2))
    m["cb"] = np.ascontiguousarray(inp["ffn_conv_b"].reshape(DEPTH, 44, 128).transpose(2, 0, 1))
    cbv = inp["state_ffn_conv"][:, 4 * c:4 * c + 4]
    m["cbuf"] = np.ascontiguousarray(cbv.reshape(DEPTH, 8, 44, 128).transpose(3, 0, 2, 1))
    m["ssmd"] = fm(inp["ssm_d"])
    lam = np.stack([inp["ssm_lambda_re"], inp["ssm_lambda_im"],
                    np.broadcast_to(inp["ssm_log_dt"][:, :, None], (2, 64, 64))], 0)
    m["par"] = np.ascontiguousarray(lam.reshape(3, 2, 32, 2, 64).transpose(3, 4, 0, 1, 2).reshape(128, 3, 2, 32))
    st = np.stack([inp["state_ssm_re"][:, 4 * c:4 * c + 4], inp["state_ssm_im"][:, 4 * c:4 * c + 4]], 0)
    m["sst"] = np.ascontiguousarray(st.reshape(2, 2, 4, 32, 2, 64).transpose(4, 5, 0, 1, 3, 2).reshape(128, 2, 2, 32, 4))
    B = np.stack([inp["ssm_b_re"], inp["ssm_b_im"]], 0).reshape(2, 2, 32, 2, 64, 16)
    bm = np.zeros((2, 64, 2, 2, 32, 2, 16), f)
    for g2 in range(2):
        bm[g2, :, :, :, :, g2, :] = B[:, :, :, g2].transpose(3, 0, 1, 2, 4)
    m["bm"] = bm.reshape(128, 2, 2, 32, 32)
    C = np.stack([inp["ssm_c_re"], inp["ssm_c_im"]], 0).reshape(2, 2, 32, 2, 16, 64)
    cm = np.zeros((2, 64, 2, 2, 32, 2, 16), f)
    for g2 in range(2):
        cm[g2, :, :, :, :, g2, :] = C[:, :, :, g2].transpose(4, 0, 1, 2, 3)
    m["cm"] = cm.reshape(128, 2, 2, 32, 32)
    rm = np.zeros((128, 4), f)
    for jj in range(4):
        rm[32 * jj:32 * jj + 32, jj] = 1.0
    m["rowmask"] = rm
    sp_ = inp["state_pool"][0, 4 * c:4 * c + 4]
    m["spool"] = np.ascontiguousarray(sp_.reshape(4, 15, NCH, 128).transpose(3, 2, 0, 1))
    m["pscale"] = fm(inp["pool_scale"][0])
    rc = np.zeros((128, 4, 16), f)
    for gi in range(4):
        for t in range(16):
            rc[:, gi, t] = 1.0 / min(2 << gi, t + 1)
    m["rcnt"] = rc
    m["pool_w"] = inp["pool_w"][0]
    m["w_qkv"] = inp["attn_w_qkv"][0]
    m["w_o"] = inp["attn_w_o"][0]
    m["abias"] = np.ascontiguousarray(np.broadcast_to(inp["attn_logit_bias"][0][None, :], (128, 16))).astype(f)
    pidx = np.arange(128)
    m["tri"] = (pidx[:, None] >= pidx[None, :]).astype(f)
    jx = np.arange(1088)
    m["tm"] = ((jx[None, :] - 544) > pidx[:, None]).astype(f)
    mc = np.zeros((16, 4, 16, 4), f)
    for sq_ in range(4):
        for r in range(4):
            for q_ in range(4):
                if r < q_:
                    mc[4 * sq_ + r, sq_, :, q_] = 1.0
    m["mcur"] = mc.reshape(16, 4, 64)
    pt = inp["page_table"][4 * c:4 * c + 4].reshape(1, 256).astype(np.int32)
    m["ptab"] = np.ascontiguousarray(np.broadcast_to(pt, (128, 256)))
    m["iota"] = np.arange(128, dtype=f).reshape(128, 1)
    if "cache_k" in inp:
        m["cache_k"] = inp["cache_k"].reshape(2560 * 128, D)
        m["cache_v"] = inp["cache_v"].reshape(2560 * 128, D)
    m["w_glu"] = inp["ssm_w_glu"]
    m["w_up"] = inp["ffn_w_up"]
    m["w_down"] = inp["ffn_w_down"]
    return m


def kernel(**inp):
    global _NC
    inp = {k: np.asarray(v) for k, v in inp.items()}
    if _NC is None:
        _NC = build_nc()
    nc = _NC
    in_maps = [prep_core(inp, c) for c in range(NCORES)]
    res = run_bass_kernel_spmd(nc, in_maps, core_ids=list(range(NCORES)))
    R = res.results
    y_prompt = np.stack([R[c]["y_p"] for c in range(NCORES)])
    y_sample = np.stack([R[c]["y_s"].reshape(4, 4, D) for c in range(NCORES)]).reshape(32, 4, D)
    sp = np.stack([R[c]["ssm_p"] for c in range(NCORES)])
    sp = sp.reshape(NCORES, 2, 64, 2, 2, 32).transpose(3, 4, 0, 5, 1, 2).reshape(2, 2, NCORES, 64, 64)
    ss = np.stack([R[c]["ssm_s"] for c in range(NCORES)])
    ss = ss.reshape(NCORES, 2, 64, 2, 2, 32, 4).transpose(3, 4, 0, 6, 5, 1, 2).reshape(2, 2, 32, 64, 64)
    co = np.stack([R[c]["conv_o"] for c in range(NCORES)])
    co = co.transpose(2, 0, 4, 3, 1).reshape(DEPTH, NCORES, 10, 2 * DFF)
    conv_prompt = np.ascontiguousarray(co[:, :, 0:2])
    conv_sample = np.ascontiguousarray(co[:, :, 2:10].reshape(DEPTH, NCORES, 4, 2, 2 * DFF).reshape(DEPTH, 32, 2, 2 * DFF))
    pp_ = np.stack([R[c]["pool_p"] for c in range(NCORES)])
    pool_prompt = np.ascontiguousarray(pp_.transpose(0, 3, 2, 1).reshape(1, NCORES, 15, D))
    ps_ = np.stack([R[c]["pool_s"] for c in range(NCORES)])
    pool_sample = np.ascontiguousarray(ps_.transpose(0, 3, 4, 2, 1).reshape(1, 32, 15, D))
    k_prompt = np.stack([R[c]["k_p"] for c in range(NCORES)]).reshape(1, NCORES, TP, 16, 64)
    v_prompt = np.stack([R[c]["v_p"] for c in range(NCORES)]).reshape(1, NCORES, TP, 16, 64)
    k_sample = np.stack([R[c]["k_s"] for c in range(NCORES)]).reshape(1, 32, 4, 16, 64)
    v_sample = np.stack([R[c]["v_s"] for c in range(NCORES)]).reshape(1, 32, 4, 16, 64)
    return (y_prompt, y_sample, k_prompt, v_prompt, k_sample, v_sample,
            np.ascontiguousarray(sp[0]), np.ascontiguousarray(sp[1]), np.ascontiguousarray(ss[0]), np.ascontiguousarray(ss[1]),
            pool_prompt, pool_sample, conv_prompt, conv_sample)
```
